# Optimizing a Trainium2 kernel written in Bass

```python
import math
import jax, jax.numpy as jnp
from jax import lax
import numpy as np

D_MODEL = 1024
BATCH = 8
SEQ = 4096
DEPTH = 1

MLA_HEADS = 8
MLA_NOPE_DIM = 64
MLA_ROPE_DIM = 32
MLA_V_DIM = 64
MLA_Q_RANK = 384
MLA_KV_RANK = 128
MLA_WIDTH = MLA_HEADS * MLA_V_DIM
MLA_QK_DIM = MLA_NOPE_DIM + MLA_ROPE_DIM
ROPE_THETA = 10000.0
Q_BLOCK = 128
HGRN_HEADS = 4
HGRN_HEAD_DIM = 128
HGRN_WIDTH = HGRN_HEADS * HGRN_HEAD_DIM
HGRN_CHUNK = 64
D_MIX = MLA_WIDTH + HGRN_WIDTH
D_IN_PROJ = MLA_Q_RANK + MLA_KV_RANK + MLA_ROPE_DIM + 4 * HGRN_WIDTH
D_FF = -(-8 * D_MODEL // (3 * 256)) * 256
EPS = 1e-6

kernel_name = 'hymba_mla_hgrn2_sandwich_block'


def rmsnorm(x, w):
    x32 = x.astype(jnp.float32)
    inv = lax.rsqrt(jnp.mean(x32 * x32, axis=-1, keepdims=True) + EPS)
    return (x32 * inv).astype(x.dtype) * w


def rope_cos_sin(positions, dtype):
    inv_freq = 1.0 / (ROPE_THETA ** (jnp.arange(0, MLA_ROPE_DIM, 2, dtype=jnp.float32) / MLA_ROPE_DIM))
    ang = positions.astype(jnp.float32)[..., None] * inv_freq
    return jnp.cos(ang).astype(dtype), jnp.sin(ang).astype(dtype)


def apply_rope(x, cos, sin):
    x1, x2 = jnp.split(x, 2, axis=-1)
    return jnp.concatenate([x1 * cos - x2 * sin, x1 * sin + x2 * cos], axis=-1)


def mla_mixer(c_q, c_kv, k_rope_raw, positions, q_norm_w, w_uq, kv_norm_w, w_ukv, out_norm_w):
    B, S, _ = c_q.shape
    nb = S // Q_BLOCK
    q = jnp.einsum('bsr,rhd->bshd', rmsnorm(c_q, q_norm_w), w_uq)
    q_nope, q_rope = q[..., :MLA_NOPE_DIM], q[..., MLA_NOPE_DIM:]
    kv = jnp.einsum('bsr,rhd->bshd', rmsnorm(c_kv, kv_norm_w), w_ukv)
    k_nope, v = kv[..., :MLA_NOPE_DIM], kv[..., MLA_NOPE_DIM:]
    cos, sin = rope_cos_sin(positions, q.dtype)
    q_rope = apply_rope(q_rope, cos[:, :, None, :], sin[:, :, None, :])
    k_rope = apply_rope(k_rope_raw, cos, sin)
    scale = MLA_QK_DIM ** -0.5
    qn_b = q_nope.reshape(B, nb, Q_BLOCK, MLA_HEADS, MLA_NOPE_DIM).transpose(1, 0, 2, 3, 4)
    qr_b = q_rope.reshape(B, nb, Q_BLOCK, MLA_HEADS, MLA_ROPE_DIM).transpose(1, 0, 2, 3, 4)
    key_idx = jnp.arange(S)

    def block(args):
        qn, qr, blk = args
        s = (jnp.einsum('bqhd,bkhd->bhqk', qn, k_nope)
             + jnp.einsum('bqhd,bkd->bhqk', qr, k_rope)).astype(jnp.float32) * scale
        q_idx = blk * Q_BLOCK + jnp.arange(Q_BLOCK)
        mask = key_idx[None, :] <= q_idx[:, None]
        p = jax.nn.softmax(jnp.where(mask, s, -jnp.inf), axis=-1).astype(v.dtype)
        return jnp.einsum('bhqk,bkhd->bqhd', p, v)

    o = lax.map(block, (qn_b, qr_b, jnp.arange(nb)))
    o = o.transpose(1, 0, 2, 3, 4).reshape(B, S, MLA_HEADS, MLA_V_DIM)
    o = rmsnorm(o, out_norm_w.reshape(MLA_HEADS, MLA_V_DIM))
    return o.reshape(B, S, MLA_WIDTH)


def hgrn2_mixer(q_raw, f_raw, i_raw, g_raw, lb, out_norm_w):
    B, S, _ = q_raw.shape
    H, D, C = HGRN_HEADS, HGRN_HEAD_DIM, HGRN_CHUNK
    nc = S // C
    lb32 = lb.astype(jnp.float32)
    f = lb32 + (1.0 - lb32) * jax.nn.sigmoid(f_raw.astype(jnp.float32))
    log_f = jnp.log(f)
    k = 1.0 - f
    q = jax.nn.silu(q_raw.astype(jnp.float32))
    v = i_raw.astype(jnp.float32)

    def to_chunks(t):
        return t.reshape(B, nc, C, H, D).transpose(1, 0, 3, 2, 4)

    causal = jnp.tril(jnp.ones((C, C), dtype=bool))[:, :, None]

    def step(state, inp):
        qc, kc, vc, lfc = inp
        b = jnp.cumsum(lfc, axis=2)
        o_inter = jnp.einsum('bhtk,bhkv->bhtv', qc * jnp.exp(b), state)
        diff = b[:, :, :, None, :] - b[:, :, None, :, :]
        decay = jnp.exp(jnp.where(causal, diff, -jnp.inf))
        a = jnp.einsum('bhtk,bhtsk,bhsk->bhts', qc, decay, kc)
        o_intra = jnp.einsum('bhts,bhsv->bhtv', a, vc)
        b_last = b[:, :, -1:, :]
        k_dec = kc * jnp.exp(b_last - b)
        state = jnp.exp(b_last[:, :, 0, :])[..., None] * state + jnp.einsum('bhsk,bhsv->bhkv', k_dec, vc)
        return state, o_inter + o_intra

    s0 = jnp.zeros((B, H, D, D), jnp.float32)
    _, o = lax.scan(step, s0, (to_chunks(q), to_chunks(k), to_chunks(v), to_chunks(log_f)))
    o = o.transpose(1, 0, 3, 2, 4).reshape(B, S, H, D)
    o = rmsnorm(o, out_norm_w.reshape(H, D).astype(jnp.float32))
    o = o.reshape(B, S, HGRN_WIDTH) * jax.nn.silu(g_raw.astype(jnp.float32))
    return o.astype(q_raw.dtype)


def setup_inputs(seed: int = 0) -> dict:
    key = jax.random.key(seed)
    ks = jax.random.split(key, 24)
    nrm = lambda k, shape, fan_in: jax.random.normal(k, shape, jnp.float32) * fan_in ** -0.5
    gain = lambda k, shape: 1.0 + 0.02 * jax.random.normal(k, shape, jnp.float32)
    x = jax.random.normal(ks[0], (BATCH, SEQ, D_MODEL), jnp.float32)
    offset = jax.random.randint(ks[1], (BATCH, 1), 0, 2048, dtype=jnp.int32)
    positions = (offset + jnp.arange(SEQ, dtype=jnp.int32)[None, :]).astype(jnp.int32)
    lb_base = jnp.concatenate([-jnp.ones((1, HGRN_WIDTH), jnp.float32),
                               jnp.ones((DEPTH, HGRN_WIDTH), jnp.float32)], axis=0)
    hgrn_lb_logits = lb_base + 0.1 * jax.random.normal(ks[2], (DEPTH + 1, HGRN_WIDTH), jnp.float32)
    return {
        'x': x,
        'positions': positions,
        'attn_pre_norm': gain(ks[3], (DEPTH, D_MODEL)),
        'w_in': nrm(ks[4], (DEPTH, D_MODEL, D_IN_PROJ), D_MODEL),
        'mla_q_norm': gain(ks[5], (DEPTH, MLA_Q_RANK)),
        'mla_w_uq': nrm(ks[6], (DEPTH, MLA_Q_RANK, MLA_HEADS, MLA_QK_DIM), MLA_Q_RANK),
        'mla_kv_norm': gain(ks[7], (DEPTH, MLA_KV_RANK)),
        'mla_w_ukv': nrm(ks[8], (DEPTH, MLA_KV_RANK, MLA_HEADS, MLA_NOPE_DIM + MLA_V_DIM), MLA_KV_RANK),
        'mla_out_norm': gain(ks[9], (DEPTH, MLA_WIDTH)),
        'hgrn_lb_logits': hgrn_lb_logits,
        'hgrn_out_norm': gain(ks[10], (DEPTH, HGRN_WIDTH)),
        'w_out': nrm(ks[11], (DEPTH, D_MIX, D_MODEL), D_MIX),
        'attn_post_norm': gain(ks[12], (DEPTH, D_MODEL)),
        'ffn_pre_norm': gain(ks[13], (DEPTH, D_MODEL)),
        'w_gate': nrm(ks[14], (DEPTH, D_MODEL, D_FF), D_MODEL),
        'w_up': nrm(ks[15], (DEPTH, D_MODEL, D_FF), D_MODEL),
        'w_down': nrm(ks[16], (DEPTH, D_FF, D_MODEL), D_FF),
        'ffn_post_norm': gain(ks[17], (DEPTH, D_MODEL)),
    }


def reference(x, positions, attn_pre_norm, w_in, mla_q_norm, mla_w_uq, mla_kv_norm, mla_w_ukv,
              mla_out_norm, hgrn_lb_logits, hgrn_out_norm, w_out, attn_post_norm, ffn_pre_norm,
              w_gate, w_up, w_down, ffn_post_norm):
    lb_all = jnp.cumsum(jax.nn.softmax(hgrn_lb_logits.astype(jnp.float32), axis=0), axis=0)[:DEPTH]
    s1 = MLA_Q_RANK
    s2 = s1 + MLA_KV_RANK
    s3 = s2 + MLA_ROPE_DIM
    s4 = s3 + HGRN_WIDTH
    s5 = s4 + HGRN_WIDTH
    s6 = s5 + HGRN_WIDTH
    h = x
    for l in range(DEPTH):
        u = rmsnorm(h, attn_pre_norm[l])
        xp = jnp.einsum('bsd,de->bse', u, w_in[l])
        c_q, c_kv, k_rope_raw = xp[..., :s1], xp[..., s1:s2], xp[..., s2:s3]
        hq, hf, hi, hg = xp[..., s3:s4], xp[..., s4:s5], xp[..., s5:s6], xp[..., s6:]
        o_mla = mla_mixer(c_q, c_kv, k_rope_raw, positions, mla_q_norm[l], mla_w_uq[l],
                          mla_kv_norm[l], mla_w_ukv[l], mla_out_norm[l])
        o_hgrn = hgrn2_mixer(hq, hf, hi, hg, lb_all[l], hgrn_out_norm[l])
        mix = jnp.concatenate([o_mla, o_hgrn.astype(o_mla.dtype)], axis=-1)
        h = h + rmsnorm(jnp.einsum('bse,ed->bsd', mix, w_out[l]), attn_post_norm[l])
        z = rmsnorm(h, ffn_pre_norm[l])
        ff = jax.nn.silu(jnp.einsum('bsd,df->bsf', z, w_gate[l])) * jnp.einsum('bsd,df->bsf', z, w_up[l])
        h = h + rmsnorm(jnp.einsum('bsf,fd->bsd', ff, w_down[l]), ffn_post_norm[l])
    return h
```

```python
import contextlib
import numpy as np
import concourse.bass as bass
import concourse.mybir as mybir

F32 = mybir.dt.float32
BF16 = mybir.dt.bfloat16
I32 = mybir.dt.int32
AF = mybir.ActivationFunctionType
ALU = mybir.AluOpType
AX = mybir.AxisListType

ENGS = ("pe", "act", "dve", "pool", "sp")
STRICT = True


class Buf:
    __slots__ = ("name", "last_w", "readers")

    def __init__(self, name):
        self.name = name
        self.last_w = None
        self.readers = {}


class _Op:
    __slots__ = ("fn", "waits", "sig", "dma")

    def __init__(self, fn, waits, sig, dma):
        self.fn, self.waits, self.sig, self.dma = fn, waits, sig, dma


class KB:
    def __init__(self, nc):
        self.nc = nc
        self.prog = {e: [] for e in ENGS}
        self.nsig = {e: 0 for e in ENGS}
        self.marked = {e: set() for e in ENGS}
        self.seen = {e: {} for e in ENGS}
        self.streams = []

    def stream(self, name=None):
        key = "dma%d" % len(self.streams)
        self.streams.append(key)
        self.nsig[key] = 0
        self.marked[key] = None
        return key

    def _need(self, eng, waits, sig, same_ok):
        if sig is None:
            return
        key, idx = sig
        if key == eng and not same_ok:
            return
        if self.seen[eng].get(key, -1) >= idx:
            return
        waits[key] = max(waits.get(key, -1), idx)

    def _deps(self, eng, r, w):
        waits = {}
        for b in r:
            self._need(eng, waits, b.last_w, True)
        for b in w:
            strict = STRICT and eng != "pe" and not b.name.startswith("bank")
            self._need(eng, waits, b.last_w, strict)
            for k, i in b.readers.items():
                self._need(eng, waits, (k, i), strict)
        for k, i in waits.items():
            self.seen[eng][k] = i
            if self.marked[k] is not None:
                self.marked[k].add(i)
        return list(waits.items())

    def op(self, eng, fn, r=(), w=()):
        waits = self._deps(eng, r, w)
        idx = self.nsig[eng]
        self.nsig[eng] += 1
        sig = (eng, idx)
        for b in r:
            b.readers[eng] = idx
        for b in w:
            b.last_w = sig
            b.readers = {}
        self.prog[eng].append(_Op(fn, waits, sig, False))
        return sig

    def dma(self, eng, stream, fn, r=(), w=()):
        waits = self._deps(eng, r, w)
        idx = self.nsig[stream]
        self.nsig[stream] += 1
        sig = (stream, idx)
        for b in r:
            b.readers[stream] = idx
        for b in w:
            b.last_w = sig
            b.readers = {}
        self.prog[eng].append(_Op(fn, waits, sig, True))
        return sig

    def wait_all(self, eng, bufs):
        waits = {}
        for b in bufs:
            self._need(eng, waits, b.last_w, True)
            for k, i in b.readers.items():
                self._need(eng, waits, (k, i), True)
        for k, i in waits.items():
            self.seen[eng][k] = i
            if self.marked[k] is not None:
                self.marked[k].add(i)
        self.prog[eng].append(_Op(None, list(waits.items()), None, False))

    def barrier(self, streams=True):
        for e in ENGS:
            waits = {}
            for k in list(ENGS) + (self.streams if streams else []):
                if k == e or self.nsig[k] == 0:
                    continue
                self._need(e, waits, (k, self.nsig[k] - 1), True)
            for k, i in waits.items():
                self.seen[e][k] = i
                if self.marked[k] is not None:
                    self.marked[k].add(i)
            self.prog[e].append(_Op(None, list(waits.items()), None, False))

    def emit(self):
        nc = self.nc
        with contextlib.ExitStack() as st:
            sems = {}
            for k in list(ENGS) + self.streams:
                sems[k] = st.enter_context(nc.semaphore("s_" + k))
            rank = {}
            for k in ENGS:
                m = sorted(self.marked[k])
                rank[k] = {i: n + 1 for n, i in enumerate(m)}

            def val(k, i):
                if self.marked[k] is None:
                    return 16 * (i + 1)
                return rank[k][i]

            def run(e, eng):
                for o in self.prog[e]:
                    for k, i in o.waits:
                        eng.wait_ge(sems[k], val(k, i))
                    if o.fn is None:
                        continue
                    ins = o.fn(eng)
                    k, i = o.sig
                    if o.dma:
                        ins.then_inc(sems[k], 16)
                    elif i in self.marked[k]:
                        ins.then_inc(sems[k], 1)

            block = st.enter_context(nc.Block())

            @block.tensor
            def _(e):
                run("pe", e)

            @block.scalar
            def _(e):
                run("act", e)

            @block.vector
            def _(e):
                run("dve", e)

            @block.gpsimd
            def _(e):
                run("pool", e)

            @block.sync
            def _(e):
                run("sp", e)
from concourse.bass_utils import run_bass_kernel_spmd

D = 1024
DIN = 2592
DFF = 2816
NFC = DFF // 128
EPS = 1e-6
SCALE = 96 ** -0.5
C_CQ, C_CKV, C_KR, C_HQ, C_HF, C_HI, C_HG = 0, 384, 512, 544, 1056, 1568, 2080


def build(S, debug=False, PIPE=True):
    NT = S // 128
    NB = S // 512
    nc = bass.Bass("TRN2", target_bir_lowering=False)
    dt_in = lambda n, shp, dt=F32: nc.dram_tensor(n, list(shp), dt, kind="ExternalInput").ap()
    x_d = dt_in("x", [S, D])
    pos_d = dt_in("pos", [128, NT], I32)
    w_in_d = dt_in("w_in", [D, DIN]); w_uq_d = dt_in("w_uq", [384, 768]); w_ukv_d = dt_in("w_ukv", [128, 1024])
    w_out_d = dt_in("w_out", [D, D]); w_gate_d = dt_in("w_gate", [D, DFF]); w_up_d = dt_in("w_up", [D, DFF])
    w_down_d = dt_in("w_down", [DFF, D])
    g_pre_d = dt_in("g_pre", [D]); g_post_d = dt_in("g_post", [D]); g_fpre_d = dt_in("g_fpre", [D]); g_fpost_d = dt_in("g_fpost", [D])
    g_q_d = dt_in("g_q", [128, 3]); g_kv_d = dt_in("g_kv", [128, 1]); g_mla_d = dt_in("g_mla", [128, 4]); g_hn_d = dt_in("g_hn", [128, 4])
    lbl_d = dt_in("lbl", [128, 8])
    ident_d = dt_in("ident", [128, 128]); tri_d = dt_in("tri", [128, 128]); negtri_d = dt_in("negtri", [128, 128]); mask2_d = dt_in("mask2", [128, 128])
    resetm_d = dt_in("resetm", [512]); invf_d = dt_in("invf", [16]); invf_lo_d = dt_in("invf_lo", [16]); wcol_d = dt_in("wcol", [128, 2])
    out_d = nc.dram_tensor("out", [S, D], F32, kind="ExternalOutput").ap()
    dbg = {}
    def DBG(name, shape):
        if debug:
            dbg[name] = nc.dram_tensor(name, list(shape), F32, kind="ExternalOutput").ap()
        return dbg.get(name)
    d_mixH = DBG("d_mixH", [128, 4, S]); d_mixM = DBG("d_mixM", [128, 4, S]); d_h1 = None
    d_cq = DBG("d_cq", [128, 3, S]); d_rq = DBG("d_rq", [128, NT]); d_kr = DBG("d_kr", [128, NT, 32])

    kb = KB(nc)
    ES = contextlib.ExitStack
    with ES() as st0:
        ARENA_BYTES = 211968
        arena = st0.enter_context(nc.sbuf_tensor("arena", [128, ARENA_BYTES // 2], BF16))
        ar = {"lo": 0, "hi": ARENA_BYTES}
        def _view(off, shape, dt):
            esz = 2 if dt == BF16 else 4
            n = 1
            for d_ in shape[1:]:
                n *= d_
            v = arena[:, off // 2: off // 2 + n * esz // 2]
            if esz == 4:
                v = v.bitcast(dt)
            if len(shape) > 2:
                names = ["a%d" % i for i in range(len(shape) - 1)]
                kw = {nm: d_ for nm, d_ in zip(names[1:], shape[2:])}
                v = v.rearrange("p (%s) -> p %s" % (" ".join(names), " ".join(names)), **kw)
            if shape[0] < 128:
                v = v[0:shape[0]]
            return v
        def T(st, name, shape, dt):
            esz = 2 if dt == BF16 else 4
            n = esz
            for d_ in shape[1:]:
                n *= d_
            n = (n + 63) // 64 * 64
            if st == "top":
                ar["hi"] -= n
                off = ar["hi"]
            else:
                off = ar["lo"]
                ar["lo"] += n
            assert ar["lo"] <= ar["hi"], ("SBUF arena overflow", name, ar)
            return _view(off, list(shape), dt)
        psum = st0.enter_context(nc.psum_tensor("psum", [128, 4096], F32))
        PB = [Buf("bank%d" % i) for i in range(8)]
        bank_ctr = [0]
        def bank():
            i = bank_ctr[0] % 6
            bank_ctr[0] += 1
            return i
        lbank_ctr = [0]
        def lbank():
            i = 6 + lbank_ctr[0] % 2
            lbank_ctr[0] += 1
            return i
        def pf(i):
            return psum[:, i * 512:(i + 1) * 512]
        def pb(i):
            return psum[:, i * 512:(i + 1) * 512].bitcast(BF16)
        ob = [Buf("out%d" % g) for g in range(S // 128)]
        st_ld = [kb.stream() for _ in range(4)]
        st_w = kb.stream()
        st_x = [kb.stream() for _ in range(2)]
        st_o = [kb.stream() for _ in range(2)]
        st_dbg = kb.stream()

        def mm(out, lhsT, rhs, start, stop, r, w):
            kb.op("pe", lambda e: e.matmul(out, lhsT=lhsT, rhs=rhs, start=start, stop=stop), r=r, w=w)
        def tr(out, in_, ident, r, w):
            kb.op("pe", lambda e: e.transpose(out=out, in_=in_, identity=ident), r=r, w=w)
        def act(out, in_, func, r, w, scale=1.0, bias=0.0, accum=None):
            if accum is None:
                kb.op("act", lambda e: e.activation(out=out, in_=in_, func=func, bias=bias, scale=scale), r=r, w=w)
            else:
                kb.op("act", lambda e: e.activation(out=out, in_=in_, func=func, bias=bias, scale=scale, accum_out=accum), r=r, w=w)
        def ts(eng, out, in0, s1, s2, op0, op1, r, w):
            if s2 is None:
                kb.op(eng, lambda e: e.tensor_scalar(out=out, in0=in0, scalar1=s1, scalar2=None, op0=op0), r=r, w=w)
            else:
                kb.op(eng, lambda e: e.tensor_scalar(out=out, in0=in0, scalar1=s1, scalar2=s2, op0=op0, op1=op1), r=r, w=w)
        def tt(eng, out, in0, in1, op, r, w):
            kb.op(eng, lambda e: e.tensor_tensor(out=out, in0=in0, in1=in1, op=op), r=r, w=w)
        def stt(eng, out, in0, scalar, in1, op0, op1, r, w):
            kb.op(eng, lambda e: e.scalar_tensor_tensor(out=out, in0=in0, scalar=scalar, in1=in1, op0=op0, op1=op1), r=r, w=w)
        def cp(eng, out, in_, r, w):
            if eng == "act":
                kb.op("act", lambda e: e.copy(out=out, in_=in_), r=r, w=w)
            else:
                kb.op(eng, lambda e: e.tensor_copy(out=out, in_=in_), r=r, w=w)
        def recip(out, in_, r, w):
            kb.op("dve", lambda e: e.reciprocal(out=out, in_=in_), r=r, w=w)
        def dma(eng, stream, out, in_, r, w):
            kb.dma(eng, stream, lambda e: e.dma_start(out=out, in_=in_), r=r, w=w)
        def rsqrt_act(out, in_, scale, r, w):
            act(out, in_, AF.Ln, r=r, w=w, scale=scale, bias=epsb[:in_.shape[0], 0:1])
            act(out, out, AF.Exp, r=w, w=w, scale=-0.5)
        def dump(dap, tile_ap, buf):
            if dap is not None:
                dma("sp", st_dbg, dap, tile_ap, r=[buf], w=[Buf("dbgout")])

        identb = T(st0, "identb", [128, 128], BF16); b_identb = Buf("identb")
        identf = T(st0, "identf", [128, 128], F32); b_identf = Buf("identf")
        trib = T(st0, "trib", [128, 128], BF16); b_trib = Buf("trib")
        negtrib = T(st0, "negtrib", [128, 128], BF16); b_negtri = Buf("negtrib")
        mask2 = T(st0, "mask2", [128, 128], F32); b_mask2 = Buf("mask2")
        ones_bf = T(st0, "ones_bf", [128, 128], BF16); b_ones = Buf("ones")
        epsb = T(st0, "epsb", [128, 1], F32); b_eps = Buf("epsb")
        wcol = T(st0, "wcol", [128, 2], F32); b_wcol = Buf("wcol")
        g_q = T(st0, "g_q", [128, 3], F32); g_kv = T(st0, "g_kv", [128, 1], F32)
        g_mla = T(st0, "g_mla", [128, 4], F32); g_hn = T(st0, "g_hn", [128, 4], F32)
        b_gq = Buf("g_q"); b_gkv = Buf("g_kv"); b_gmla = Buf("g_mla"); b_ghn = Buf("g_hn")
        lb = T(st0, "lb", [128, 4], F32); oml = T(st0, "oml", [128, 4], F32); b_lb = Buf("lb")
        fA = T(st0, "fA", [128, 4], F32); fB = T(st0, "fB", [128, 4], F32)
        rstdq = T(st0, "rstdq", [128, NT], F32); b_rstdq = Buf("rstdq")
        rstdkv = T(st0, "rstdkv", [128, NT], F32); b_rstdkv = Buf("rstdkv")
        Wuq = T(st0, "Wuq", [128, 3, 768], BF16); Wukv = T(st0, "Wukv", [128, 1024], BF16); b_Wu = Buf("Wu")
        markMix = ar["lo"]
        mixH = T(st0, "mixH", [128, 4, S], BF16); b_mixH = [[Buf("mixH%d_%d" % (h, b)) for b in range(NB)] for h in range(4)]
        c_qT = T("top", "c_qT", [128, 3, S], BF16); b_cq = [Buf("cq%d" % b) for b in range(NB)]
        c_kvT = T("top", "c_kvT", [128, S], BF16); b_ckv = [Buf("ckv%d" % b) for b in range(NB)]
        krope = T("top", "krope", [128, NT, 32], BF16); b_krope = [Buf("krope%d" % b) for b in range(NB)]
        cosq = T("top", "cosq", [128, NT, 16], F32); sinq = T("top", "sinq", [128, NT, 16], F32); b_csq = Buf("cossinq")

        dma("pool", kb.stream(), identb[:], ident_d, r=[], w=[b_identb])
        dma("sp", kb.stream(), identf[:], ident_d, r=[], w=[b_identf])
        dma("pool", kb.stream(), trib[:], tri_d, r=[], w=[b_trib])
        dma("pool", kb.stream(), negtrib[:], negtri_d, r=[], w=[b_negtri])
        dma("sp", kb.stream(), mask2[:], mask2_d, r=[], w=[b_mask2])
        dma("sp", kb.stream(), wcol[:], wcol_d, r=[], w=[b_wcol])
        dma("sp", kb.stream(), g_q[:], g_q_d, r=[], w=[b_gq])
        dma("sp", kb.stream(), g_kv[:], g_kv_d, r=[], w=[b_gkv])
        dma("sp", kb.stream(), g_mla[:], g_mla_d, r=[], w=[b_gmla])
        dma("sp", kb.stream(), g_hn[:], g_hn_d, r=[], w=[b_ghn])
        kb.op("pool", lambda e: e.memset(ones_bf[:], 1.0), w=[b_ones])
        kb.op("pool", lambda e: e.memset(epsb[:], EPS), w=[b_eps])

        mixM = T(st0, "mixM", [128, 4, S], BF16); b_mixM = Buf("mixM")
        markA = dict(ar)
        ar["lo"] = markA["lo"] - 4 * S * 2
        if True:
            stA = None
            W_in = T(stA, "W_in", [128, 8, DIN], BF16)
            WG = {"cq": (0, 544), "hi": (C_HI, C_HI + 512), "hq": (C_HQ, C_HQ + 512), "hf": (C_HF, C_HF + 512), "hg": (C_HG, C_HG + 512)}
            b_Wg_in = {k: Buf("W_in_" + k) for k in WG}
            stw_k = kb.stream()
            b_Wall = Buf("W_in_all")
            for k in WG:
                b_Wg_in[k] = b_Wall
            for dc in range(8):
                dma("pool", stw_k, W_in[:, dc, :], w_in_d[dc * 128:(dc + 1) * 128, :], r=[], w=[b_Wall])
            st_wu = kb.stream()
            for rc in range(3):
                dma("pool", st_wu, Wuq[:, rc, :], w_uq_d[rc * 128:(rc + 1) * 128, :], r=[], w=[b_Wu])
            dma("pool", st_wu, Wukv[:], w_ukv_d, r=[], w=[b_Wu])
            def b_Win_for(col):
                for k, (c0, c1) in WG.items():
                    if c0 <= col < c1:
                        return b_Wg_in[k]
                raise AssertionError(col)
            gpre_b = T(stA, "gpre_b", [128, D], F32); b_gpre = Buf("gpre")
            dma("sp", kb.stream(), gpre_b[:], g_pre_d.partition_broadcast(128), r=[], w=[b_gpre])
            resetm = T(stA, "resetm", [128, 512], F32); b_resetm = Buf("resetm")
            dma("sp", kb.stream(), resetm[:], resetm_d.partition_broadcast(128), r=[], w=[b_resetm])
            cosT = T(stA, "cosT", [128, NT, 16], F32); sinT = T(stA, "sinT", [128, NT, 16], F32); b_cs = Buf("cossin")
            lbl = T(stA, "lbl", [128, 8], F32); b_lbl = Buf("lbl")
            dma("sp", kb.stream(), lbl[:], lbl_d, r=[], w=[b_lbl])
            tt("dve", lb[:], lbl[:, 4:8], lbl[:, 0:4], ALU.subtract, r=[b_lbl], w=[b_lb])
            act(lb[:], lb[:], AF.Exp, r=[b_lb], w=[b_lb])
            ts("dve", lb[:], lb[:], 1.0, None, ALU.add, None, r=[b_lb], w=[b_lb])
            recip(lb[:], lb[:], r=[b_lb], w=[b_lb])
            ts("dve", oml[:], lb[:], -1.0, 1.0, ALU.mult, ALU.add, r=[b_lb], w=[b_lb])
            ts("dve", fA[:], oml[:], 0.5, None, ALU.mult, None, r=[b_lb], w=[b_lb])
            tt("dve", fB[:], lb[:], fA[:], ALU.add, r=[b_lb], w=[b_lb])
            markR = dict(ar)
            if True:
                stR = None
                posi = T(stR, "posi", [128, NT], I32); posf = T(stR, "posf", [128, NT], F32)
                invf = T(stR, "invf", [128, 16], F32)
                invf_lo = T(stR, "invf_lo", [128, 16], F32)
                ang = T(stR, "ang", [128, NT, 16], F32); a2 = T(stR, "a2", [128, NT, 16], F32)
                nq = T(stR, "nq", [128, NT, 16], F32); ni = T(stR, "ni", [128, NT, 16], I32)
                b_r = Buf("ropetmp")
                b_rp = Buf("rope_pos"); b_ri = Buf("rope_invf"); b_ril = Buf("rope_invf_lo")
                dma("sp", kb.stream(), posi[:], pos_d, r=[], w=[b_rp])
                dma("sp", kb.stream(), invf[:], invf_d.partition_broadcast(128), r=[], w=[b_ri])
                dma("sp", kb.stream(), invf_lo[:], invf_lo_d.partition_broadcast(128), r=[], w=[b_ril])
                cp("dve", posf[:], posi[:], r=[b_rp], w=[b_r])
                tt("dve", ang[:], posf[:].unsqueeze(2).broadcast_to([128, NT, 16]),
                   invf[:].unsqueeze(1).broadcast_to([128, NT, 16]), ALU.mult, r=[b_r, b_ri], w=[b_r])
                tt("dve", nq[:], posf[:].unsqueeze(2).broadcast_to([128, NT, 16]),
                   invf_lo[:].unsqueeze(1).broadcast_to([128, NT, 16]), ALU.mult, r=[b_r, b_ril], w=[b_r])
                tt("dve", ang[:], ang[:], nq[:], ALU.add, r=[b_r], w=[b_r])
                TWO_PI = float(2 * np.pi)
                HI = float(np.float32(6.28125)); LO = float(np.float32(2 * np.pi - 6.28125))
                def reduce_sin(dst, shift):
                    ts("dve", a2[:], ang[:], shift, None, ALU.add, None, r=[b_r], w=[b_r])
                    ts("dve", nq[:], a2[:], 1.0 / TWO_PI, None, ALU.mult, None, r=[b_r], w=[b_r])
                    cp("dve", ni[:], nq[:], r=[b_r], w=[b_r])
                    cp("dve", nq[:], ni[:], r=[b_r], w=[b_r])
                    stt("dve", a2[:], nq[:], -HI, a2[:], ALU.mult, ALU.add, r=[b_r], w=[b_r])
                    stt("dve", a2[:], nq[:], -LO, a2[:], ALU.mult, ALU.add, r=[b_r], w=[b_r])
                    ts("dve", nq[:], a2[:], float(np.pi), -TWO_PI, ALU.is_gt, ALU.mult, r=[b_r], w=[b_r])
                    tt("dve", a2[:], a2[:], nq[:], ALU.add, r=[b_r], w=[b_r])
                    ts("dve", nq[:], a2[:], float(-np.pi), TWO_PI, ALU.is_lt, ALU.mult, r=[b_r], w=[b_r])
                    tt("dve", a2[:], a2[:], nq[:], ALU.add, r=[b_r], w=[b_r])
                    act(dst, a2[:], AF.Sin, r=[b_r], w=[b_cs])
                reduce_sin(sinT[:], 0.0)
                reduce_sin(cosT[:], float(np.pi / 2))
                kb.barrier(streams=False)
            ar.update(markR)

            uT = [T(stA, "uT%d" % i, [128, 8, 512], BF16) for i in range(2)]; b_uT = [Buf("uT%d" % i) for i in range(2)]
            xt = [T(stA, "xt%d" % i, [128, D], F32) for i in range(2)]; b_xt = [Buf("xt%d" % i) for i in range(2)]
            ub = [T(stA, "ub%d" % i, [128, D], BF16) for i in range(2)]; b_ub = [Buf("ub%d" % i) for i in range(2)]
            junk = T(stA, "junk", [128, D], BF16); b_junk = Buf("junk")
            ssx = T(stA, "ssx", [128, 2], F32); b_ssx = [Buf("ssx0"), Buf("ssx1")]
            sqb = [T(stA, "sqb%d" % i, [128, 512], BF16) for i in range(4)]; b_sqb = [Buf("sqb%d" % i) for i in range(4)]
            v_tm = [T(stA, "v_tm%d" % i, [128, 4, 512], BF16) for i in range(2)]
            b_vtm = [[Buf("vtm%d_%d" % (j, i)) for i in range(4)] for j in range(2)]
            rtmp = T(stA, "rtmp", [128, 2, 4, 16], F32); b_rtmp = Buf("rtmp")
            tgn = [T(stA, "tgn%d" % i, [128, 512], F32) for i in range(6)]; b_tgn = [Buf("tgn%d" % i) for i in range(6)]
            kt_bf = T(stA, "kt_bf", [128, 512], BF16); kdT_bf = T(stA, "kdT_bf", [128, 512], BF16)
            b_kt, b_kdT = Buf("kt"), Buf("kdT")
            teb = [T(stA, "teb%d" % i, [128, 512], F32) for i in range(2)]; b_teb = [Buf("teb%d" % i) for i in range(2)]
            tg = [T(stA, "tg%d" % i, [128, 512], F32) for i in range(2)]; b_tg = [Buf("tg%d" % i) for i in range(2)]
            qt_bf = [T(stA, "qt_bf%d" % i, [128, 512], BF16) for i in range(2)]; b_qt = [Buf("qt%d" % i) for i in range(2)]
            kd_tm = [T(stA, "kd_tm%d" % i, [128, 4, 128], BF16) for i in range(2)]; b_kdtm = [Buf("kdtm%d" % i) for i in range(2)]
            aT_sb = [T(stA, "aT_sb%d" % i, [128, 4, 128], BF16) for i in range(2)]; b_aT = [Buf("aT%d" % i) for i in range(2)]
            t_x = T(stA, "t_x", [128, 512], F32); B_x = Buf("t_x")
            osq = T(stA, "osq", [128, 512], BF16); b_osq = Buf("osq")
            S32 = T(stA, "S32", [128, 4, 128], F32); Sbf = T(stA, "Sbf", [128, 4, 128], BF16)
            b_S32 = [Buf("S32_%d" % h) for h in range(4)]; b_Sbf = [Buf("Sbf_%d" % h) for h in range(4)]
            kb.op("pool", lambda e: e.memset(S32[:], 0.0), w=b_S32)
            kb.op("pool", lambda e: e.memset(Sbf[:], 0.0), w=b_Sbf)

            def A123(b):
                ub_ = uT[b % 2]; bu = b_uT[b % 2]
                blk = slice(b * 512, (b + 1) * 512)
                for i in range(4):
                    g = 4 * b + i
                    sl = g % 2
                    dma("sp", st_x[sl], xt[sl][:], x_d[g * 128:(g + 1) * 128, :], r=[], w=[b_xt[sl]])
                    act(junk[:], xt[sl][:], AF.Square, r=[b_xt[sl]], w=[b_junk, b_ssx[sl]], accum=ssx[:, sl:sl + 1])
                    rsqrt_act(ssx[:, sl:sl + 1], ssx[:, sl:sl + 1], 1.0 / D, r=[b_ssx[sl], b_eps], w=[b_ssx[sl]])
                    stt("dve", ub[sl][:], xt[sl][:], ssx[:, sl:sl + 1], gpre_b[:], ALU.mult, ALU.mult,
                        r=[b_xt[sl], b_ssx[sl], b_gpre], w=[b_ub[sl]])
                    bk = bank()
                    for dc in range(8):
                        tr(pb(bk)[:, dc * 128:(dc + 1) * 128], ub[sl][:, dc * 128:(dc + 1) * 128], identb[:],
                           r=[b_ub[sl], b_identb], w=[PB[bk]])
                    cp("act", ub_[:, :, i * 128:(i + 1) * 128], pb(bk).rearrange("p (c t) -> p c t", t=128),
                       r=[], w=[PB[bk], bu])
                    yield
                def inproj_fm(col):
                    bk = bank()
                    for dc in range(8):
                        mm(pf(bk), W_in[:, dc, col:col + 128], ub_[:, dc, :], dc == 0, dc == 7, r=[b_Win_for(col), bu], w=[PB[bk]])
                    return bk
                for rc in range(4):
                    bk = inproj_fm(C_CQ + rc * 128)
                    act(sqb[rc][:], pf(bk), AF.Square, r=[], w=[PB[bk], b_sqb[rc]])
                    if rc < 3:
                        ts("dve", c_qT[:, rc, blk], pf(bk), g_q[:, rc:rc + 1], None, ALU.mult, None, r=[b_gq], w=[PB[bk], b_cq[b]])
                    else:
                        ts("dve", c_kvT[:, blk], pf(bk), g_kv[:, 0:1], None, ALU.mult, None, r=[b_gkv], w=[PB[bk], b_ckv[b]])
                    yield
                bk = bank()
                for i in range(4):
                    for rc in range(3):
                        mm(pf(bk)[:, i:i + 1], sqb[rc][:, i * 128:(i + 1) * 128], ones_bf[:, 0:1], rc == 0, rc == 2,
                           r=[b_sqb[rc], b_ones], w=[PB[bk]])
                for i in range(4):
                    mm(pf(bk)[:, 4 + i:5 + i], sqb[3][:, i * 128:(i + 1) * 128], ones_bf[:, 0:1], True, True,
                       r=[b_sqb[3], b_ones], w=[PB[bk]])
                act(rstdq[:, 4 * b:4 * b + 4], pf(bk)[:, 0:4], AF.Ln, r=[b_eps], w=[PB[bk], b_rstdq], scale=1.0 / 384, bias=epsb[:, 0:1])
                act(rstdq[:, 4 * b:4 * b + 4], rstdq[:, 4 * b:4 * b + 4], AF.Exp, r=[b_rstdq], w=[b_rstdq], scale=-0.5)
                act(rstdkv[:, 4 * b:4 * b + 4], pf(bk)[:, 4:8], AF.Ln, r=[b_eps], w=[PB[bk], b_rstdkv], scale=1.0 / 128, bias=epsb[:, 0:1])
                act(rstdkv[:, 4 * b:4 * b + 4], rstdkv[:, 4 * b:4 * b + 4], AF.Exp, r=[b_rstdkv], w=[b_rstdkv], scale=-0.5)
                bkr = bank()
                for i in range(4):
                    for dc in range(8):
                        mm(pf(bkr)[:, i * 32:(i + 1) * 32], ub_[:, dc, i * 128:(i + 1) * 128], W_in[:, dc, C_KR:C_KR + 32],
                           dc == 0, dc == 7, r=[b_Wg_in["cq"], bu], w=[PB[bkr]])
                kr3 = pf(bkr)[:, 0:128].rearrange("p (i c) -> p i c", c=32)
                x1 = kr3[:, :, 0:16]; x2 = kr3[:, :, 16:32]
                cs = cosT[:, 4 * b:4 * b + 4, :]; sn = sinT[:, 4 * b:4 * b + 4, :]
                tt("dve", rtmp[:, 0], x1, cs, ALU.mult, r=[b_cs], w=[PB[bkr], b_rtmp])
                tt("dve", rtmp[:, 1], x2, sn, ALU.mult, r=[b_cs], w=[PB[bkr], b_rtmp])
                tt("dve", krope[:, 4 * b:4 * b + 4, 0:16], rtmp[:, 0], rtmp[:, 1], ALU.subtract, r=[b_rtmp], w=[b_krope[b]])
                tt("dve", rtmp[:, 0], x1, sn, ALU.mult, r=[b_cs], w=[PB[bkr], b_rtmp])
                tt("dve", rtmp[:, 1], x2, cs, ALU.mult, r=[b_cs], w=[PB[bkr], b_rtmp])
                tt("dve", krope[:, 4 * b:4 * b + 4, 16:32], rtmp[:, 0], rtmp[:, 1], ALU.add, r=[b_rtmp], w=[b_krope[b]])

                yield
                for i in range(4):
                    bk = bank()
                    for dc in range(8):
                        mm(pf(bk), ub_[:, dc, i * 128:(i + 1) * 128], W_in[:, dc, C_HI:C_HI + 512], dc == 0, dc == 7,
                           r=[b_Wg_in["hi"], bu], w=[PB[bk]])
                    cp("act", v_tm[b % 2][:, i, :], pf(bk), r=[], w=[PB[bk], b_vtm[b % 2][i]])
                    yield
            def inproj_fm2(b, col):
                ub_ = uT[b % 2]; bu = b_uT[b % 2]
                bk = bank()
                for dc in range(8):
                    mm(pf(bk), W_in[:, dc, col:col + 128], ub_[:, dc, :], dc == 0, dc == 7, r=[b_Win_for(col), bu], w=[PB[bk]])
                return bk

            def G(b, h, st):
                bq = inproj_fm2(b, C_HQ + h * 128)
                bf = inproj_fm2(b, C_HF + h * 128)
                bg = inproj_fm2(b, C_HG + h * 128)
                t_f, t_lf, t_kk, t_b, t_enb, t_q = tgn
                B_f, B_lf, B_kk, B_b, B_enb, B_q = b_tgn
                t_eb = teb[st]; B_eb = b_teb[st]; t_g = tg[st]; B_g = b_tg[st]
                act(t_f[:], pf(bf), AF.Tanh, r=[], w=[PB[bf], B_f], scale=0.5)
                act(t_q[:], pf(bq), AF.Silu, r=[], w=[PB[bq], B_q])
                act(t_g[:], pf(bg), AF.Silu, r=[], w=[PB[bg], B_g])
                yield
                ts("dve", t_f[:], t_f[:], fA[:, h:h + 1], fB[:, h:h + 1], ALU.mult, ALU.add, r=[B_f, b_lb], w=[B_f])
                act(t_lf[:], t_f[:], AF.Ln, r=[B_f], w=[B_lf])
                yield
                ts("dve", t_kk[:], t_f[:], -1.0, 1.0, ALU.mult, ALU.add, r=[B_f], w=[B_kk])
                kb.op("dve", lambda e: e.tensor_tensor_scan(out=t_b[:], data0=resetm[:], data1=t_lf[:], initial=0.0,
                                                            op0=ALU.mult, op1=ALU.add), r=[b_resetm, B_lf], w=[B_b])
                yield
                act(t_eb[:], t_b[:], AF.Exp, r=[B_b], w=[B_eb])
                act(t_enb[:], t_b[:], AF.Exp, r=[B_b], w=[B_enb], scale=-1.0)
                yield
                tt("dve", t_kk[:], t_kk[:], t_enb[:], ALU.mult, r=[B_kk, B_enb], w=[B_kk])
                cp("act", kt_bf[:], t_kk[:], r=[B_kk], w=[b_kt])
                eb3 = t_eb[:].rearrange("p (c t) -> p c t", t=64)
                tt("dve", kdT_bf[:].rearrange("p (c t) -> p c t", t=64), t_kk[:].rearrange("p (c t) -> p c t", t=64),
                   eb3[:, :, 63:64].broadcast_to([128, 8, 64]), ALU.mult, r=[B_kk, B_eb], w=[b_kdT])
                tt("dve", qt_bf[st][:], t_q[:], t_eb[:], ALU.mult, r=[B_q, B_eb], w=[b_qt[st]])
                yield
                bk = bank()
                for i in range(4):
                    tr(pb(bk)[:, i * 128:(i + 1) * 128], kdT_bf[:, i * 128:(i + 1) * 128], identb[:], r=[b_kdT, b_identb], w=[PB[bk]])
                cp("act", kd_tm[st][:], pb(bk)[:, 0:512].rearrange("p (i k) -> p i k", k=128), r=[], w=[PB[bk], b_kdtm[st]])
                yield
                bk = bank()
                for i in range(4):
                    mm(pf(bk)[:, i * 128:(i + 1) * 128], kt_bf[:, i * 128:(i + 1) * 128], qt_bf[st][:, i * 128:(i + 1) * 128], True, True,
                       r=[b_kt, b_qt[st]], w=[PB[bk]])
                tt("dve", aT_sb[st][:], pf(bk).rearrange("p (i t) -> p i t", t=128), mask2[:].unsqueeze(1).broadcast_to([128, 4, 128]),
                   ALU.mult, r=[b_mask2], w=[PB[bk], b_aT[st]])
                yield

            def R(b, h, st):
                blk = slice(b * 512, (b + 1) * 512)
                t_eb = teb[st]; B_eb = b_teb[st]; t_g = tg[st]; B_g = b_tg[st]
                vt = v_tm[b % 2]; bv = b_vtm[b % 2]
                bo = lbank()
                for i in range(4):
                    mm(pf(bo)[:, i * 128:(i + 1) * 128], vt[:, i, h * 128:(h + 1) * 128], aT_sb[st][:, i, :], True, False,
                       r=[bv[i], b_aT[st]], w=[PB[bo]])
                    for cc in range(2):
                        c = 2 * i + cc
                        mm(pf(bo)[:, c * 64:(c + 1) * 64], Sbf[:, h, :], qt_bf[st][:, c * 64:(c + 1) * 64], False, cc == 1,
                           r=[b_Sbf[h], b_qt[st]], w=[PB[bo]])
                        bs = bank()
                        rows = slice(cc * 64, cc * 64 + 64)
                        mm(pf(bs)[:, 0:128], kd_tm[st][rows, i, :], vt[rows, i, h * 128:(h + 1) * 128], True, True,
                           r=[b_kdtm[st], bv[i]], w=[PB[bs]])
                        dec = t_eb[:, c * 64 + 63:c * 64 + 64]
                        stt("dve", Sbf[:, h, :], S32[:, h, :], dec, pf(bs)[:, 0:128], ALU.mult, ALU.add,
                            r=[B_eb, b_S32[h]], w=[PB[bs], b_Sbf[h]])
                        stt("dve", S32[:, h, :], S32[:, h, :], dec, pf(bs)[:, 0:128], ALU.mult, ALU.add,
                            r=[B_eb], w=[PB[bs], b_S32[h]])
                        yield
                act(osq[:], pf(bo), AF.Square, r=[], w=[PB[bo], b_osq])
                bn = bank()
                mm(pf(bn), ones_bf[:], osq[:], True, True, r=[b_ones, b_osq], w=[PB[bn]])
                act(t_x[:], pf(bn), AF.Ln, r=[b_eps], w=[PB[bn], B_x], scale=1.0 / 128, bias=epsb[:, 0:1])
                act(t_x[:], t_x[:], AF.Exp, r=[B_x], w=[B_x], scale=-0.5)
                tt("dve", t_x[:], pf(bo), t_x[:], ALU.mult, r=[B_x], w=[PB[bo], B_x])
                stt("dve", mixH[:, h, blk], t_x[:], g_hn[:, h:h + 1], t_g[:], ALU.mult, ALU.mult, r=[B_x, B_g, b_ghn], w=[b_mixH[h][b]])

            seq = [(b, h) for b in range(NB) for h in range(4)]

            def run_merged(gens):
                gens = list(gens)
                while gens:
                    for g_ in list(gens):
                        try:
                            next(g_)
                        except StopIteration:
                            gens.remove(g_)

            for _ in A123(0):
                pass
            carry = None
            for k in range(len(seq) + 1):
                gens = []
                if k > 0:
                    gens.append(R(seq[k - 1][0], seq[k - 1][1], (k - 1) % 2))
                if k < len(seq):
                    b, h = seq[k]
                    gens.append(G(b, h, k % 2))
                    if h == 1 and b + 1 < NB:
                        carry = A123(b + 1)
                if carry is not None:
                    hh = seq[k][1] if k < len(seq) else 0
                    if hh == 3:
                        gens.append(carry)
                        carry = None
                    else:
                        def part(gc, n):
                            for _ in range(n):
                                try:
                                    next(gc)
                                except StopIteration:
                                    return
                                yield
                        gens.append(part(carry, 5))
                run_merged(gens)
            print("arena phase A: lo", ar["lo"], "hi", ar["hi"])
            tt("dve", cosq[:], cosT[:], rstdq[:].unsqueeze(2).broadcast_to([128, NT, 16]), ALU.mult, r=[b_cs, b_rstdq], w=[b_csq])
            tt("dve", sinq[:], sinT[:], rstdq[:].unsqueeze(2).broadcast_to([128, NT, 16]), ALU.mult, r=[b_cs, b_rstdq], w=[b_csq])
            if debug:
                dump(d_mixH, mixH[:], b_mixH[0][0]); dump(d_cq, c_qT[:], b_cq[0]); dump(d_rq, rstdq[:], b_rstdq); dump(d_kr, None, None) if False else None
            kb.barrier()

        ar["lo"] = markA["lo"]
        if True:
            stB = None
            Wout = T(stB, "Wout", [128, 8, D], BF16); b_Wout = Buf("Wout")
            st_wout = kb.stream()
            for ecx in range(8):
                dma("pool", st_wout, Wout[:, ecx, :], w_out_d[ecx * 128:(ecx + 1) * 128, :], r=[], w=[b_Wout])
            gpost_b = T(stB, "gpost_b", [128, D], F32); b_gpost = Buf("gpost")
            dma("sp", kb.stream(), gpost_b[:], g_post_d.partition_broadcast(128), r=[], w=[b_gpost])
            markC = ar["lo"]
            QT = [T(stB, "QT%d" % i, [128, S], BF16) for i in range(2)]; b_QT = [[Buf("QT%d_%d" % (i, b)) for b in range(NB)] for i in range(2)]
            KT = [T(stB, "KT%d" % i, [128, S], BF16) for i in range(2)]; b_KT = [[Buf("KT%d_%d" % (i, b)) for b in range(NB)] for i in range(2)]
            Vh = [T(stB, "Vh%d" % i, [128, NT, 128], BF16) for i in range(2)]; b_Vh = [[Buf("Vh%d_%d" % (i, b)) for b in range(NB)] for i in range(2)]
            q_tm2 = [T(stB, "q_tm%d" % i, [128, 4, 96], BF16) for i in range(2)]; k_tm2 = [T(stB, "k_tm%d" % i, [128, 4, 96], BF16) for i in range(2)]
            b_qtm2 = [Buf("qtm0"), Buf("qtm1")]; b_ktm2 = [Buf("ktm0"), Buf("ktm1")]
            rt2 = T(stB, "rt2", [128, 2, 4, 16], F32); b_rt2 = Buf("rt2")
            PT = [T(stB, "PT%d" % i, [128, 1024], BF16) for i in range(3)]; b_PT = [Buf("PT%d" % i) for i in range(3)]
            sqw = T(stB, "sqw", [128, 512], BF16); b_sqw = Buf("sqw")
            rs_t = T(stB, "rs_t", [128, 512], F32); b_rs = Buf("rs_t")
            kb.op("pool", lambda e: e.memset(Vh[0][:], 0.0), w=b_Vh[0])
            kb.op("pool", lambda e: e.memset(Vh[1][:], 0.0), w=b_Vh[1])
            kb.op("pool", lambda e: e.memset(Vh[0][:, :, 64:65], 1.0), w=b_Vh[0])
            kb.op("pool", lambda e: e.memset(Vh[1][:, :, 0:1], 1.0), w=b_Vh[1])
            if True:
                pt_state = {"ctr": 0}
                pair_state = {"ctr": 0}
                NPT = 3

                def prep_head(h):
                    hp = h % 2
                    voff = 0 if hp == 0 else 64

                    def seg1(b):
                        q_tm = q_tm2[b % 2]; k_tm = k_tm2[b % 2]; b_qtm = b_qtm2[b % 2]; b_ktm = b_ktm2[b % 2]
                        t4 = slice(4 * b, 4 * b + 4)
                        bq = bank(); bkv = bank()
                        for i in range(4):
                            tok = slice(b * 512 + i * 128, b * 512 + (i + 1) * 128)
                            for rc in range(3):
                                mm(pf(bq)[:, i * 96:(i + 1) * 96], c_qT[:, rc, tok], Wuq[:, rc, h * 96:(h + 1) * 96], rc == 0, rc == 2,
                                   r=[b_cq[b], b_Wu], w=[PB[bq]])
                            mm(pf(bkv)[:, i * 128:(i + 1) * 128], c_kvT[:, tok], Wukv[:, h * 128:(h + 1) * 128], True, True,
                               r=[b_ckv[b], b_Wu], w=[PB[bkv]])
                        q3 = pf(bq)[:, 0:384].rearrange("p (i c) -> p i c", c=96)
                        kv3 = pf(bkv).rearrange("p (i c) -> p i c", c=128)
                        rq_b = rstdq[:, t4].unsqueeze(2).broadcast_to([128, 4, 64])
                        rkv_b = rstdkv[:, t4].unsqueeze(2).broadcast_to([128, 4, 64])
                        tt("dve", q_tm[:, :, 0:64], q3[:, :, 0:64], rq_b, ALU.mult, r=[b_rstdq], w=[PB[bq], b_qtm])
                        x1 = q3[:, :, 64:80]; x2 = q3[:, :, 80:96]
                        cs = cosq[:, t4, :]; sn = sinq[:, t4, :]
                        tt("dve", rt2[:, 0], x1, cs, ALU.mult, r=[b_csq], w=[PB[bq], b_rt2])
                        tt("dve", rt2[:, 1], x2, sn, ALU.mult, r=[b_csq], w=[PB[bq], b_rt2])
                        tt("dve", q_tm[:, :, 64:80], rt2[:, 0], rt2[:, 1], ALU.subtract, r=[b_rt2], w=[b_qtm])
                        tt("dve", rt2[:, 0], x1, sn, ALU.mult, r=[b_csq], w=[PB[bq], b_rt2])
                        tt("dve", rt2[:, 1], x2, cs, ALU.mult, r=[b_csq], w=[PB[bq], b_rt2])
                        tt("dve", q_tm[:, :, 80:96], rt2[:, 0], rt2[:, 1], ALU.add, r=[b_rt2], w=[b_qtm])
                        tt("dve", k_tm[:, :, 0:64], kv3[:, :, 0:64], rkv_b, ALU.mult, r=[b_rstdkv], w=[PB[bkv], b_ktm])
                        cp("pool", k_tm[:, :, 64:96], krope[:, t4, :], r=[b_krope[b]], w=[b_ktm])
                        tt("dve", Vh[hp][:, t4, voff:voff + 64], kv3[:, :, 64:128], rkv_b, ALU.mult, r=[b_rstdkv], w=[PB[bkv], b_Vh[hp][b]])

                    def seg2(b):
                        q_tm = q_tm2[b % 2]; k_tm = k_tm2[b % 2]; b_qtm = b_qtm2[b % 2]; b_ktm = b_ktm2[b % 2]
                        blk = slice(b * 512, (b + 1) * 512)
                        bk = bank()
                        for i in range(4):
                            tr(pb(bk)[0:96, i * 128:(i + 1) * 128], q_tm[:, i, :], identb[:], r=[b_qtm, b_identb], w=[PB[bk]])
                        cp("act", QT[hp][0:96, blk], pb(bk)[0:96, 0:512], r=[], w=[PB[bk], b_QT[hp][b]])
                        bk = bank()
                        for i in range(4):
                            tr(pb(bk)[0:96, i * 128:(i + 1) * 128], k_tm[:, i, :], identb[:], r=[b_ktm, b_identb], w=[PB[bk]])
                        cp("act", KT[hp][0:96, blk], pb(bk)[0:96, 0:512], r=[], w=[PB[bk], b_KT[hp][b]])

                    seg1(0)
                    yield ("s1", 0)
                    for b in range(NB):
                        if b + 1 < NB:
                            seg1(b + 1)
                            yield ("s1", b + 1)
                        seg2(b)
                        yield ("s2", b)

                def attn_head(h):
                    hp = h % 2
                    voff = 0 if hp == 0 else 64
                    ec = h // 2
                    M = 65 if hp == 0 else 128
                    Mo = 64 if hp == 0 else 128
                    groups = []
                    for qb in range(NB):
                        for j in range(2 * qb):
                            groups.append((qb, [2 * j, 2 * j + 1]))
                        for i in range(4):
                            groups.append((qb, [4 * qb + i]))
                    obank = {}
                    info = {}

                    def emit_qk(gi):
                        qb, kbs = groups[gi]
                        pr = pair_state["ctr"] % 3
                        pair_state["ctr"] += 1
                        b0 = 2 * pr
                        p = pr
                        if len(kbs) == 2:
                            for j, kbi in enumerate(kbs):
                                mm(pf(b0 + j), KT[hp][0:96, kbi * 128:(kbi + 1) * 128], QT[hp][0:96, qb * 512:(qb + 1) * 512], True, True,
                                   r=[b_KT[hp][kbi // 4], b_QT[hp][qb]], w=[PB[b0 + j]])
                            act(PT[p][:, 0:1024], psum[:, b0 * 512:(b0 + 2) * 512], AF.Exp, r=[], w=[PB[b0], PB[b0 + 1], b_PT[p]], scale=SCALE)
                            info[gi] = (p, [(kbs[0], 0, 0), (kbs[1], 512, 0)])
                        else:
                            kbi = kbs[0]
                            i = kbi - 4 * qb
                            qlo = 128 * i
                            qs = slice(qb * 512 + qlo, (qb + 1) * 512)
                            mm(pf(b0)[:, qlo:512], KT[hp][0:96, kbi * 128:(kbi + 1) * 128], QT[hp][0:96, qs], True, False,
                               r=[b_KT[hp][kbi // 4], b_QT[hp][qb]], w=[PB[b0]])
                            mm(pf(b0)[:, qlo:qlo + 128], identb[:], negtrib[:], False, True, r=[b_identb, b_negtri], w=[PB[b0]])
                            act(PT[p][:, qlo:512], pf(b0)[:, qlo:512], AF.Exp, r=[], w=[PB[b0], b_PT[p]], scale=SCALE)
                            info[gi] = (p, [(kbi, 0, qlo)])

                    def emit_pv(gi):
                        qb, kbs = groups[gi]
                        p, lst = info.pop(gi)
                        nkb = 4 * qb + 4
                        for (kbi, off, qlo) in lst:
                            if kbi == 0:
                                obank[qb] = lbank()
                            bo = obank[qb]
                            mm(pf(bo)[0:M, qlo:512], Vh[hp][:, kbi, 0:M], PT[p][:, off + qlo:off + 512], kbi == 0, kbi == nkb - 1,
                               r=[b_Vh[hp][kbi // 4], b_PT[p]], w=[PB[bo]])
                            if kbi == nkb - 1:
                                blk = slice(qb * 512, (qb + 1) * 512)
                                act(sqw[0:M, :], pf(bo)[0:M, :], AF.Square, r=[b_wcol], w=[PB[bo], b_sqw], scale=wcol[0:M, hp:hp + 1])
                                bn = bank()
                                mm(pf(bn)[0:Mo, :], ones_bf[0:M, 0:Mo], sqw[0:M, :], True, True, r=[b_ones, b_sqw], w=[PB[bn]])
                                act(rs_t[0:Mo, :], pf(bn)[0:Mo, :], AF.Ln, r=[], w=[PB[bn], b_rs], scale=1.0 / 64)
                                act(rs_t[0:Mo, :], rs_t[0:Mo, :], AF.Exp, r=[b_rs], w=[b_rs], scale=-0.5)
                                rows = slice(voff, voff + 64)
                                stt("dve", mixM[rows, ec, blk], pf(bo)[rows, :], g_mla[rows, ec:ec + 1], rs_t[rows, :], ALU.mult, ALU.mult,
                                    r=[b_rs, b_gmla], w=[PB[bo], b_mixM])

                    LOOK = 2
                    for gi in range(len(groups)):
                        yield groups[gi][0]
                        emit_qk(gi)
                        if gi >= LOOK:
                            emit_pv(gi - LOOK)
                    yield None
                    for gi in range(max(0, len(groups) - LOOK), len(groups)):
                        emit_pv(gi)

                gp0 = prep_head(0)
                p0_done = [-1]

                def advance_p0(upto):
                    while p0_done[0] < upto:
                        try:
                            tag = next(gp0)
                        except StopIteration:
                            p0_done[0] = NB
                            return
                        if tag[0] == "s2":
                            p0_done[0] = tag[1]

                for h in range(8):
                    ga = attn_head(h)
                    gp = prep_head(h + 1) if h + 1 < 8 else None
                    n_groups = sum(2 * qb + 4 for qb in range(NB))
                    n_seg = 2 * NB
                    every = max(1, n_groups // (n_seg + 1))
                    cnt = 0
                    for need in ga:
                        if h == 0 and need is not None:
                            advance_p0(need)
                        cnt += 1
                        ok = (h != 0) or (p0_done[0] >= NB - 1)
                        ev = every if h != 0 else 1
                        if gp is not None and ok and cnt % ev == 0:
                            try:
                                next(gp)
                            except StopIteration:
                                gp = None
                    if h == 0:
                        advance_p0(NB)
                    if gp is not None:
                        for _ in gp:
                            pass
            if debug:
                dump(d_mixM, mixM[:], b_mixM)
            kb.barrier()

        ar["lo"] = markC; ar["hi"] = ARENA_BYTES
        if True:
            stC = None
            Wg = T("top", "Wg", [128, 8, DFF], BF16); Wu = T("top", "Wu", [128, 8, DFF], BF16)
            NWG = 4
            WGC = DFF // NWG
            b_Wg = [Buf("Wg%d" % i) for i in range(NWG)]; b_Wuu = [Buf("Wu%d" % i) for i in range(NWG)]
            sg_ = kb.stream(); su_ = kb.stream()
            bwg1 = Buf("Wg_all"); bwu1 = Buf("Wu_all")
            b_Wg = [bwg1] * NWG; b_Wuu = [bwu1] * NWG
            for dc in range(8):
                dma("pool", sg_, Wg[:, dc, :], w_gate_d[dc * 128:(dc + 1) * 128, :], r=[], w=[bwg1])
                dma("pool", su_, Wu[:, dc, :], w_up_d[dc * 128:(dc + 1) * 128, :], r=[], w=[bwu1])
            NXC = 3
            xc = [T(stC, "xc%d" % i, [128, D], F32) for i in range(NXC)]; b_xc = [Buf("xc%d" % i) for i in range(NXC)]
            st_xc = [kb.stream() for _ in range(NXC)]
            NYC = 3
            yc = [T(stC, "yc%d" % i, [128, D], F32) for i in range(NYC)]; b_yc = [Buf("yc%d" % i) for i in range(NYC)]
            st_oc = [kb.stream() for _ in range(NYC)]
            junkc = T(stC, "junkc", [128, 512], BF16); b_junkc = Buf("junkc")
            ssc = T(stC, "ssc", [128, 2, 3], F32); b_ssc = [Buf("ssc0"), Buf("ssc1")]
            print("arena phase C: lo", ar["lo"], "hi", ar["hi"])

            def ldx(g):
                xs = g % NXC
                dma("sp", st_xc[xs], xc[xs][:], x_d[g * 128:(g + 1) * 128, :], r=[], w=[b_xc[xs]])

            for g in range(min(NXC, NT)):
                ldx(g)
            for g in range(NT):
                sl = g % 2
                xs = g % NXC
                ys = g % NYC
                tok = slice(g * 128, (g + 1) * 128)
                by = [bank(), bank()]
                for hh in range(2):
                    for ecx in range(8):
                        lhs = mixM[:, ecx, tok] if ecx < 4 else mixH[:, ecx - 4, tok]
                        mm(pf(by[hh]), lhs, Wout[:, ecx, hh * 512:(hh + 1) * 512], ecx == 0, ecx == 7,
                           r=[b_mixM, b_Wout] + [b_mixH[hx][g // 4] for hx in range(4)], w=[PB[by[hh]]])
                    act(junkc[:], pf(by[hh]), AF.Square, r=[], w=[PB[by[hh]], b_junkc, b_ssc[sl]], accum=ssc[:, sl, hh:hh + 1])
                tt("dve", ssc[:, sl, 2:3], ssc[:, sl, 0:1], ssc[:, sl, 1:2], ALU.add, r=[b_ssc[sl]], w=[b_ssc[sl]])
                rsqrt_act(ssc[:, sl, 2:3], ssc[:, sl, 2:3], 1.0 / D, r=[b_ssc[sl], b_eps], w=[b_ssc[sl]])
                for hh in range(2):
                    cs_ = slice(hh * 512, (hh + 1) * 512)
                    tt("dve", yc[ys][:, cs_], pf(by[hh]), gpost_b[:, cs_], ALU.mult, r=[b_gpost], w=[PB[by[hh]], b_yc[ys]])
                stt("dve", yc[ys][:], yc[ys][:], ssc[:, sl, 2:3], xc[xs][:], ALU.mult, ALU.add,
                    r=[b_ssc[sl], b_xc[xs], b_yc[ys]], w=[b_yc[ys]])
                dma("sp", st_oc[ys], out_d[tok, :], yc[ys][:], r=[b_yc[ys]], w=[ob[g]])
                if g + NXC < NT:
                    ldx(g + NXC)
            kb.barrier()

        ar["lo"] = markMix
        if True:
            stD = None
            Wd = T(stD, "Wd", [128, NFC, D], BF16)
            b_Wd = [Buf("Wd0"), Buf("Wd1")]
            st_wd = [kb.stream(), kb.stream()]
            for fc in range(NFC):
                j = 0 if fc < NFC // 2 else 1
                dma("pool", st_wd[j], Wd[:, fc, :], w_down_d[fc * 128:(fc + 1) * 128, :], r=[], w=[b_Wd[j]])
            gfpre_b = T(stD, "gfpre_b", [128, D], F32); gfpost_b = T(stD, "gfpost_b", [128, D], F32); b_gfpre = Buf("gfpre"); b_gfpost = Buf("gfpost")
            dma("sp", kb.stream(), gfpre_b[:], g_fpre_d.partition_broadcast(128), r=[], w=[b_gfpre])
            dma("sp", kb.stream(), gfpost_b[:], g_fpost_d.partition_broadcast(128), r=[], w=[b_gfpost])
            h1 = [T(stD, "h1_%d" % i, [128, D], F32) for i in range(2)]; b_h1 = [Buf("h1_%d" % i) for i in range(2)]
            zb = T(stD, "zb", [128, D], BF16); b_zb = Buf("zb")
            zT = T(stD, "zT", [128, 8, 512], BF16); b_zT = Buf("zT")
            ffT = T(stD, "ffT", [128, NFC, 512], BF16); b_ffTs = [Buf("ffT%d" % i) for i in range(NFC)]
            sg = [T(stD, "sg%d" % i, [128, 512], F32) for i in range(2)]; b_sg = [Buf("sg%d" % i) for i in range(2)]
            junkd = T(stD, "junkd", [128, D], BF16); b_junkd = Buf("junkd")
            yd = [T(stD, "yd%d" % i, [128, D], F32) for i in range(2)]; b_yd = [Buf("yd%d" % i) for i in range(2)]
            ssd = T(stD, "ssd", [128, 2, 3], F32); b_ssd = [Buf("ssd0"), Buf("ssd1")]
            ssp = T(stD, "ssp", [128, 2], F32); b_ssp = Buf("ssp")
            print("arena phase D: lo", ar["lo"], "hi", ar["hi"])

            def s1(bb, i):
                g = 4 * bb + i
                tok = slice(g * 128, (g + 1) * 128)
                dma("sp", st_x[0], h1[0][:], out_d[tok, :], r=[ob[g]], w=[b_h1[0]])
                act(junkd[:], h1[0][:], AF.Square, r=[b_h1[0]], w=[b_junkd, b_ssp], accum=ssp[:, 0:1])
                rsqrt_act(ssp[:, 0:1], ssp[:, 0:1], 1.0 / D, r=[b_ssp, b_eps], w=[b_ssp])
                stt("dve", zb[:], h1[0][:], ssp[:, 0:1], gfpre_b[:], ALU.mult, ALU.mult, r=[b_h1[0], b_ssp, b_gfpre], w=[b_zb])

            def s2(bb, i):
                bk = bank()
                for dc in range(8):
                    tr(pb(bk)[:, dc * 128:(dc + 1) * 128], zb[:, dc * 128:(dc + 1) * 128], identb[:], r=[b_zb, b_identb], w=[PB[bk]])
                cp("act", zT[:, :, i * 128:(i + 1) * 128], pb(bk).rearrange("p (c t) -> p c t", t=128), r=[], w=[PB[bk], b_zT])

            def gateup(bb):
                for fc in range(NFC):
                    bg_ = bank(); bu_ = bank()
                    for dc in range(8):
                        mm(pf(bg_), Wg[:, dc, fc * 128:(fc + 1) * 128], zT[:, dc, :], dc == 0, dc == 7, r=[b_Wg[fc * 128 // WGC], b_zT], w=[PB[bg_]])
                    for dc in range(8):
                        mm(pf(bu_), Wu[:, dc, fc * 128:(fc + 1) * 128], zT[:, dc, :], dc == 0, dc == 7, r=[b_Wuu[fc * 128 // WGC], b_zT], w=[PB[bu_]])
                    s = fc % 2
                    act(sg[s][:], pf(bg_), AF.Silu, r=[], w=[PB[bg_], b_sg[s]])
                    tt("dve", ffT[:, fc, :], pf(bu_), sg[s][:], ALU.mult, r=[b_sg[s]], w=[PB[bu_], b_ffTs[fc]])

            def down(bb, i):
                g = 4 * bb + i
                sl = g % 2
                tok = slice(g * 128, (g + 1) * 128)
                dma("sp", st_x[1], h1[1][:], out_d[tok, :], r=[ob[g]], w=[b_h1[1]])
                bd = [bank(), bank()]
                for hh in range(2):
                    for fc in range(NFC):
                        mm(pf(bd[hh]), ffT[:, fc, i * 128:(i + 1) * 128], Wd[:, fc, hh * 512:(hh + 1) * 512], fc == 0, fc == NFC - 1,
                           r=[b_ffTs[fc], b_Wd[0 if fc < NFC // 2 else 1]], w=[PB[bd[hh]]])
                    act(junkd[:, 0:512], pf(bd[hh]), AF.Square, r=[], w=[PB[bd[hh]], b_junkd, b_ssd[sl]], accum=ssd[:, sl, hh:hh + 1])
                tt("dve", ssd[:, sl, 2:3], ssd[:, sl, 0:1], ssd[:, sl, 1:2], ALU.add, r=[b_ssd[sl]], w=[b_ssd[sl]])
                rsqrt_act(ssd[:, sl, 2:3], ssd[:, sl, 2:3], 1.0 / D, r=[b_ssd[sl], b_eps], w=[b_ssd[sl]])
                for hh in range(2):
                    cs_ = slice(hh * 512, (hh + 1) * 512)
                    tt("dve", yd[sl][:, cs_], pf(bd[hh]), gfpost_b[:, cs_], ALU.mult, r=[b_gfpost], w=[PB[bd[hh]], b_yd[sl]])
                stt("dve", yd[sl][:], yd[sl][:], ssd[:, sl, 2:3], h1[1][:], ALU.mult, ALU.add,
                    r=[b_ssd[sl], b_h1[1], b_yd[sl]], w=[b_yd[sl]])
                dma("pool", st_o[sl], out_d[tok, :], yd[sl][:], r=[b_yd[sl]], w=[ob[g]])

            for i in range(4):
                s1(0, i)
                s2(0, i)
            for bb in range(NB):
                gateup(bb)
                for i in range(4):
                    if bb + 1 < NB:
                        s1(bb + 1, i)
                    down(bb, i)
                    if bb + 1 < NB:
                        s2(bb + 1, i)
            kb.wait_all("sp", ob)
        kb.emit()
    return nc


def _consts():
    ident = np.eye(128, dtype=np.float32)
    tri = (np.arange(128)[:, None] <= np.arange(128)[None, :]).astype(np.float32)
    s = np.arange(128)[:, None]
    t = np.arange(128)[None, :]
    mask2 = ((s // 64 == t // 64) & (s <= t)).astype(np.float32)
    resetm = (np.arange(512) % 64 != 0).astype(np.float32)
    invf64 = 1.0 / (10000.0 ** (np.arange(0, 32, 2, dtype=np.float64) / 32.0))
    invf = invf64.astype(np.float32)
    invf_lo = (invf64 - invf.astype(np.float64)).astype(np.float32)
    wcol = np.zeros((128, 2), np.float32)
    c = np.float32(np.sqrt(64 * EPS))
    wcol[0:64, 0] = 1.0
    wcol[64, 0] = c
    wcol[0, 1] = c
    wcol[64:128, 1] = 1.0
    negtri = ((tri - 1.0) * 30000.0).astype(np.float32)
    return dict(ident=ident, tri=tri, negtri=negtri, mask2=mask2, resetm=resetm, invf=invf, invf_lo=invf_lo, wcol=wcol)


def _pcol(v, n):
    return np.ascontiguousarray(np.asarray(v, np.float32).reshape(n, 128).T)


def make_in_maps(inputs, S, batch_ids):
    f = lambda a: np.ascontiguousarray(np.asarray(a, np.float32))
    shared = dict(
        w_in=f(inputs["w_in"][0]), w_uq=f(inputs["mla_w_uq"][0]).reshape(384, 768),
        w_ukv=f(inputs["mla_w_ukv"][0]).reshape(128, 1024), w_out=f(inputs["w_out"][0]),
        w_gate=f(inputs["w_gate"][0]), w_up=f(inputs["w_up"][0]), w_down=f(inputs["w_down"][0]),
        g_pre=f(inputs["attn_pre_norm"][0]), g_post=f(inputs["attn_post_norm"][0]),
        g_fpre=f(inputs["ffn_pre_norm"][0]), g_fpost=f(inputs["ffn_post_norm"][0]),
        g_q=_pcol(inputs["mla_q_norm"][0], 3), g_kv=_pcol(inputs["mla_kv_norm"][0], 1),
        g_mla=_pcol(inputs["mla_out_norm"][0], 4), g_hn=_pcol(inputs["hgrn_out_norm"][0], 4),
        lbl=np.ascontiguousarray(np.concatenate([_pcol(inputs["hgrn_lb_logits"][0], 4), _pcol(inputs["hgrn_lb_logits"][1], 4)], axis=1)),
    )
    shared.update(_consts())
    maps = []
    NT = S // 128
    for b in batch_ids:
        m = dict(shared)
        m["x"] = f(inputs["x"][b])
        m["pos"] = np.ascontiguousarray(np.asarray(inputs["positions"][b], np.int32).reshape(NT, 128).T)
        maps.append(m)
    return maps


_NC_CACHE = {}


def kernel(**inputs):
    x = np.asarray(inputs["x"])
    B, S, _ = x.shape
    if S not in _NC_CACHE:
        _NC_CACHE[S] = build(S)
    nc = _NC_CACHE[S]
    maps = make_in_maps(inputs, S, list(range(B)))
    res = run_bass_kernel_spmd(nc, maps, core_ids=list(range(B)))
    out = np.stack([np.asarray(r["out"], np.float32) for r in res.results], axis=0)
    return out.astype(np.float32)
```

```python
import contextlib
import numpy as np
import concourse.bass as bass
import concourse.mybir as mybir

F32 = mybir.dt.float32
BF16 = mybir.dt.bfloat16
I32 = mybir.dt.int32
AF = mybir.ActivationFunctionType
ALU = mybir.AluOpType
AX = mybir.AxisListType

ENGS = ("pe", "act", "dve", "pool", "sp")
STRICT = True


class Buf:
    __slots__ = ("name", "last_w", "readers")

    def __init__(self, name):
        self.name = name
        self.last_w = None
        self.readers = {}


class _Op:
    __slots__ = ("fn", "waits", "sig", "dma")

    def __init__(self, fn, waits, sig, dma):
        self.fn, self.waits, self.sig, self.dma = fn, waits, sig, dma


class KB:
    def __init__(self, nc):
        self.nc = nc
        self.prog = {e: [] for e in ENGS}
        self.nsig = {e: 0 for e in ENGS}
        self.marked = {e: set() for e in ENGS}
        self.seen = {e: {} for e in ENGS}
        self.streams = []

    def stream(self, name=None):
        key = "dma%d" % len(self.streams)
        self.streams.append(key)
        self.nsig[key] = 0
        self.marked[key] = None
        return key

    def _need(self, eng, waits, sig, same_ok):
        if sig is None:
            return
        key, idx = sig
        if key == eng and not same_ok:
            return
        if self.seen[eng].get(key, -1) >= idx:
            return
        waits[key] = max(waits.get(key, -1), idx)

    def _deps(self, eng, r, w):
        waits = {}
        for b in r:
            self._need(eng, waits, b.last_w, True)
        for b in w:
            strict = STRICT and eng != "pe" and not b.name.startswith("bank")
            self._need(eng, waits, b.last_w, strict)
            for k, i in b.readers.items():
                self._need(eng, waits, (k, i), strict)
        for k, i in waits.items():
            self.seen[eng][k] = i
            if self.marked[k] is not None:
                self.marked[k].add(i)
        return list(waits.items())

    def op(self, eng, fn, r=(), w=()):
        waits = self._deps(eng, r, w)
        idx = self.nsig[eng]
        self.nsig[eng] += 1
        sig = (eng, idx)
        for b in r:
            b.readers[eng] = idx
        for b in w:
            b.last_w = sig
            b.readers = {}
        self.prog[eng].append(_Op(fn, waits, sig, False))
        return sig

    def dma(self, eng, stream, fn, r=(), w=()):
        waits = self._deps(eng, r, w)
        idx = self.nsig[stream]
        self.nsig[stream] += 1
        sig = (stream, idx)
        for b in r:
            b.readers[stream] = idx
        for b in w:
            b.last_w = sig
            b.readers = {}
        self.prog[eng].append(_Op(fn, waits, sig, True))
        return sig

    def wait_all(self, eng, bufs):
        waits = {}
        for b in bufs:
            self._need(eng, waits, b.last_w, True)
            for k, i in b.readers.items():
                self._need(eng, waits, (k, i), True)
        for k, i in waits.items():
            self.seen[eng][k] = i
            if self.marked[k] is not None:
                self.marked[k].add(i)
        self.prog[eng].append(_Op(None, list(waits.items()), None, False))

    def barrier(self, streams=True):
        for e in ENGS:
            waits = {}
            for k in list(ENGS) + (self.streams if streams else []):
                if k == e or self.nsig[k] == 0:
                    continue
                self._need(e, waits, (k, self.nsig[k] - 1), True)
            for k, i in waits.items():
                self.seen[e][k] = i
                if self.marked[k] is not None:
                    self.marked[k].add(i)
            self.prog[e].append(_Op(None, list(waits.items()), None, False))

    def emit(self):
        nc = self.nc
        with contextlib.ExitStack() as st:
            sems = {}
            for k in list(ENGS) + self.streams:
                sems[k] = st.enter_context(nc.semaphore("s_" + k))
            rank = {}
            for k in ENGS:
                m = sorted(self.marked[k])
                rank[k] = {i: n + 1 for n, i in enumerate(m)}

            def val(k, i):
                if self.marked[k] is None:
                    return 16 * (i + 1)
                return rank[k][i]

            def run(e, eng):
                for o in self.prog[e]:
                    for k, i in o.waits:
                        eng.wait_ge(sems[k], val(k, i))
                    if o.fn is None:
                        continue
                    ins = o.fn(eng)
                    k, i = o.sig
                    if o.dma:
                        ins.then_inc(sems[k], 16)
                    elif i in self.marked[k]:
                        ins.then_inc(sems[k], 1)

            block = st.enter_context(nc.Block())

            @block.tensor
            def _(e):
                run("pe", e)

            @block.scalar
            def _(e):
                run("act", e)

            @block.vector
            def _(e):
                run("dve", e)

            @block.gpsimd
            def _(e):
                run("pool", e)

            @block.sync
            def _(e):
                run("sp", e)
from concourse.bass_utils import run_bass_kernel_spmd

D = 1024
DIN = 2592
DFF = 2816
NFC = DFF // 128
EPS = 1e-6
SCALE = 96 ** -0.5
C_CQ, C_CKV, C_KR, C_HQ, C_HF, C_HI, C_HG = 0, 384, 512, 544, 1056, 1568, 2080


def build(S, debug=False, PIPE=True):
    NT = S // 128
    NB = S // 512
    nc = bass.Bass("TRN2", target_bir_lowering=False)
    dt_in = lambda n, shp, dt=F32: nc.dram_tensor(n, list(shp), dt, kind="ExternalInput").ap()
    x_d = dt_in("x", [S, D])
    pos_d = dt_in("pos", [128, NT], I32)
    w_in_d = dt_in("w_in", [D, DIN]); w_uq_d = dt_in("w_uq", [384, 768]); w_ukv_d = dt_in("w_ukv", [128, 1024])
    w_out_d = dt_in("w_out", [D, D]); w_gate_d = dt_in("w_gate", [D, DFF]); w_up_d = dt_in("w_up", [D, DFF])
    w_down_d = dt_in("w_down", [DFF, D])
    g_pre_d = dt_in("g_pre", [D]); g_post_d = dt_in("g_post", [D]); g_fpre_d = dt_in("g_fpre", [D]); g_fpost_d = dt_in("g_fpost", [D])
    g_q_d = dt_in("g_q", [128, 3]); g_kv_d = dt_in("g_kv", [128, 1]); g_mla_d = dt_in("g_mla", [128, 4]); g_hn_d = dt_in("g_hn", [128, 4])
    lbl_d = dt_in("lbl", [128, 8])
    ident_d = dt_in("ident", [128, 128]); tri_d = dt_in("tri", [128, 128]); negtri_d = dt_in("negtri", [128, 128]); mask2_d = dt_in("mask2", [128, 128])
    resetm_d = dt_in("resetm", [512]); invf_d = dt_in("invf", [16]); invf_lo_d = dt_in("invf_lo", [16]); wcol_d = dt_in("wcol", [128, 2])
    out_d = nc.dram_tensor("out", [S, D], F32, kind="ExternalOutput").ap()
    dbg = {}
    def DBG(name, shape):
        if debug:
            dbg[name] = nc.dram_tensor(name, list(shape), F32, kind="ExternalOutput").ap()
        return dbg.get(name)
    d_mixH = DBG("d_mixH", [128, 4, S]); d_mixM = DBG("d_mixM", [128, 4, S]); d_h1 = None
    d_cq = DBG("d_cq", [128, 3, S]); d_rq = DBG("d_rq", [128, NT]); d_kr = DBG("d_kr", [128, NT, 32])

    kb = KB(nc)
    ES = contextlib.ExitStack
    with ES() as st0:
        ARENA_BYTES = 211968
        arena = st0.enter_context(nc.sbuf_tensor("arena", [128, ARENA_BYTES // 2], BF16))
        ar = {"lo": 0, "hi": ARENA_BYTES}
        def _view(off, shape, dt):
            esz = 2 if dt == BF16 else 4
            n = 1
            for d_ in shape[1:]:
                n *= d_
            v = arena[:, off // 2: off // 2 + n * esz // 2]
            if esz == 4:
                v = v.bitcast(dt)
            if len(shape) > 2:
                names = ["a%d" % i for i in range(len(shape) - 1)]
                kw = {nm: d_ for nm, d_ in zip(names[1:], shape[2:])}
                v = v.rearrange("p (%s) -> p %s" % (" ".join(names), " ".join(names)), **kw)
            if shape[0] < 128:
                v = v[0:shape[0]]
            return v
        def T(st, name, shape, dt):
            esz = 2 if dt == BF16 else 4
            n = esz
            for d_ in shape[1:]:
                n *= d_
            n = (n + 63) // 64 * 64
            if st == "top":
                ar["hi"] -= n
                off = ar["hi"]
            else:
                off = ar["lo"]
                ar["lo"] += n
            assert ar["lo"] <= ar["hi"], ("SBUF arena overflow", name, ar)
            return _view(off, list(shape), dt)
        psum = st0.enter_context(nc.psum_tensor("psum", [128, 4096], F32))
        PB = [Buf("bank%d" % i) for i in range(8)]
        bank_ctr = [0]
        def bank():
            i = bank_ctr[0] % 6
            bank_ctr[0] += 1
            return i
        lbank_ctr = [0]
        def lbank():
            i = 6 + lbank_ctr[0] % 2
            lbank_ctr[0] += 1
            return i
        def idle_lbank():
            return 6 + lbank_ctr[0] % 2
        def pf(i):
            return psum[:, i * 512:(i + 1) * 512]
        def pb(i):
            return psum[:, i * 512:(i + 1) * 512].bitcast(BF16)
        ob = [Buf("out%d" % g) for g in range(S // 128)]
        st_ld = [kb.stream() for _ in range(4)]
        st_w = kb.stream()
        st_x = [kb.stream() for _ in range(2)]
        st_o = [kb.stream() for _ in range(2)]
        st_dbg = kb.stream()

        def mm(out, lhsT, rhs, start, stop, r, w):
            kb.op("pe", lambda e: e.matmul(out, lhsT=lhsT, rhs=rhs, start=start, stop=stop), r=r, w=w)
        def tr(out, in_, ident, r, w):
            kb.op("pe", lambda e: e.transpose(out=out, in_=in_, identity=ident), r=r, w=w)
        def act(out, in_, func, r, w, scale=1.0, bias=0.0, accum=None):
            if accum is None:
                kb.op("act", lambda e: e.activation(out=out, in_=in_, func=func, bias=bias, scale=scale), r=r, w=w)
            else:
                kb.op("act", lambda e: e.activation(out=out, in_=in_, func=func, bias=bias, scale=scale, accum_out=accum), r=r, w=w)
        def ts(eng, out, in0, s1, s2, op0, op1, r, w):
            if s2 is None:
                kb.op(eng, lambda e: e.tensor_scalar(out=out, in0=in0, scalar1=s1, scalar2=None, op0=op0), r=r, w=w)
            else:
                kb.op(eng, lambda e: e.tensor_scalar(out=out, in0=in0, scalar1=s1, scalar2=s2, op0=op0, op1=op1), r=r, w=w)
        def tt(eng, out, in0, in1, op, r, w):
            kb.op(eng, lambda e: e.tensor_tensor(out=out, in0=in0, in1=in1, op=op), r=r, w=w)
        def stt(eng, out, in0, scalar, in1, op0, op1, r, w):
            kb.op(eng, lambda e: e.scalar_tensor_tensor(out=out, in0=in0, scalar=scalar, in1=in1, op0=op0, op1=op1), r=r, w=w)
        def cp(eng, out, in_, r, w):
            if eng == "act":
                kb.op("act", lambda e: e.copy(out=out, in_=in_), r=r, w=w)
            else:
                kb.op(eng, lambda e: e.tensor_copy(out=out, in_=in_), r=r, w=w)
        def recip(out, in_, r, w):
            kb.op("dve", lambda e: e.reciprocal(out=out, in_=in_), r=r, w=w)
        def dma(eng, stream, out, in_, r, w):
            kb.dma(eng, stream, lambda e: e.dma_start(out=out, in_=in_), r=r, w=w)
        def rsqrt_act(out, in_, scale, r, w):
            act(out, in_, AF.Ln, r=r, w=w, scale=scale, bias=epsb[:in_.shape[0], 0:1])
            act(out, out, AF.Exp, r=w, w=w, scale=-0.5)
        def dump(dap, tile_ap, buf):
            if dap is not None:
                dma("sp", st_dbg, dap, tile_ap, r=[buf], w=[Buf("dbgout")])

        identb = T(st0, "identb", [128, 128], BF16); b_identb = Buf("identb")
        identf = T(st0, "identf", [128, 128], F32); b_identf = Buf("identf")
        trib = T(st0, "trib", [128, 128], BF16); b_trib = Buf("trib")
        negtrib = T(st0, "negtrib", [128, 128], BF16); b_negtri = Buf("negtrib")
        mask2 = T(st0, "mask2", [128, 128], F32); b_mask2 = Buf("mask2")
        ones_bf = T(st0, "ones_bf", [128, 128], BF16); b_ones = Buf("ones")
        epsb = T(st0, "epsb", [128, 1], F32); b_eps = Buf("epsb")
        wcol = T(st0, "wcol", [128, 2], F32); b_wcol = Buf("wcol")
        g_q = T(st0, "g_q", [128, 3], F32); g_kv = T(st0, "g_kv", [128, 1], F32)
        g_mla = T(st0, "g_mla", [128, 4], F32); g_hn = T(st0, "g_hn", [128, 4], F32)
        b_gq = Buf("g_q"); b_gkv = Buf("g_kv"); b_gmla = Buf("g_mla"); b_ghn = Buf("g_hn")
        lb = T(st0, "lb", [128, 4], F32); oml = T(st0, "oml", [128, 4], F32); b_lb = Buf("lb")
        fA = T(st0, "fA", [128, 4], F32); fB = T(st0, "fB", [128, 4], F32)
        rstdq = T(st0, "rstdq", [128, NT], F32); b_rstdq = Buf("rstdq")
        rstdkv = T(st0, "rstdkv", [128, NT], F32); b_rstdkv = Buf("rstdkv")
        Wuq = T(st0, "Wuq", [128, 3, 768], BF16); Wukv = T(st0, "Wukv", [128, 1024], BF16); b_Wu = Buf("Wu")
        markMix = ar["lo"]
        mixH = T(st0, "mixH", [128, 4, S], BF16); b_mixH = [[Buf("mixH%d_%d" % (h, b)) for b in range(NB)] for h in range(4)]
        c_qT = T("top", "c_qT", [128, 3, S], BF16); b_cq = [Buf("cq%d" % b) for b in range(NB)]
        c_kvT = T("top", "c_kvT", [128, S], BF16); b_ckv = [Buf("ckv%d" % b) for b in range(NB)]
        krope = T("top", "krope", [128, NT, 32], BF16); b_krope = [Buf("krope%d" % b) for b in range(NB)]
        cosq = T("top", "cosq", [128, NT, 16], F32); sinq = T("top", "sinq", [128, NT, 16], F32); b_csq = Buf("cossinq")

        dma("pool", kb.stream(), identb[:], ident_d, r=[], w=[b_identb])
        dma("sp", kb.stream(), identf[:], ident_d, r=[], w=[b_identf])
        dma("pool", kb.stream(), trib[:], tri_d, r=[], w=[b_trib])
        dma("pool", kb.stream(), negtrib[:], negtri_d, r=[], w=[b_negtri])
        dma("sp", kb.stream(), mask2[:], mask2_d, r=[], w=[b_mask2])
        dma("sp", kb.stream(), wcol[:], wcol_d, r=[], w=[b_wcol])
        dma("sp", kb.stream(), g_q[:], g_q_d, r=[], w=[b_gq])
        dma("sp", kb.stream(), g_kv[:], g_kv_d, r=[], w=[b_gkv])
        dma("sp", kb.stream(), g_mla[:], g_mla_d, r=[], w=[b_gmla])
        dma("sp", kb.stream(), g_hn[:], g_hn_d, r=[], w=[b_ghn])
        kb.op("pool", lambda e: e.memset(ones_bf[:], 1.0), w=[b_ones])
        kb.op("pool", lambda e: e.memset(epsb[:], EPS), w=[b_eps])

        mixM = T(st0, "mixM", [128, 4, S], BF16); b_mixM = Buf("mixM")
        markA = dict(ar)
        ar["lo"] = markA["lo"] - 4 * S * 2
        if True:
            stA = None
            W_in = T(stA, "W_in", [128, 8, DIN], BF16)
            WG = {"cq": (0, 544), "hi": (C_HI, C_HI + 512), "hq": (C_HQ, C_HQ + 512), "hf": (C_HF, C_HF + 512), "hg": (C_HG, C_HG + 512)}
            b_Wg_in = {k: Buf("W_in_" + k) for k in WG}
            stw_k = kb.stream()
            b_Wall = Buf("W_in_all")
            for k in WG:
                b_Wg_in[k] = b_Wall
            for dc in range(8):
                dma("pool", stw_k, W_in[:, dc, :], w_in_d[dc * 128:(dc + 1) * 128, :], r=[], w=[b_Wall])
            st_wu = kb.stream()
            for rc in range(3):
                dma("pool", st_wu, Wuq[:, rc, :], w_uq_d[rc * 128:(rc + 1) * 128, :], r=[], w=[b_Wu])
            dma("pool", st_wu, Wukv[:], w_ukv_d, r=[], w=[b_Wu])
            def b_Win_for(col):
                for k, (c0, c1) in WG.items():
                    if c0 <= col < c1:
                        return b_Wg_in[k]
                raise AssertionError(col)
            gpre_b = T(stA, "gpre_b", [128, D], F32); b_gpre = Buf("gpre")
            dma("sp", kb.stream(), gpre_b[:], g_pre_d.partition_broadcast(128), r=[], w=[b_gpre])
            resetm = T(stA, "resetm", [128, 512], F32); b_resetm = Buf("resetm")
            dma("sp", kb.stream(), resetm[:], resetm_d.partition_broadcast(128), r=[], w=[b_resetm])
            cosT = T(stA, "cosT", [128, NT, 16], F32); sinT = T(stA, "sinT", [128, NT, 16], F32); b_cs = Buf("cossin")
            lbl = T(stA, "lbl", [128, 8], F32); b_lbl = Buf("lbl")
            dma("sp", kb.stream(), lbl[:], lbl_d, r=[], w=[b_lbl])
            tt("dve", lb[:], lbl[:, 4:8], lbl[:, 0:4], ALU.subtract, r=[b_lbl], w=[b_lb])
            act(lb[:], lb[:], AF.Exp, r=[b_lb], w=[b_lb])
            ts("dve", lb[:], lb[:], 1.0, None, ALU.add, None, r=[b_lb], w=[b_lb])
            recip(lb[:], lb[:], r=[b_lb], w=[b_lb])
            ts("dve", oml[:], lb[:], -1.0, 1.0, ALU.mult, ALU.add, r=[b_lb], w=[b_lb])
            ts("dve", fA[:], oml[:], 0.5, None, ALU.mult, None, r=[b_lb], w=[b_lb])
            tt("dve", fB[:], lb[:], fA[:], ALU.add, r=[b_lb], w=[b_lb])
            markR = dict(ar)
            if True:
                stR = None
                posi = T(stR, "posi", [128, NT], I32); posf = T(stR, "posf", [128, NT], F32)
                invf = T(stR, "invf", [128, 16], F32)
                invf_lo = T(stR, "invf_lo", [128, 16], F32)
                ang = T(stR, "ang", [128, NT, 16], F32); a2 = T(stR, "a2", [128, NT, 16], F32)
                nq = T(stR, "nq", [128, NT, 16], F32); ni = T(stR, "ni", [128, NT, 16], I32)
                b_r = Buf("ropetmp")
                b_rp = Buf("rope_pos"); b_ri = Buf("rope_invf"); b_ril = Buf("rope_invf_lo")
                dma("sp", kb.stream(), posi[:], pos_d, r=[], w=[b_rp])
                dma("sp", kb.stream(), invf[:], invf_d.partition_broadcast(128), r=[], w=[b_ri])
                dma("sp", kb.stream(), invf_lo[:], invf_lo_d.partition_broadcast(128), r=[], w=[b_ril])
                cp("dve", posf[:], posi[:], r=[b_rp], w=[b_r])
                tt("dve", ang[:], posf[:].unsqueeze(2).broadcast_to([128, NT, 16]),
                   invf[:].unsqueeze(1).broadcast_to([128, NT, 16]), ALU.mult, r=[b_r, b_ri], w=[b_r])
                tt("dve", nq[:], posf[:].unsqueeze(2).broadcast_to([128, NT, 16]),
                   invf_lo[:].unsqueeze(1).broadcast_to([128, NT, 16]), ALU.mult, r=[b_r, b_ril], w=[b_r])
                tt("dve", ang[:], ang[:], nq[:], ALU.add, r=[b_r], w=[b_r])
                TWO_PI = float(2 * np.pi)
                HI = float(np.float32(6.28125)); LO = float(np.float32(2 * np.pi - 6.28125))
                def reduce_sin(dst, shift):
                    ts("dve", a2[:], ang[:], shift, None, ALU.add, None, r=[b_r], w=[b_r])
                    ts("dve", nq[:], a2[:], 1.0 / TWO_PI, None, ALU.mult, None, r=[b_r], w=[b_r])
                    cp("dve", ni[:], nq[:], r=[b_r], w=[b_r])
                    cp("dve", nq[:], ni[:], r=[b_r], w=[b_r])
                    stt("dve", a2[:], nq[:], -HI, a2[:], ALU.mult, ALU.add, r=[b_r], w=[b_r])
                    stt("dve", a2[:], nq[:], -LO, a2[:], ALU.mult, ALU.add, r=[b_r], w=[b_r])
                    ts("dve", nq[:], a2[:], float(np.pi), -TWO_PI, ALU.is_gt, ALU.mult, r=[b_r], w=[b_r])
                    tt("dve", a2[:], a2[:], nq[:], ALU.add, r=[b_r], w=[b_r])
                    ts("dve", nq[:], a2[:], float(-np.pi), TWO_PI, ALU.is_lt, ALU.mult, r=[b_r], w=[b_r])
                    tt("dve", a2[:], a2[:], nq[:], ALU.add, r=[b_r], w=[b_r])
                    act(dst, a2[:], AF.Sin, r=[b_r], w=[b_cs])
                reduce_sin(sinT[:], 0.0)
                reduce_sin(cosT[:], float(np.pi / 2))
                kb.barrier(streams=False)
            ar.update(markR)

            uT = [T(stA, "uT%d" % i, [128, 8, 512], BF16) for i in range(2)]; b_uT = [Buf("uT%d" % i) for i in range(2)]
            xt = [T(stA, "xt%d" % i, [128, D], F32) for i in range(2)]; b_xt = [Buf("xt%d" % i) for i in range(2)]
            ub = [T(stA, "ub%d" % i, [128, D], BF16) for i in range(2)]; b_ub = [Buf("ub%d" % i) for i in range(2)]
            junk = T(stA, "junk", [128, D], BF16); b_junk = Buf("junk")
            ssx = T(stA, "ssx", [128, 2], F32); b_ssx = [Buf("ssx0"), Buf("ssx1")]
            sqb = [T(stA, "sqb%d" % i, [128, 512], BF16) for i in range(4)]; b_sqb = [Buf("sqb%d" % i) for i in range(4)]
            v_tm = [T(stA, "v_tm%d" % i, [128, 4, 512], BF16) for i in range(2)]
            b_vtm = [[Buf("vtm%d_%d" % (j, i)) for i in range(4)] for j in range(2)]
            rtmp = T(stA, "rtmp", [128, 2, 4, 16], F32); b_rtmp = Buf("rtmp")
            tgn = [T(stA, "tgn%d" % i, [128, 512], F32) for i in range(6)]; b_tgn = [Buf("tgn%d" % i) for i in range(6)]
            kt_bf = T(stA, "kt_bf", [128, 512], BF16); kdT_bf = T(stA, "kdT_bf", [128, 512], BF16)
            b_kt, b_kdT = Buf("kt"), Buf("kdT")
            teb = [T(stA, "teb%d" % i, [128, 512], F32) for i in range(2)]; b_teb = [Buf("teb%d" % i) for i in range(2)]
            tg = [T(stA, "tg%d" % i, [128, 512], F32) for i in range(2)]; b_tg = [Buf("tg%d" % i) for i in range(2)]
            qt_bf = [T(stA, "qt_bf%d" % i, [128, 512], BF16) for i in range(2)]; b_qt = [Buf("qt%d" % i) for i in range(2)]
            kd_tm = [T(stA, "kd_tm%d" % i, [128, 4, 128], BF16) for i in range(2)]; b_kdtm = [Buf("kdtm%d" % i) for i in range(2)]
            aT_sb = [T(stA, "aT_sb%d" % i, [128, 4, 128], BF16) for i in range(2)]; b_aT = [Buf("aT%d" % i) for i in range(2)]
            t_x = T(stA, "t_x", [128, 512], F32); B_x = Buf("t_x")
            osq = T(stA, "osq", [128, 512], BF16); b_osq = Buf("osq")
            S32 = T(stA, "S32", [128, 4, 128], F32); Sbf = T(stA, "Sbf", [128, 4, 128], BF16)
            b_S32 = [Buf("S32_%d" % h) for h in range(4)]; b_Sbf = [Buf("Sbf_%d" % h) for h in range(4)]
            kb.op("pool", lambda e: e.memset(S32[:], 0.0), w=b_S32)
            kb.op("pool", lambda e: e.memset(Sbf[:], 0.0), w=b_Sbf)

            def A123(b):
                ub_ = uT[b % 2]; bu = b_uT[b % 2]
                blk = slice(b * 512, (b + 1) * 512)
                for i in range(4):
                    g = 4 * b + i
                    sl = g % 2
                    dma("sp", st_x[sl], xt[sl][:], x_d[g * 128:(g + 1) * 128, :], r=[], w=[b_xt[sl]])
                    act(junk[:], xt[sl][:], AF.Square, r=[b_xt[sl]], w=[b_junk, b_ssx[sl]], accum=ssx[:, sl:sl + 1])
                    rsqrt_act(ssx[:, sl:sl + 1], ssx[:, sl:sl + 1], 1.0 / D, r=[b_ssx[sl], b_eps], w=[b_ssx[sl]])
                    stt("dve", ub[sl][:], xt[sl][:], ssx[:, sl:sl + 1], gpre_b[:], ALU.mult, ALU.mult,
                        r=[b_xt[sl], b_ssx[sl], b_gpre], w=[b_ub[sl]])
                    bk = bank()
                    for dc in range(8):
                        tr(pb(bk)[:, dc * 128:(dc + 1) * 128], ub[sl][:, dc * 128:(dc + 1) * 128], identb[:],
                           r=[b_ub[sl], b_identb], w=[PB[bk]])
                    cp("act", ub_[:, :, i * 128:(i + 1) * 128], pb(bk).rearrange("p (c t) -> p c t", t=128),
                       r=[], w=[PB[bk], bu])
                    yield
                def inproj_fm(col):
                    bk = bank()
                    for dc in range(8):
                        mm(pf(bk), W_in[:, dc, col:col + 128], ub_[:, dc, :], dc == 0, dc == 7, r=[b_Win_for(col), bu], w=[PB[bk]])
                    return bk
                for rc in range(4):
                    bk = inproj_fm(C_CQ + rc * 128)
                    act(sqb[rc][:], pf(bk), AF.Square, r=[], w=[PB[bk], b_sqb[rc]])
                    if rc < 3:
                        ts("dve", c_qT[:, rc, blk], pf(bk), g_q[:, rc:rc + 1], None, ALU.mult, None, r=[b_gq], w=[PB[bk], b_cq[b]])
                    else:
                        ts("dve", c_kvT[:, blk], pf(bk), g_kv[:, 0:1], None, ALU.mult, None, r=[b_gkv], w=[PB[bk], b_ckv[b]])
                    yield
                bk = bank()
                for i in range(4):
                    for rc in range(3):
                        mm(pf(bk)[:, i:i + 1], sqb[rc][:, i * 128:(i + 1) * 128], ones_bf[:, 0:1], rc == 0, rc == 2,
                           r=[b_sqb[rc], b_ones], w=[PB[bk]])
                for i in range(4):
                    mm(pf(bk)[:, 4 + i:5 + i], sqb[3][:, i * 128:(i + 1) * 128], ones_bf[:, 0:1], True, True,
                       r=[b_sqb[3], b_ones], w=[PB[bk]])
                act(rstdq[:, 4 * b:4 * b + 4], pf(bk)[:, 0:4], AF.Ln, r=[b_eps], w=[PB[bk], b_rstdq], scale=1.0 / 384, bias=epsb[:, 0:1])
                act(rstdq[:, 4 * b:4 * b + 4], rstdq[:, 4 * b:4 * b + 4], AF.Exp, r=[b_rstdq], w=[b_rstdq], scale=-0.5)
                act(rstdkv[:, 4 * b:4 * b + 4], pf(bk)[:, 4:8], AF.Ln, r=[b_eps], w=[PB[bk], b_rstdkv], scale=1.0 / 128, bias=epsb[:, 0:1])
                act(rstdkv[:, 4 * b:4 * b + 4], rstdkv[:, 4 * b:4 * b + 4], AF.Exp, r=[b_rstdkv], w=[b_rstdkv], scale=-0.5)
                bkr = bank()
                for i in range(4):
                    for dc in range(8):
                        mm(pf(bkr)[:, i * 32:(i + 1) * 32], ub_[:, dc, i * 128:(i + 1) * 128], W_in[:, dc, C_KR:C_KR + 32],
                           dc == 0, dc == 7, r=[b_Wg_in["cq"], bu], w=[PB[bkr]])
                kr3 = pf(bkr)[:, 0:128].rearrange("p (i c) -> p i c", c=32)
                x1 = kr3[:, :, 0:16]; x2 = kr3[:, :, 16:32]
                cs = cosT[:, 4 * b:4 * b + 4, :]; sn = sinT[:, 4 * b:4 * b + 4, :]
                tt("dve", rtmp[:, 0], x1, cs, ALU.mult, r=[b_cs], w=[PB[bkr], b_rtmp])
                tt("dve", rtmp[:, 1], x2, sn, ALU.mult, r=[b_cs], w=[PB[bkr], b_rtmp])
                tt("dve", krope[:, 4 * b:4 * b + 4, 0:16], rtmp[:, 0], rtmp[:, 1], ALU.subtract, r=[b_rtmp], w=[b_krope[b]])
                tt("dve", rtmp[:, 0], x1, sn, ALU.mult, r=[b_cs], w=[PB[bkr], b_rtmp])
                tt("dve", rtmp[:, 1], x2, cs, ALU.mult, r=[b_cs], w=[PB[bkr], b_rtmp])
                tt("dve", krope[:, 4 * b:4 * b + 4, 16:32], rtmp[:, 0], rtmp[:, 1], ALU.add, r=[b_rtmp], w=[b_krope[b]])

                yield
                for i in range(4):
                    bk = bank()
                    for dc in range(8):
                        mm(pf(bk), ub_[:, dc, i * 128:(i + 1) * 128], W_in[:, dc, C_HI:C_HI + 512], dc == 0, dc == 7,
                           r=[b_Wg_in["hi"], bu], w=[PB[bk]])
                    cp("act", v_tm[b % 2][:, i, :], pf(bk), r=[], w=[PB[bk], b_vtm[b % 2][i]])
                    yield
            def inproj_fm2(b, col):
                ub_ = uT[b % 2]; bu = b_uT[b % 2]
                bk = bank()
                for dc in range(8):
                    mm(pf(bk), W_in[:, dc, col:col + 128], ub_[:, dc, :], dc == 0, dc == 7, r=[b_Win_for(col), bu], w=[PB[bk]])
                return bk

            def G(b, h, st):
                bq = inproj_fm2(b, C_HQ + h * 128)
                bf = inproj_fm2(b, C_HF + h * 128)
                bg = inproj_fm2(b, C_HG + h * 128)
                t_f, t_lf, t_kk, t_b, t_enb, t_q = tgn
                B_f, B_lf, B_kk, B_b, B_enb, B_q = b_tgn
                t_eb = teb[st]; B_eb = b_teb[st]; t_g = tg[st]; B_g = b_tg[st]
                act(t_f[:], pf(bf), AF.Tanh, r=[], w=[PB[bf], B_f], scale=0.5)
                act(t_q[:], pf(bq), AF.Silu, r=[], w=[PB[bq], B_q])
                act(t_g[:], pf(bg), AF.Silu, r=[], w=[PB[bg], B_g])
                yield
                ts("dve", t_f[:], t_f[:], fA[:, h:h + 1], fB[:, h:h + 1], ALU.mult, ALU.add, r=[B_f, b_lb], w=[B_f])
                act(t_lf[:], t_f[:], AF.Ln, r=[B_f], w=[B_lf])
                yield
                ts("dve", t_kk[:], t_f[:], -1.0, 1.0, ALU.mult, ALU.add, r=[B_f], w=[B_kk])
                kb.op("dve", lambda e: e.tensor_tensor_scan(out=t_b[:], data0=resetm[:], data1=t_lf[:], initial=0.0,
                                                            op0=ALU.mult, op1=ALU.add), r=[b_resetm, B_lf], w=[B_b])
                yield
                act(t_eb[:], t_b[:], AF.Exp, r=[B_b], w=[B_eb])
                act(t_enb[:], t_b[:], AF.Exp, r=[B_b], w=[B_enb], scale=-1.0)
                yield
                tt("dve", t_kk[:], t_kk[:], t_enb[:], ALU.mult, r=[B_kk, B_enb], w=[B_kk])
                cp("act", kt_bf[:], t_kk[:], r=[B_kk], w=[b_kt])
                eb3 = t_eb[:].rearrange("p (c t) -> p c t", t=64)
                tt("dve", kdT_bf[:].rearrange("p (c t) -> p c t", t=64), t_kk[:].rearrange("p (c t) -> p c t", t=64),
                   eb3[:, :, 63:64].broadcast_to([128, 8, 64]), ALU.mult, r=[B_kk, B_eb], w=[b_kdT])
                tt("dve", qt_bf[st][:], t_q[:], t_eb[:], ALU.mult, r=[B_q, B_eb], w=[b_qt[st]])
                yield
                bk = bank()
                for i in range(4):
                    tr(pb(bk)[:, i * 128:(i + 1) * 128], kdT_bf[:, i * 128:(i + 1) * 128], identb[:], r=[b_kdT, b_identb], w=[PB[bk]])
                cp("act", kd_tm[st][:], pb(bk)[:, 0:512].rearrange("p (i k) -> p i k", k=128), r=[], w=[PB[bk], b_kdtm[st]])
                yield
                bk = bank()
                for i in range(4):
                    mm(pf(bk)[:, i * 128:(i + 1) * 128], kt_bf[:, i * 128:(i + 1) * 128], qt_bf[st][:, i * 128:(i + 1) * 128], True, True,
                       r=[b_kt, b_qt[st]], w=[PB[bk]])
                tt("dve", aT_sb[st][:], pf(bk).rearrange("p (i t) -> p i t", t=128), mask2[:].unsqueeze(1).broadcast_to([128, 4, 128]),
                   ALU.mult, r=[b_mask2], w=[PB[bk], b_aT[st]])
                yield

            def R(b, h, st):
                blk = slice(b * 512, (b + 1) * 512)
                t_eb = teb[st]; B_eb = b_teb[st]; t_g = tg[st]; B_g = b_tg[st]
                vt = v_tm[b % 2]; bv = b_vtm[b % 2]
                bo = lbank()
                for i in range(4):
                    mm(pf(bo)[:, i * 128:(i + 1) * 128], vt[:, i, h * 128:(h + 1) * 128], aT_sb[st][:, i, :], True, False,
                       r=[bv[i], b_aT[st]], w=[PB[bo]])
                    for cc in range(2):
                        c = 2 * i + cc
                        mm(pf(bo)[:, c * 64:(c + 1) * 64], Sbf[:, h, :], qt_bf[st][:, c * 64:(c + 1) * 64], False, cc == 1,
                           r=[b_Sbf[h], b_qt[st]], w=[PB[bo]])
                        bs = bank()
                        rows = slice(cc * 64, cc * 64 + 64)
                        mm(pf(bs)[:, 0:128], kd_tm[st][rows, i, :], vt[rows, i, h * 128:(h + 1) * 128], True, True,
                           r=[b_kdtm[st], bv[i]], w=[PB[bs]])
                        dec = t_eb[:, c * 64 + 63:c * 64 + 64]
                        stt("dve", Sbf[:, h, :], S32[:, h, :], dec, pf(bs)[:, 0:128], ALU.mult, ALU.add,
                            r=[B_eb, b_S32[h]], w=[PB[bs], b_Sbf[h]])
                        stt("dve", S32[:, h, :], S32[:, h, :], dec, pf(bs)[:, 0:128], ALU.mult, ALU.add,
                            r=[B_eb], w=[PB[bs], b_S32[h]])
                        yield
                act(osq[:], pf(bo), AF.Square, r=[], w=[PB[bo], b_osq])
                bn = bank()
                mm(pf(bn), ones_bf[:], osq[:], True, True, r=[b_ones, b_osq], w=[PB[bn]])
                act(t_x[:], pf(bn), AF.Ln, r=[b_eps], w=[PB[bn], B_x], scale=1.0 / 128, bias=epsb[:, 0:1])
                act(t_x[:], t_x[:], AF.Exp, r=[B_x], w=[B_x], scale=-0.5)
                tt("dve", t_x[:], pf(bo), t_x[:], ALU.mult, r=[B_x], w=[PB[bo], B_x])
                stt("dve", mixH[:, h, blk], t_x[:], g_hn[:, h:h + 1], t_g[:], ALU.mult, ALU.mult, r=[B_x, B_g, b_ghn], w=[b_mixH[h][b]])

            seq = [(b, h) for b in range(NB) for h in range(4)]

            def run_merged(gens):
                gens = list(gens)
                while gens:
                    for g_ in list(gens):
                        try:
                            next(g_)
                        except StopIteration:
                            gens.remove(g_)

            for _ in A123(0):
                pass
            carry = None
            for k in range(len(seq) + 1):
                gens = []
                if k < len(seq):
                    b, h = seq[k]
                    gens.append(G(b, h, k % 2))
                    if h == 1 and b + 1 < NB:
                        carry = A123(b + 1)
                if k > 0:
                    gens.append(R(seq[k - 1][0], seq[k - 1][1], (k - 1) % 2))
                if carry is not None:
                    hh = seq[k][1] if k < len(seq) else 0
                    if hh == 3:
                        gens.append(carry)
                        carry = None
                    else:
                        def part(gc, n):
                            for _ in range(n):
                                try:
                                    next(gc)
                                except StopIteration:
                                    return
                                yield
                        gens.append(part(carry, 5))
                run_merged(gens)
            print("arena phase A: lo", ar["lo"], "hi", ar["hi"])
            tt("dve", cosq[:], cosT[:], rstdq[:].unsqueeze(2).broadcast_to([128, NT, 16]), ALU.mult, r=[b_cs, b_rstdq], w=[b_csq])
            tt("dve", sinq[:], sinT[:], rstdq[:].unsqueeze(2).broadcast_to([128, NT, 16]), ALU.mult, r=[b_cs, b_rstdq], w=[b_csq])
            if debug:
                dump(d_mixH, mixH[:], b_mixH[0][0]); dump(d_cq, c_qT[:], b_cq[0]); dump(d_rq, rstdq[:], b_rstdq); dump(d_kr, None, None) if False else None
            kb.barrier()

        ar["lo"] = markA["lo"]
        if True:
            stB = None
            Wout = T(stB, "Wout", [128, 8, D], BF16); b_Wout = Buf("Wout")
            st_wout = kb.stream()
            for ecx in range(8):
                dma("pool", st_wout, Wout[:, ecx, :], w_out_d[ecx * 128:(ecx + 1) * 128, :], r=[], w=[b_Wout])
            gpost_b = T(stB, "gpost_b", [128, D], F32); b_gpost = Buf("gpost")
            dma("sp", kb.stream(), gpost_b[:], g_post_d.partition_broadcast(128), r=[], w=[b_gpost])
            markC = ar["lo"]
            QT = [T(stB, "QT%d" % i, [128, S], BF16) for i in range(2)]; b_QT = [[Buf("QT%d_%d" % (i, b)) for b in range(NB)] for i in range(2)]
            KT = [T(stB, "KT%d" % i, [128, S], BF16) for i in range(2)]; b_KT = [[Buf("KT%d_%d" % (i, b)) for b in range(NB)] for i in range(2)]
            Vh = [T(stB, "Vh%d" % i, [128, NT, 128], BF16) for i in range(2)]; b_Vh = [[Buf("Vh%d_%d" % (i, b)) for b in range(NB)] for i in range(2)]
            q_tm2 = [T(stB, "q_tm%d" % i, [128, 4, 96], BF16) for i in range(2)]; k_tm2 = [T(stB, "k_tm%d" % i, [128, 4, 96], BF16) for i in range(2)]
            b_qtm2 = [Buf("qtm0"), Buf("qtm1")]; b_ktm2 = [Buf("ktm0"), Buf("ktm1")]
            rt2 = T(stB, "rt2", [128, 2, 4, 16], F32); b_rt2 = Buf("rt2")
            PT = [T(stB, "PT%d" % i, [128, 1024], BF16) for i in range(3)]; b_PT = [Buf("PT%d" % i) for i in range(3)]
            sqw = T(stB, "sqw", [128, 512], BF16); b_sqw = Buf("sqw")
            rs_t = T(stB, "rs_t", [128, 512], F32); b_rs = Buf("rs_t")
            kb.op("pool", lambda e: e.memset(Vh[0][:], 0.0), w=b_Vh[0])
            kb.op("pool", lambda e: e.memset(Vh[1][:], 0.0), w=b_Vh[1])
            kb.op("pool", lambda e: e.memset(Vh[0][:, :, 64:65], 1.0), w=b_Vh[0])
            kb.op("pool", lambda e: e.memset(Vh[1][:, :, 0:1], 1.0), w=b_Vh[1])
            if True:
                pt_state = {"ctr": 0}
                pair_state = {"ctr": 0}
                NPT = 3

                def prep_head(h):
                    hp = h % 2
                    voff = 0 if hp == 0 else 64

                    def seg1(b):
                        q_tm = q_tm2[b % 2]; k_tm = k_tm2[b % 2]; b_qtm = b_qtm2[b % 2]; b_ktm = b_ktm2[b % 2]
                        t4 = slice(4 * b, 4 * b + 4)
                        bq = idle_lbank(); bkv = bank()
                        for i in range(4):
                            tok = slice(b * 512 + i * 128, b * 512 + (i + 1) * 128)
                            for rc in range(3):
                                mm(pf(bq)[:, i * 96:(i + 1) * 96], c_qT[:, rc, tok], Wuq[:, rc, h * 96:(h + 1) * 96], rc == 0, rc == 2,
                                   r=[b_cq[b], b_Wu], w=[PB[bq]])
                            mm(pf(bkv)[:, i * 128:(i + 1) * 128], c_kvT[:, tok], Wukv[:, h * 128:(h + 1) * 128], True, True,
                               r=[b_ckv[b], b_Wu], w=[PB[bkv]])
                        q3 = pf(bq)[:, 0:384].rearrange("p (i c) -> p i c", c=96)
                        kv3 = pf(bkv).rearrange("p (i c) -> p i c", c=128)
                        rq_b = rstdq[:, t4].unsqueeze(2).broadcast_to([128, 4, 64])
                        rkv_b = rstdkv[:, t4].unsqueeze(2).broadcast_to([128, 4, 64])
                        tt("dve", q_tm[:, :, 0:64], q3[:, :, 0:64], rq_b, ALU.mult, r=[b_rstdq], w=[PB[bq], b_qtm])
                        x1 = q3[:, :, 64:80]; x2 = q3[:, :, 80:96]
                        cs = cosq[:, t4, :]; sn = sinq[:, t4, :]
                        tt("dve", rt2[:, 0], x1, cs, ALU.mult, r=[b_csq], w=[PB[bq], b_rt2])
                        tt("dve", rt2[:, 1], x2, sn, ALU.mult, r=[b_csq], w=[PB[bq], b_rt2])
                        tt("dve", q_tm[:, :, 64:80], rt2[:, 0], rt2[:, 1], ALU.subtract, r=[b_rt2], w=[b_qtm])
                        tt("dve", rt2[:, 0], x1, sn, ALU.mult, r=[b_csq], w=[PB[bq], b_rt2])
                        tt("dve", rt2[:, 1], x2, cs, ALU.mult, r=[b_csq], w=[PB[bq], b_rt2])
                        tt("dve", q_tm[:, :, 80:96], rt2[:, 0], rt2[:, 1], ALU.add, r=[b_rt2], w=[b_qtm])
                        tt("dve", k_tm[:, :, 0:64], kv3[:, :, 0:64], rkv_b, ALU.mult, r=[b_rstdkv], w=[PB[bkv], b_ktm])
                        cp("pool", k_tm[:, :, 64:96], krope[:, t4, :], r=[b_krope[b]], w=[b_ktm])
                        tt("dve", Vh[hp][:, t4, voff:voff + 64], kv3[:, :, 64:128], rkv_b, ALU.mult, r=[b_rstdkv], w=[PB[bkv], b_Vh[hp][b]])

                    def seg2(b):
                        q_tm = q_tm2[b % 2]; k_tm = k_tm2[b % 2]; b_qtm = b_qtm2[b % 2]; b_ktm = b_ktm2[b % 2]
                        blk = slice(b * 512, (b + 1) * 512)
                        bk = idle_lbank()
                        for i in range(4):
                            tr(pb(bk)[0:96, i * 128:(i + 1) * 128], q_tm[:, i, :], identb[:], r=[b_qtm, b_identb], w=[PB[bk]])
                        cp("act", QT[hp][0:96, blk], pb(bk)[0:96, 0:512], r=[], w=[PB[bk], b_QT[hp][b]])
                        bk = bank()
                        for i in range(4):
                            tr(pb(bk)[0:96, i * 128:(i + 1) * 128], k_tm[:, i, :], identb[:], r=[b_ktm, b_identb], w=[PB[bk]])
                        cp("act", KT[hp][0:96, blk], pb(bk)[0:96, 0:512], r=[], w=[PB[bk], b_KT[hp][b]])

                    seg1(0)
                    yield ("s1", 0)
                    for b in range(NB):
                        if b + 1 < NB:
                            seg1(b + 1)
                            yield ("s1", b + 1)
                        seg2(b)
                        yield ("s2", b)

                def attn_head(h):
                    hp = h % 2
                    voff = 0 if hp == 0 else 64
                    ec = h // 2
                    M = 65 if hp == 0 else 128
                    Mo = 64 if hp == 0 else 128
                    groups = []
                    for qb in range(NB):
                        for j in range(2 * qb):
                            groups.append((qb, [2 * j, 2 * j + 1]))
                        for i in range(4):
                            groups.append((qb, [4 * qb + i]))
                    obank = {}
                    info = {}

                    def emit_qk(gi):
                        qb, kbs = groups[gi]
                        pr = pair_state["ctr"] % 3
                        pair_state["ctr"] += 1
                        b0 = 2 * pr
                        p = pr
                        if len(kbs) == 2:
                            for j, kbi in enumerate(kbs):
                                mm(pf(b0 + j), KT[hp][0:96, kbi * 128:(kbi + 1) * 128], QT[hp][0:96, qb * 512:(qb + 1) * 512], True, True,
                                   r=[b_KT[hp][kbi // 4], b_QT[hp][qb]], w=[PB[b0 + j]])
                            act(PT[p][:, 0:1024], psum[:, b0 * 512:(b0 + 2) * 512], AF.Exp, r=[], w=[PB[b0], PB[b0 + 1], b_PT[p]], scale=SCALE)
                            info[gi] = (p, [(kbs[0], 0, 0), (kbs[1], 512, 0)])
                        else:
                            kbi = kbs[0]
                            i = kbi - 4 * qb
                            qlo = 128 * i
                            qs = slice(qb * 512 + qlo, (qb + 1) * 512)
                            mm(pf(b0)[:, qlo:512], KT[hp][0:96, kbi * 128:(kbi + 1) * 128], QT[hp][0:96, qs], True, False,
                               r=[b_KT[hp][kbi // 4], b_QT[hp][qb]], w=[PB[b0]])
                            mm(pf(b0)[:, qlo:qlo + 128], identb[:], negtrib[:], False, True, r=[b_identb, b_negtri], w=[PB[b0]])
                            act(PT[p][:, qlo:512], pf(b0)[:, qlo:512], AF.Exp, r=[], w=[PB[b0], b_PT[p]], scale=SCALE)
                            info[gi] = (p, [(kbi, 0, qlo)])

                    def emit_pv(gi):
                        qb, kbs = groups[gi]
                        p, lst = info.pop(gi)
                        nkb = 4 * qb + 4
                        for (kbi, off, qlo) in lst:
                            if kbi == 0:
                                obank[qb] = lbank()
                            bo = obank[qb]
                            mm(pf(bo)[0:M, qlo:512], Vh[hp][:, kbi, 0:M], PT[p][:, off + qlo:off + 512], kbi == 0, kbi == nkb - 1,
                               r=[b_Vh[hp][kbi // 4], b_PT[p]], w=[PB[bo]])
                            if kbi == nkb - 1:
                                blk = slice(qb * 512, (qb + 1) * 512)
                                act(sqw[0:M, :], pf(bo)[0:M, :], AF.Square, r=[b_wcol], w=[PB[bo], b_sqw], scale=wcol[0:M, hp:hp + 1])
                                bn = bank()
                                mm(pf(bn)[0:Mo, :], ones_bf[0:M, 0:Mo], sqw[0:M, :], True, True, r=[b_ones, b_sqw], w=[PB[bn]])
                                act(rs_t[0:Mo, :], pf(bn)[0:Mo, :], AF.Ln, r=[], w=[PB[bn], b_rs], scale=1.0 / 64)
                                act(rs_t[0:Mo, :], rs_t[0:Mo, :], AF.Exp, r=[b_rs], w=[b_rs], scale=-0.5)
                                rows = slice(voff, voff + 64)
                                stt("dve", mixM[rows, ec, blk], pf(bo)[rows, :], g_mla[rows, ec:ec + 1], rs_t[rows, :], ALU.mult, ALU.mult,
                                    r=[b_rs, b_gmla], w=[PB[bo], b_mixM])

                    LOOK = 2
                    for gi in range(len(groups)):
                        yield groups[gi][0]
                        emit_qk(gi)
                        if gi >= LOOK:
                            emit_pv(gi - LOOK)
                    yield None
                    for gi in range(max(0, len(groups) - LOOK), len(groups)):
                        emit_pv(gi)

                gp0 = prep_head(0)
                p0_done = [-1]

                def advance_p0(upto):
                    while p0_done[0] < upto:
                        try:
                            tag = next(gp0)
                        except StopIteration:
                            p0_done[0] = NB
                            return
                        if tag[0] == "s2":
                            p0_done[0] = tag[1]

                for h in range(8):
                    ga = attn_head(h)
                    gp = prep_head(h + 1) if h + 1 < 8 else None
                    n_groups = sum(2 * qb + 4 for qb in range(NB))
                    n_seg = 2 * NB
                    every = max(1, n_groups // (n_seg + 1))
                    cnt = 0
                    for need in ga:
                        if h == 0 and need is not None:
                            advance_p0(need)
                        cnt += 1
                        ok = (h != 0) or (p0_done[0] >= NB - 1)
                        ev = every if h != 0 else 1
                        if gp is not None and ok and cnt % ev == 0:
                            try:
                                next(gp)
                            except StopIteration:
                                gp = None
                    if h == 0:
                        advance_p0(NB)
                    if gp is not None:
                        for _ in gp:
                            pass
            if debug:
                dump(d_mixM, mixM[:], b_mixM)
            kb.barrier()

        ar["lo"] = markC; ar["hi"] = ARENA_BYTES
        if True:
            stC = None
            Wg = T("top", "Wg", [128, 8, DFF], BF16); Wu = T("top", "Wu", [128, 8, DFF], BF16)
            NWG = 4
            WGC = DFF // NWG
            b_Wg = [Buf("Wg%d" % i) for i in range(NWG)]; b_Wuu = [Buf("Wu%d" % i) for i in range(NWG)]
            sg_ = kb.stream(); su_ = kb.stream()
            bwg1 = Buf("Wg_all"); bwu1 = Buf("Wu_all")
            b_Wg = [bwg1] * NWG; b_Wuu = [bwu1] * NWG
            for dc in range(8):
                dma("pool", sg_, Wg[:, dc, :], w_gate_d[dc * 128:(dc + 1) * 128, :], r=[], w=[bwg1])
                dma("pool", su_, Wu[:, dc, :], w_up_d[dc * 128:(dc + 1) * 128, :], r=[], w=[bwu1])
            NXC = 3
            xc = [T(stC, "xc%d" % i, [128, D], F32) for i in range(NXC)]; b_xc = [Buf("xc%d" % i) for i in range(NXC)]
            st_xc = [kb.stream() for _ in range(NXC)]
            NYC = 3
            yc = [T(stC, "yc%d" % i, [128, D], F32) for i in range(NYC)]; b_yc = [Buf("yc%d" % i) for i in range(NYC)]
            st_oc = [kb.stream() for _ in range(NYC)]
            junkc = T(stC, "junkc", [128, 512], BF16); b_junkc = Buf("junkc")
            ssc = T(stC, "ssc", [128, 2, 3], F32); b_ssc = [Buf("ssc0"), Buf("ssc1")]
            print("arena phase C: lo", ar["lo"], "hi", ar["hi"])

            def ldx(g):
                xs = g % NXC
                dma("sp", st_xc[xs], xc[xs][:], x_d[g * 128:(g + 1) * 128, :], r=[], w=[b_xc[xs]])

            for g in range(min(NXC, NT)):
                ldx(g)
            for g in range(NT):
                sl = g % 2
                xs = g % NXC
                ys = g % NYC
                tok = slice(g * 128, (g + 1) * 128)
                by = [bank(), bank()]
                for hh in range(2):
                    for ecx in range(8):
                        lhs = mixM[:, ecx, tok] if ecx < 4 else mixH[:, ecx - 4, tok]
                        mm(pf(by[hh]), lhs, Wout[:, ecx, hh * 512:(hh + 1) * 512], ecx == 0, ecx == 7,
                           r=[b_mixM, b_Wout] + [b_mixH[hx][g // 4] for hx in range(4)], w=[PB[by[hh]]])
                    act(junkc[:], pf(by[hh]), AF.Square, r=[], w=[PB[by[hh]], b_junkc, b_ssc[sl]], accum=ssc[:, sl, hh:hh + 1])
                tt("dve", ssc[:, sl, 2:3], ssc[:, sl, 0:1], ssc[:, sl, 1:2], ALU.add, r=[b_ssc[sl]], w=[b_ssc[sl]])
                rsqrt_act(ssc[:, sl, 2:3], ssc[:, sl, 2:3], 1.0 / D, r=[b_ssc[sl], b_eps], w=[b_ssc[sl]])
                for hh in range(2):
                    cs_ = slice(hh * 512, (hh + 1) * 512)
                    tt("dve", yc[ys][:, cs_], pf(by[hh]), gpost_b[:, cs_], ALU.mult, r=[b_gpost], w=[PB[by[hh]], b_yc[ys]])
                stt("dve", yc[ys][:], yc[ys][:], ssc[:, sl, 2:3], xc[xs][:], ALU.mult, ALU.add,
                    r=[b_ssc[sl], b_xc[xs], b_yc[ys]], w=[b_yc[ys]])
                dma("sp", st_oc[ys], out_d[tok, :], yc[ys][:], r=[b_yc[ys]], w=[ob[g]])
                if g + NXC < NT:
                    ldx(g + NXC)
            kb.barrier()

        ar["lo"] = markMix
        if True:
            stD = None
            Wd = T(stD, "Wd", [128, NFC, D], BF16)
            b_Wd = [Buf("Wd0"), Buf("Wd1")]
            st_wd = [kb.stream(), kb.stream()]
            for fc in range(NFC):
                j = 0 if fc < NFC // 2 else 1
                dma("pool", st_wd[j], Wd[:, fc, :], w_down_d[fc * 128:(fc + 1) * 128, :], r=[], w=[b_Wd[j]])
            gfpre_b = T(stD, "gfpre_b", [128, D], F32); gfpost_b = T(stD, "gfpost_b", [128, D], F32); b_gfpre = Buf("gfpre"); b_gfpost = Buf("gfpost")
            dma("sp", kb.stream(), gfpre_b[:], g_fpre_d.partition_broadcast(128), r=[], w=[b_gfpre])
            dma("sp", kb.stream(), gfpost_b[:], g_fpost_d.partition_broadcast(128), r=[], w=[b_gfpost])
            h1 = [T(stD, "h1_%d" % i, [128, D], F32) for i in range(2)]; b_h1 = [Buf("h1_%d" % i) for i in range(2)]
            zb = T(stD, "zb", [128, D], BF16); b_zb = Buf("zb")
            zT = T(stD, "zT", [128, 8, 512], BF16); b_zT = Buf("zT")
            ffT = T(stD, "ffT", [128, NFC, 512], BF16); b_ffTs = [Buf("ffT%d" % i) for i in range(NFC)]
            sg = [T(stD, "sg%d" % i, [128, 512], F32) for i in range(2)]; b_sg = [Buf("sg%d" % i) for i in range(2)]
            junkd = T(stD, "junkd", [128, D], BF16); b_junkd = Buf("junkd")
            yd = [T(stD, "yd%d" % i, [128, D], F32) for i in range(2)]; b_yd = [Buf("yd%d" % i) for i in range(2)]
            ssd = T(stD, "ssd", [128, 2, 3], F32); b_ssd = [Buf("ssd0"), Buf("ssd1")]
            ssp = T(stD, "ssp", [128, 2], F32); b_ssp = Buf("ssp")
            print("arena phase D: lo", ar["lo"], "hi", ar["hi"])

            def s1(bb, i):
                g = 4 * bb + i
                tok = slice(g * 128, (g + 1) * 128)
                dma("sp", st_x[0], h1[0][:], out_d[tok, :], r=[ob[g]], w=[b_h1[0]])
                act(junkd[:], h1[0][:], AF.Square, r=[b_h1[0]], w=[b_junkd, b_ssp], accum=ssp[:, 0:1])
                rsqrt_act(ssp[:, 0:1], ssp[:, 0:1], 1.0 / D, r=[b_ssp, b_eps], w=[b_ssp])
                stt("dve", zb[:], h1[0][:], ssp[:, 0:1], gfpre_b[:], ALU.mult, ALU.mult, r=[b_h1[0], b_ssp, b_gfpre], w=[b_zb])

            def s2(bb, i):
                bk = bank()
                for dc in range(8):
                    tr(pb(bk)[:, dc * 128:(dc + 1) * 128], zb[:, dc * 128:(dc + 1) * 128], identb[:], r=[b_zb, b_identb], w=[PB[bk]])
                cp("act", zT[:, :, i * 128:(i + 1) * 128], pb(bk).rearrange("p (c t) -> p c t", t=128), r=[], w=[PB[bk], b_zT])

            def gateup(bb):
                for fc in range(NFC):
                    bg_ = bank(); bu_ = bank()
                    for dc in range(8):
                        mm(pf(bg_), Wg[:, dc, fc * 128:(fc + 1) * 128], zT[:, dc, :], dc == 0, dc == 7, r=[b_Wg[fc * 128 // WGC], b_zT], w=[PB[bg_]])
                    for dc in range(8):
                        mm(pf(bu_), Wu[:, dc, fc * 128:(fc + 1) * 128], zT[:, dc, :], dc == 0, dc == 7, r=[b_Wuu[fc * 128 // WGC], b_zT], w=[PB[bu_]])
                    s = fc % 2
                    act(sg[s][:], pf(bg_), AF.Silu, r=[], w=[PB[bg_], b_sg[s]])
                    tt("dve", ffT[:, fc, :], pf(bu_), sg[s][:], ALU.mult, r=[b_sg[s]], w=[PB[bu_], b_ffTs[fc]])

            def down(bb, i):
                g = 4 * bb + i
                sl = g % 2
                tok = slice(g * 128, (g + 1) * 128)
                dma("sp", st_x[1], h1[1][:], out_d[tok, :], r=[ob[g]], w=[b_h1[1]])
                bd = [bank(), bank()]
                for hh in range(2):
                    for fc in range(NFC):
                        mm(pf(bd[hh]), ffT[:, fc, i * 128:(i + 1) * 128], Wd[:, fc, hh * 512:(hh + 1) * 512], fc == 0, fc == NFC - 1,
                           r=[b_ffTs[fc], b_Wd[0 if fc < NFC // 2 else 1]], w=[PB[bd[hh]]])
                    act(junkd[:, 0:512], pf(bd[hh]), AF.Square, r=[], w=[PB[bd[hh]], b_junkd, b_ssd[sl]], accum=ssd[:, sl, hh:hh + 1])
                tt("dve", ssd[:, sl, 2:3], ssd[:, sl, 0:1], ssd[:, sl, 1:2], ALU.add, r=[b_ssd[sl]], w=[b_ssd[sl]])
                rsqrt_act(ssd[:, sl, 2:3], ssd[:, sl, 2:3], 1.0 / D, r=[b_ssd[sl], b_eps], w=[b_ssd[sl]])
                for hh in range(2):
                    cs_ = slice(hh * 512, (hh + 1) * 512)
                    tt("dve", yd[sl][:, cs_], pf(bd[hh]), gfpost_b[:, cs_], ALU.mult, r=[b_gfpost], w=[PB[bd[hh]], b_yd[sl]])
                stt("dve", yd[sl][:], yd[sl][:], ssd[:, sl, 2:3], h1[1][:], ALU.mult, ALU.add,
                    r=[b_ssd[sl], b_h1[1], b_yd[sl]], w=[b_yd[sl]])
                dma("pool", st_o[sl], out_d[tok, :], yd[sl][:], r=[b_yd[sl]], w=[ob[g]])

            for i in range(4):
                s1(0, i)
                s2(0, i)
            for bb in range(NB):
                gateup(bb)
                for i in range(4):
                    if bb + 1 < NB:
                        s1(bb + 1, i)
                    down(bb, i)
                    if bb + 1 < NB:
                        s2(bb + 1, i)
            kb.wait_all("sp", ob)
        kb.emit()
    return nc


def _consts():
    ident = np.eye(128, dtype=np.float32)
    tri = (np.arange(128)[:, None] <= np.arange(128)[None, :]).astype(np.float32)
    s = np.arange(128)[:, None]
    t = np.arange(128)[None, :]
    mask2 = ((s // 64 == t // 64) & (s <= t)).astype(np.float32)
    resetm = (np.arange(512) % 64 != 0).astype(np.float32)
    invf64 = 1.0 / (10000.0 ** (np.arange(0, 32, 2, dtype=np.float64) / 32.0))
    invf = invf64.astype(np.float32)
    invf_lo = (invf64 - invf.astype(np.float64)).astype(np.float32)
    wcol = np.zeros((128, 2), np.float32)
    c = np.float32(np.sqrt(64 * EPS))
    wcol[0:64, 0] = 1.0
    wcol[64, 0] = c
    wcol[0, 1] = c
    wcol[64:128, 1] = 1.0
    negtri = ((tri - 1.0) * 30000.0).astype(np.float32)
    return dict(ident=ident, tri=tri, negtri=negtri, mask2=mask2, resetm=resetm, invf=invf, invf_lo=invf_lo, wcol=wcol)


def _pcol(v, n):
    return np.ascontiguousarray(np.asarray(v, np.float32).reshape(n, 128).T)


def make_in_maps(inputs, S, batch_ids):
    f = lambda a: np.ascontiguousarray(np.asarray(a, np.float32))
    shared = dict(
        w_in=f(inputs["w_in"][0]), w_uq=f(inputs["mla_w_uq"][0]).reshape(384, 768),
        w_ukv=f(inputs["mla_w_ukv"][0]).reshape(128, 1024), w_out=f(inputs["w_out"][0]),
        w_gate=f(inputs["w_gate"][0]), w_up=f(inputs["w_up"][0]), w_down=f(inputs["w_down"][0]),
        g_pre=f(inputs["attn_pre_norm"][0]), g_post=f(inputs["attn_post_norm"][0]),
        g_fpre=f(inputs["ffn_pre_norm"][0]), g_fpost=f(inputs["ffn_post_norm"][0]),
        g_q=_pcol(inputs["mla_q_norm"][0], 3), g_kv=_pcol(inputs["mla_kv_norm"][0], 1),
        g_mla=_pcol(inputs["mla_out_norm"][0], 4), g_hn=_pcol(inputs["hgrn_out_norm"][0], 4),
        lbl=np.ascontiguousarray(np.concatenate([_pcol(inputs["hgrn_lb_logits"][0], 4), _pcol(inputs["hgrn_lb_logits"][1], 4)], axis=1)),
    )
    shared.update(_consts())
    maps = []
    NT = S // 128
    for b in batch_ids:
        m = dict(shared)
        m["x"] = f(inputs["x"][b])
        m["pos"] = np.ascontiguousarray(np.asarray(inputs["positions"][b], np.int32).reshape(NT, 128).T)
        maps.append(m)
    return maps


_NC_CACHE = {}


def kernel(**inputs):
    x = np.asarray(inputs["x"])
    B, S, _ = x.shape
    if S not in _NC_CACHE:
        _NC_CACHE[S] = build(S)
    nc = _NC_CACHE[S]
    maps = make_in_maps(inputs, S, list(range(B)))
    res = run_bass_kernel_spmd(nc, maps, core_ids=list(range(B)))
    out = np.stack([np.asarray(r["out"], np.float32) for r in res.results], axis=0)
    return out.astype(np.float32)
```

```python
import contextlib
import numpy as np
import concourse.bass as bass
import concourse.mybir as mybir

F32 = mybir.dt.float32
BF16 = mybir.dt.bfloat16
I32 = mybir.dt.int32
AF = mybir.ActivationFunctionType
ALU = mybir.AluOpType
AX = mybir.AxisListType

ENGS = ("pe", "act", "dve", "pool", "sp")
STRICT = False


class Buf:
    __slots__ = ("name", "last_w", "readers")

    def __init__(self, name):
        self.name = name
        self.last_w = None
        self.readers = {}


class _Op:
    __slots__ = ("fn", "waits", "sig", "dma")

    def __init__(self, fn, waits, sig, dma):
        self.fn, self.waits, self.sig, self.dma = fn, waits, sig, dma


class KB:
    def __init__(self, nc):
        self.nc = nc
        self.prog = {e: [] for e in ENGS}
        self.nsig = {e: 0 for e in ENGS}
        self.marked = {e: set() for e in ENGS}
        self.seen = {e: {} for e in ENGS}
        self.streams = []

    def stream(self, name=None):
        key = "dma%d" % len(self.streams)
        self.streams.append(key)
        self.nsig[key] = 0
        self.marked[key] = None
        return key

    def _need(self, eng, waits, sig, same_ok):
        if sig is None:
            return
        key, idx = sig
        if key == eng and not same_ok:
            return
        if self.seen[eng].get(key, -1) >= idx:
            return
        waits[key] = max(waits.get(key, -1), idx)

    def _deps(self, eng, r, w):
        waits = {}
        for b in r:
            self._need(eng, waits, b.last_w, True)
        for b in w:
            strict = STRICT and eng != "pe" and not b.name.startswith("bank")
            self._need(eng, waits, b.last_w, strict)
            for k, i in b.readers.items():
                self._need(eng, waits, (k, i), strict)
        for k, i in waits.items():
            self.seen[eng][k] = i
            if self.marked[k] is not None:
                self.marked[k].add(i)
        return list(waits.items())

    def op(self, eng, fn, r=(), w=()):
        waits = self._deps(eng, r, w)
        idx = self.nsig[eng]
        self.nsig[eng] += 1
        sig = (eng, idx)
        for b in r:
            b.readers[eng] = idx
        for b in w:
            b.last_w = sig
            b.readers = {}
        self.prog[eng].append(_Op(fn, waits, sig, False))
        return sig

    def dma(self, eng, stream, fn, r=(), w=()):
        waits = self._deps(eng, r, w)
        idx = self.nsig[stream]
        self.nsig[stream] += 1
        sig = (stream, idx)
        for b in r:
            b.readers[stream] = idx
        for b in w:
            b.last_w = sig
            b.readers = {}
        self.prog[eng].append(_Op(fn, waits, sig, True))
        return sig

    def wait_all(self, eng, bufs):
        waits = {}
        for b in bufs:
            self._need(eng, waits, b.last_w, True)
            for k, i in b.readers.items():
                self._need(eng, waits, (k, i), True)
        for k, i in waits.items():
            self.seen[eng][k] = i
            if self.marked[k] is not None:
                self.marked[k].add(i)
        self.prog[eng].append(_Op(None, list(waits.items()), None, False))

    def barrier(self, streams=True):
        for e in ENGS:
            waits = {}
            for k in list(ENGS) + (self.streams if streams else []):
                if k == e or self.nsig[k] == 0:
                    continue
                self._need(e, waits, (k, self.nsig[k] - 1), True)
            for k, i in waits.items():
                self.seen[e][k] = i
                if self.marked[k] is not None:
                    self.marked[k].add(i)
            self.prog[e].append(_Op(None, list(waits.items()), None, False))

    def emit(self):
        nc = self.nc
        with contextlib.ExitStack() as st:
            sems = {}
            for k in list(ENGS) + self.streams:
                sems[k] = st.enter_context(nc.semaphore("s_" + k))
            rank = {}
            for k in ENGS:
                m = sorted(self.marked[k])
                rank[k] = {i: n + 1 for n, i in enumerate(m)}

            def val(k, i):
                if self.marked[k] is None:
                    return 16 * (i + 1)
                return rank[k][i]

            def run(e, eng):
                for o in self.prog[e]:
                    for k, i in o.waits:
                        eng.wait_ge(sems[k], val(k, i))
                    if o.fn is None:
                        continue
                    ins = o.fn(eng)
                    k, i = o.sig
                    if o.dma:
                        ins.then_inc(sems[k], 16)
                    elif i in self.marked[k]:
                        ins.then_inc(sems[k], 1)

            block = st.enter_context(nc.Block())

            @block.tensor
            def _(e):
                run("pe", e)

            @block.scalar
            def _(e):
                run("act", e)

            @block.vector
            def _(e):
                run("dve", e)

            @block.gpsimd
            def _(e):
                run("pool", e)

            @block.sync
            def _(e):
                run("sp", e)
from concourse.bass_utils import run_bass_kernel_spmd

D = 1024
DIN = 2592
DFF = 2816
NFC = DFF // 128
EPS = 1e-6
SCALE = 96 ** -0.5
C_CQ, C_CKV, C_KR, C_HQ, C_HF, C_HI, C_HG = 0, 384, 512, 544, 1056, 1568, 2080


def build(S, debug=False, PIPE=True):
    NT = S // 128
    NB = S // 512
    nc = bass.Bass("TRN2", target_bir_lowering=False)
    dt_in = lambda n, shp, dt=F32: nc.dram_tensor(n, list(shp), dt, kind="ExternalInput").ap()
    x_d = dt_in("x", [S, D])
    pos_d = dt_in("pos", [128, NT], I32)
    w_in_d = dt_in("w_in", [D, DIN]); w_uq_d = dt_in("w_uq", [384, 768]); w_ukv_d = dt_in("w_ukv", [128, 1024])
    w_out_d = dt_in("w_out", [D, D]); w_gate_d = dt_in("w_gate", [D, DFF]); w_up_d = dt_in("w_up", [D, DFF])
    w_down_d = dt_in("w_down", [DFF, D])
    g_pre_d = dt_in("g_pre", [D]); g_post_d = dt_in("g_post", [D]); g_fpre_d = dt_in("g_fpre", [D]); g_fpost_d = dt_in("g_fpost", [D])
    g_q_d = dt_in("g_q", [128, 3]); g_kv_d = dt_in("g_kv", [128, 1]); g_mla_d = dt_in("g_mla", [128, 4]); g_hn_d = dt_in("g_hn", [128, 4])
    lbl_d = dt_in("lbl", [128, 8])
    ident_d = dt_in("ident", [128, 128]); tri_d = dt_in("tri", [128, 128]); negtri_d = dt_in("negtri", [128, 128]); mask2_d = dt_in("mask2", [128, 128])
    resetm_d = dt_in("resetm", [512]); invf_d = dt_in("invf", [16]); invf_lo_d = dt_in("invf_lo", [16]); wcol_d = dt_in("wcol", [128, 2])
    out_d = nc.dram_tensor("out", [S, D], F32, kind="ExternalOutput").ap()
    dbg = {}
    def DBG(name, shape):
        if debug:
            dbg[name] = nc.dram_tensor(name, list(shape), F32, kind="ExternalOutput").ap()
        return dbg.get(name)
    d_mixH = DBG("d_mixH", [128, 4, S]); d_mixM = DBG("d_mixM", [128, 4, S]); d_h1 = None
    d_cq = DBG("d_cq", [128, 3, S]); d_rq = DBG("d_rq", [128, NT]); d_kr = DBG("d_kr", [128, NT, 32])

    kb = KB(nc)
    ES = contextlib.ExitStack
    with ES() as st0:
        ARENA_BYTES = 211968
        arena = st0.enter_context(nc.sbuf_tensor("arena", [128, ARENA_BYTES // 2], BF16))
        ar = {"lo": 0, "hi": ARENA_BYTES}
        def _view(off, shape, dt):
            esz = 2 if dt == BF16 else 4
            n = 1
            for d_ in shape[1:]:
                n *= d_
            v = arena[:, off // 2: off // 2 + n * esz // 2]
            if esz == 4:
                v = v.bitcast(dt)
            if len(shape) > 2:
                names = ["a%d" % i for i in range(len(shape) - 1)]
                kw = {nm: d_ for nm, d_ in zip(names[1:], shape[2:])}
                v = v.rearrange("p (%s) -> p %s" % (" ".join(names), " ".join(names)), **kw)
            if shape[0] < 128:
                v = v[0:shape[0]]
            return v
        def T(st, name, shape, dt):
            esz = 2 if dt == BF16 else 4
            n = esz
            for d_ in shape[1:]:
                n *= d_
            n = (n + 63) // 64 * 64
            if st == "top":
                ar["hi"] -= n
                off = ar["hi"]
            else:
                off = ar["lo"]
                ar["lo"] += n
            assert ar["lo"] <= ar["hi"], ("SBUF arena overflow", name, ar)
            return _view(off, list(shape), dt)
        psum = st0.enter_context(nc.psum_tensor("psum", [128, 4096], F32))
        PB = [Buf("bank%d" % i) for i in range(8)]
        bank_ctr = [0]
        def bank():
            i = bank_ctr[0] % 6
            bank_ctr[0] += 1
            return i
        lbank_ctr = [0]
        def lbank():
            i = 6 + lbank_ctr[0] % 2
            lbank_ctr[0] += 1
            return i
        def pf(i):
            return psum[:, i * 512:(i + 1) * 512]
        def pb(i):
            return psum[:, i * 512:(i + 1) * 512].bitcast(BF16)
        ob = [Buf("out%d" % g) for g in range(S // 128)]
        st_ld = [kb.stream() for _ in range(4)]
        st_w = kb.stream()
        st_x = [kb.stream() for _ in range(2)]
        st_o = [kb.stream() for _ in range(2)]
        st_dbg = kb.stream()

        def mm(out, lhsT, rhs, start, stop, r, w):
            kb.op("pe", lambda e: e.matmul(out, lhsT=lhsT, rhs=rhs, start=start, stop=stop), r=r, w=w)
        def tr(out, in_, ident, r, w):
            kb.op("pe", lambda e: e.transpose(out=out, in_=in_, identity=ident), r=r, w=w)
        def act(out, in_, func, r, w, scale=1.0, bias=0.0, accum=None):
            if accum is None:
                kb.op("act", lambda e: e.activation(out=out, in_=in_, func=func, bias=bias, scale=scale), r=r, w=w)
            else:
                kb.op("act", lambda e: e.activation(out=out, in_=in_, func=func, bias=bias, scale=scale, accum_out=accum), r=r, w=w)
        def ts(eng, out, in0, s1, s2, op0, op1, r, w):
            if s2 is None:
                kb.op(eng, lambda e: e.tensor_scalar(out=out, in0=in0, scalar1=s1, scalar2=None, op0=op0), r=r, w=w)
            else:
                kb.op(eng, lambda e: e.tensor_scalar(out=out, in0=in0, scalar1=s1, scalar2=s2, op0=op0, op1=op1), r=r, w=w)
        def tt(eng, out, in0, in1, op, r, w):
            kb.op(eng, lambda e: e.tensor_tensor(out=out, in0=in0, in1=in1, op=op), r=r, w=w)
        def stt(eng, out, in0, scalar, in1, op0, op1, r, w):
            kb.op(eng, lambda e: e.scalar_tensor_tensor(out=out, in0=in0, scalar=scalar, in1=in1, op0=op0, op1=op1), r=r, w=w)
        def cp(eng, out, in_, r, w):
            if eng == "act":
                kb.op("act", lambda e: e.copy(out=out, in_=in_), r=r, w=w)
            else:
                kb.op(eng, lambda e: e.tensor_copy(out=out, in_=in_), r=r, w=w)
        def recip(out, in_, r, w):
            kb.op("dve", lambda e: e.reciprocal(out=out, in_=in_), r=r, w=w)
        def dma(eng, stream, out, in_, r, w):
            kb.dma(eng, stream, lambda e: e.dma_start(out=out, in_=in_), r=r, w=w)
        def rsqrt_act(out, in_, scale, r, w):
            act(out, in_, AF.Ln, r=r, w=w, scale=scale, bias=epsb[:in_.shape[0], 0:1])
            act(out, out, AF.Exp, r=w, w=w, scale=-0.5)
        def dump(dap, tile_ap, buf):
            if dap is not None:
                dma("sp", st_dbg, dap, tile_ap, r=[buf], w=[Buf("dbgout")])

        identb = T(st0, "identb", [128, 128], BF16); b_identb = Buf("identb")
        identf = T(st0, "identf", [128, 128], F32); b_identf = Buf("identf")
        trib = T(st0, "trib", [128, 128], BF16); b_trib = Buf("trib")
        negtrib = T(st0, "negtrib", [128, 128], BF16); b_negtri = Buf("negtrib")
        mask2 = T(st0, "mask2", [128, 128], F32); b_mask2 = Buf("mask2")
        ones_bf = T(st0, "ones_bf", [128, 128], BF16); b_ones = Buf("ones")
        epsb = T(st0, "epsb", [128, 1], F32); b_eps = Buf("epsb")
        wcol = T(st0, "wcol", [128, 2], F32); b_wcol = Buf("wcol")
        g_q = T(st0, "g_q", [128, 3], F32); g_kv = T(st0, "g_kv", [128, 1], F32)
        g_mla = T(st0, "g_mla", [128, 4], F32); g_hn = T(st0, "g_hn", [128, 4], F32)
        b_gq = Buf("g_q"); b_gkv = Buf("g_kv"); b_gmla = Buf("g_mla"); b_ghn = Buf("g_hn")
        lb = T(st0, "lb", [128, 4], F32); oml = T(st0, "oml", [128, 4], F32); b_lb = Buf("lb")
        fA = T(st0, "fA", [128, 4], F32); fB = T(st0, "fB", [128, 4], F32)
        rstdq = T(st0, "rstdq", [128, NT], F32); b_rstdq = Buf("rstdq")
        rstdkv = T(st0, "rstdkv", [128, NT], F32); b_rstdkv = Buf("rstdkv")
        Wuq = T(st0, "Wuq", [128, 3, 768], BF16); Wukv = T(st0, "Wukv", [128, 1024], BF16); b_Wu = Buf("Wu")
        markMix = ar["lo"]
        mixH = T(st0, "mixH", [128, 4, S], BF16); b_mixH = [[Buf("mixH%d_%d" % (h, b)) for b in range(NB)] for h in range(4)]
        c_qT = T("top", "c_qT", [128, 3, S], BF16); b_cq = [Buf("cq%d" % b) for b in range(NB)]
        c_kvT = T("top", "c_kvT", [128, S], BF16); b_ckv = [Buf("ckv%d" % b) for b in range(NB)]
        krope = T("top", "krope", [128, NT, 32], BF16); b_krope = [Buf("krope%d" % b) for b in range(NB)]
        cosq = T("top", "cosq", [128, NT, 16], F32); sinq = T("top", "sinq", [128, NT, 16], F32); b_csq = Buf("cossinq")

        dma("pool", kb.stream(), identb[:], ident_d, r=[], w=[b_identb])
        dma("sp", kb.stream(), identf[:], ident_d, r=[], w=[b_identf])
        dma("pool", kb.stream(), trib[:], tri_d, r=[], w=[b_trib])
        dma("pool", kb.stream(), negtrib[:], negtri_d, r=[], w=[b_negtri])
        dma("sp", kb.stream(), mask2[:], mask2_d, r=[], w=[b_mask2])
        dma("sp", kb.stream(), wcol[:], wcol_d, r=[], w=[b_wcol])
        dma("sp", kb.stream(), g_q[:], g_q_d, r=[], w=[b_gq])
        dma("sp", kb.stream(), g_kv[:], g_kv_d, r=[], w=[b_gkv])
        dma("sp", kb.stream(), g_mla[:], g_mla_d, r=[], w=[b_gmla])
        dma("sp", kb.stream(), g_hn[:], g_hn_d, r=[], w=[b_ghn])
        kb.op("pool", lambda e: e.memset(ones_bf[:], 1.0), w=[b_ones])
        kb.op("pool", lambda e: e.memset(epsb[:], EPS), w=[b_eps])

        mixM = T(st0, "mixM", [128, 4, S], BF16); b_mixM = Buf("mixM")
        markA = dict(ar)
        ar["lo"] = markA["lo"] - 4 * S * 2
        if True:
            stA = None
            W_in = T(stA, "W_in", [128, 8, DIN], BF16)
            WG = {"cq": (0, 544), "hi": (C_HI, C_HI + 512), "hq": (C_HQ, C_HQ + 512), "hf": (C_HF, C_HF + 512), "hg": (C_HG, C_HG + 512)}
            b_Wg_in = {k: Buf("W_in_" + k) for k in WG}
            stw_k = kb.stream()
            b_Wall = Buf("W_in_all")
            for k in WG:
                b_Wg_in[k] = b_Wall
            for dc in range(8):
                dma("pool", stw_k, W_in[:, dc, :], w_in_d[dc * 128:(dc + 1) * 128, :], r=[], w=[b_Wall])
            st_wu = kb.stream()
            for rc in range(3):
                dma("pool", st_wu, Wuq[:, rc, :], w_uq_d[rc * 128:(rc + 1) * 128, :], r=[], w=[b_Wu])
            dma("pool", st_wu, Wukv[:], w_ukv_d, r=[], w=[b_Wu])
            def b_Win_for(col):
                for k, (c0, c1) in WG.items():
                    if c0 <= col < c1:
                        return b_Wg_in[k]
                raise AssertionError(col)
            gpre_b = T(stA, "gpre_b", [128, D], F32); b_gpre = Buf("gpre")
            dma("sp", kb.stream(), gpre_b[:], g_pre_d.partition_broadcast(128), r=[], w=[b_gpre])
            resetm = T(stA, "resetm", [128, 512], F32); b_resetm = Buf("resetm")
            dma("sp", kb.stream(), resetm[:], resetm_d.partition_broadcast(128), r=[], w=[b_resetm])
            cosT = T(stA, "cosT", [128, NT, 16], F32); sinT = T(stA, "sinT", [128, NT, 16], F32); b_cs = Buf("cossin")
            lbl = T(stA, "lbl", [128, 8], F32); b_lbl = Buf("lbl")
            dma("sp", kb.stream(), lbl[:], lbl_d, r=[], w=[b_lbl])
            tt("dve", lb[:], lbl[:, 4:8], lbl[:, 0:4], ALU.subtract, r=[b_lbl], w=[b_lb])
            act(lb[:], lb[:], AF.Exp, r=[b_lb], w=[b_lb])
            ts("dve", lb[:], lb[:], 1.0, None, ALU.add, None, r=[b_lb], w=[b_lb])
            recip(lb[:], lb[:], r=[b_lb], w=[b_lb])
            ts("dve", oml[:], lb[:], -1.0, 1.0, ALU.mult, ALU.add, r=[b_lb], w=[b_lb])
            ts("dve", fA[:], oml[:], 0.5, None, ALU.mult, None, r=[b_lb], w=[b_lb])
            tt("dve", fB[:], lb[:], fA[:], ALU.add, r=[b_lb], w=[b_lb])
            markR = dict(ar)
            if True:
                stR = None
                posi = T(stR, "posi", [128, NT], I32); posf = T(stR, "posf", [128, NT], F32)
                invf = T(stR, "invf", [128, 16], F32)
                invf_lo = T(stR, "invf_lo", [128, 16], F32)
                ang = T(stR, "ang", [128, NT, 16], F32); a2 = T(stR, "a2", [128, NT, 16], F32)
                nq = T(stR, "nq", [128, NT, 16], F32); ni = T(stR, "ni", [128, NT, 16], I32)
                b_r = Buf("ropetmp")
                b_rp = Buf("rope_pos"); b_ri = Buf("rope_invf"); b_ril = Buf("rope_invf_lo")
                dma("sp", kb.stream(), posi[:], pos_d, r=[], w=[b_rp])
                dma("sp", kb.stream(), invf[:], invf_d.partition_broadcast(128), r=[], w=[b_ri])
                dma("sp", kb.stream(), invf_lo[:], invf_lo_d.partition_broadcast(128), r=[], w=[b_ril])
                cp("dve", posf[:], posi[:], r=[b_rp], w=[b_r])
                tt("dve", ang[:], posf[:].unsqueeze(2).broadcast_to([128, NT, 16]),
                   invf[:].unsqueeze(1).broadcast_to([128, NT, 16]), ALU.mult, r=[b_r, b_ri], w=[b_r])
                tt("dve", nq[:], posf[:].unsqueeze(2).broadcast_to([128, NT, 16]),
                   invf_lo[:].unsqueeze(1).broadcast_to([128, NT, 16]), ALU.mult, r=[b_r, b_ril], w=[b_r])
                tt("dve", ang[:], ang[:], nq[:], ALU.add, r=[b_r], w=[b_r])
                TWO_PI = float(2 * np.pi)
                HI = float(np.float32(6.28125)); LO = float(np.float32(2 * np.pi - 6.28125))
                def reduce_sin(dst, shift):
                    ts("dve", a2[:], ang[:], shift, None, ALU.add, None, r=[b_r], w=[b_r])
                    ts("dve", nq[:], a2[:], 1.0 / TWO_PI, None, ALU.mult, None, r=[b_r], w=[b_r])
                    cp("dve", ni[:], nq[:], r=[b_r], w=[b_r])
                    cp("dve", nq[:], ni[:], r=[b_r], w=[b_r])
                    stt("dve", a2[:], nq[:], -HI, a2[:], ALU.mult, ALU.add, r=[b_r], w=[b_r])
                    stt("dve", a2[:], nq[:], -LO, a2[:], ALU.mult, ALU.add, r=[b_r], w=[b_r])
                    ts("dve", nq[:], a2[:], float(np.pi), -TWO_PI, ALU.is_gt, ALU.mult, r=[b_r], w=[b_r])
                    tt("dve", a2[:], a2[:], nq[:], ALU.add, r=[b_r], w=[b_r])
                    ts("dve", nq[:], a2[:], float(-np.pi), TWO_PI, ALU.is_lt, ALU.mult, r=[b_r], w=[b_r])
                    tt("dve", a2[:], a2[:], nq[:], ALU.add, r=[b_r], w=[b_r])
                    act(dst, a2[:], AF.Sin, r=[b_r], w=[b_cs])
                reduce_sin(sinT[:], 0.0)
                reduce_sin(cosT[:], float(np.pi / 2))
                kb.barrier(streams=False)
            ar.update(markR)

            uT = [T(stA, "uT%d" % i, [128, 8, 512], BF16) for i in range(2)]; b_uT = [Buf("uT%d" % i) for i in range(2)]
            xt = [T(stA, "xt%d" % i, [128, D], F32) for i in range(2)]; b_xt = [Buf("xt%d" % i) for i in range(2)]
            ub = [T(stA, "ub%d" % i, [128, D], BF16) for i in range(2)]; b_ub = [Buf("ub%d" % i) for i in range(2)]
            junk = T(stA, "junk", [128, D], BF16); b_junk = Buf("junk")
            ssx = T(stA, "ssx", [128, 2], F32); b_ssx = [Buf("ssx0"), Buf("ssx1")]
            sqb = [T(stA, "sqb%d" % i, [128, 512], BF16) for i in range(4)]; b_sqb = [Buf("sqb%d" % i) for i in range(4)]
            v_tm = [T(stA, "v_tm%d" % i, [128, 4, 512], BF16) for i in range(2)]
            b_vtm = [[Buf("vtm%d_%d" % (j, i)) for i in range(4)] for j in range(2)]
            rtmp = T(stA, "rtmp", [128, 2, 4, 16], F32); b_rtmp = Buf("rtmp")
            tgn = [T(stA, "tgn%d" % i, [128, 512], F32) for i in range(6)]; b_tgn = [Buf("tgn%d" % i) for i in range(6)]
            kt_bf = T(stA, "kt_bf", [128, 512], BF16); kdT_bf = T(stA, "kdT_bf", [128, 512], BF16)
            b_kt, b_kdT = Buf("kt"), Buf("kdT")
            teb = [T(stA, "teb%d" % i, [128, 512], F32) for i in range(2)]; b_teb = [Buf("teb%d" % i) for i in range(2)]
            tg = [T(stA, "tg%d" % i, [128, 512], F32) for i in range(2)]; b_tg = [Buf("tg%d" % i) for i in range(2)]
            qt_bf = [T(stA, "qt_bf%d" % i, [128, 512], BF16) for i in range(2)]; b_qt = [Buf("qt%d" % i) for i in range(2)]
            kd_tm = [T(stA, "kd_tm%d" % i, [128, 4, 128], BF16) for i in range(2)]; b_kdtm = [Buf("kdtm%d" % i) for i in range(2)]
            aT_sb = [T(stA, "aT_sb%d" % i, [128, 4, 128], BF16) for i in range(2)]; b_aT = [Buf("aT%d" % i) for i in range(2)]
            t_x = T(stA, "t_x", [128, 512], F32); B_x = Buf("t_x")
            osq = T(stA, "osq", [128, 512], BF16); b_osq = Buf("osq")
            S32 = T(stA, "S32", [128, 4, 128], F32); Sbf = T(stA, "Sbf", [128, 4, 128], BF16)
            b_S32 = [Buf("S32_%d" % h) for h in range(4)]; b_Sbf = [Buf("Sbf_%d" % h) for h in range(4)]
            kb.op("pool", lambda e: e.memset(S32[:], 0.0), w=b_S32)
            kb.op("pool", lambda e: e.memset(Sbf[:], 0.0), w=b_Sbf)

            def A123(b):
                ub_ = uT[b % 2]; bu = b_uT[b % 2]
                blk = slice(b * 512, (b + 1) * 512)
                for i in range(4):
                    g = 4 * b + i
                    sl = g % 2
                    dma("sp", st_x[sl], xt[sl][:], x_d[g * 128:(g + 1) * 128, :], r=[], w=[b_xt[sl]])
                    act(junk[:], xt[sl][:], AF.Square, r=[b_xt[sl]], w=[b_junk, b_ssx[sl]], accum=ssx[:, sl:sl + 1])
                    rsqrt_act(ssx[:, sl:sl + 1], ssx[:, sl:sl + 1], 1.0 / D, r=[b_ssx[sl], b_eps], w=[b_ssx[sl]])
                    stt("dve", ub[sl][:], xt[sl][:], ssx[:, sl:sl + 1], gpre_b[:], ALU.mult, ALU.mult,
                        r=[b_xt[sl], b_ssx[sl], b_gpre], w=[b_ub[sl]])
                    bk = bank()
                    for dc in range(8):
                        tr(pb(bk)[:, dc * 128:(dc + 1) * 128], ub[sl][:, dc * 128:(dc + 1) * 128], identb[:],
                           r=[b_ub[sl], b_identb], w=[PB[bk]])
                    cp("act", ub_[:, :, i * 128:(i + 1) * 128], pb(bk).rearrange("p (c t) -> p c t", t=128),
                       r=[], w=[PB[bk], bu])
                    yield
                def inproj_fm(col):
                    bk = bank()
                    for dc in range(8):
                        mm(pf(bk), W_in[:, dc, col:col + 128], ub_[:, dc, :], dc == 0, dc == 7, r=[b_Win_for(col), bu], w=[PB[bk]])
                    return bk
                for rc in range(4):
                    bk = inproj_fm(C_CQ + rc * 128)
                    act(sqb[rc][:], pf(bk), AF.Square, r=[], w=[PB[bk], b_sqb[rc]])
                    if rc < 3:
                        ts("dve", c_qT[:, rc, blk], pf(bk), g_q[:, rc:rc + 1], None, ALU.mult, None, r=[b_gq], w=[PB[bk], b_cq[b]])
                    else:
                        ts("dve", c_kvT[:, blk], pf(bk), g_kv[:, 0:1], None, ALU.mult, None, r=[b_gkv], w=[PB[bk], b_ckv[b]])
                    yield
                bk = bank()
                for i in range(4):
                    for rc in range(3):
                        mm(pf(bk)[:, i:i + 1], sqb[rc][:, i * 128:(i + 1) * 128], ones_bf[:, 0:1], rc == 0, rc == 2,
                           r=[b_sqb[rc], b_ones], w=[PB[bk]])
                for i in range(4):
                    mm(pf(bk)[:, 4 + i:5 + i], sqb[3][:, i * 128:(i + 1) * 128], ones_bf[:, 0:1], True, True,
                       r=[b_sqb[3], b_ones], w=[PB[bk]])
                act(rstdq[:, 4 * b:4 * b + 4], pf(bk)[:, 0:4], AF.Ln, r=[b_eps], w=[PB[bk], b_rstdq], scale=1.0 / 384, bias=epsb[:, 0:1])
                act(rstdq[:, 4 * b:4 * b + 4], rstdq[:, 4 * b:4 * b + 4], AF.Exp, r=[b_rstdq], w=[b_rstdq], scale=-0.5)
                act(rstdkv[:, 4 * b:4 * b + 4], pf(bk)[:, 4:8], AF.Ln, r=[b_eps], w=[PB[bk], b_rstdkv], scale=1.0 / 128, bias=epsb[:, 0:1])
                act(rstdkv[:, 4 * b:4 * b + 4], rstdkv[:, 4 * b:4 * b + 4], AF.Exp, r=[b_rstdkv], w=[b_rstdkv], scale=-0.5)
                bkr = bank()
                for i in range(4):
                    for dc in range(8):
                        mm(pf(bkr)[:, i * 32:(i + 1) * 32], ub_[:, dc, i * 128:(i + 1) * 128], W_in[:, dc, C_KR:C_KR + 32],
                           dc == 0, dc == 7, r=[b_Wg_in["cq"], bu], w=[PB[bkr]])
                kr3 = pf(bkr)[:, 0:128].rearrange("p (i c) -> p i c", c=32)
                x1 = kr3[:, :, 0:16]; x2 = kr3[:, :, 16:32]
                cs = cosT[:, 4 * b:4 * b + 4, :]; sn = sinT[:, 4 * b:4 * b + 4, :]
                tt("dve", rtmp[:, 0], x1, cs, ALU.mult, r=[b_cs], w=[PB[bkr], b_rtmp])
                tt("dve", rtmp[:, 1], x2, sn, ALU.mult, r=[b_cs], w=[PB[bkr], b_rtmp])
                tt("dve", krope[:, 4 * b:4 * b + 4, 0:16], rtmp[:, 0], rtmp[:, 1], ALU.subtract, r=[b_rtmp], w=[b_krope[b]])
                tt("dve", rtmp[:, 0], x1, sn, ALU.mult, r=[b_cs], w=[PB[bkr], b_rtmp])
                tt("dve", rtmp[:, 1], x2, cs, ALU.mult, r=[b_cs], w=[PB[bkr], b_rtmp])
                tt("dve", krope[:, 4 * b:4 * b + 4, 16:32], rtmp[:, 0], rtmp[:, 1], ALU.add, r=[b_rtmp], w=[b_krope[b]])

                yield
                for i in range(4):
                    bk = bank()
                    for dc in range(8):
                        mm(pf(bk), ub_[:, dc, i * 128:(i + 1) * 128], W_in[:, dc, C_HI:C_HI + 512], dc == 0, dc == 7,
                           r=[b_Wg_in["hi"], bu], w=[PB[bk]])
                    cp("act", v_tm[b % 2][:, i, :], pf(bk), r=[], w=[PB[bk], b_vtm[b % 2][i]])
                    yield
            def inproj_fm2(b, col):
                ub_ = uT[b % 2]; bu = b_uT[b % 2]
                bk = bank()
                for dc in range(8):
                    mm(pf(bk), W_in[:, dc, col:col + 128], ub_[:, dc, :], dc == 0, dc == 7, r=[b_Win_for(col), bu], w=[PB[bk]])
                return bk

            def G(b, h, st):
                bq = inproj_fm2(b, C_HQ + h * 128)
                bf = inproj_fm2(b, C_HF + h * 128)
                bg = inproj_fm2(b, C_HG + h * 128)
                t_f, t_lf, t_kk, t_b, t_enb, t_q = tgn
                B_f, B_lf, B_kk, B_b, B_enb, B_q = b_tgn
                t_eb = teb[st]; B_eb = b_teb[st]; t_g = tg[st]; B_g = b_tg[st]
                act(t_f[:], pf(bf), AF.Tanh, r=[], w=[PB[bf], B_f], scale=0.5)
                act(t_q[:], pf(bq), AF.Silu, r=[], w=[PB[bq], B_q])
                act(t_g[:], pf(bg), AF.Silu, r=[], w=[PB[bg], B_g])
                yield
                ts("dve", t_f[:], t_f[:], fA[:, h:h + 1], fB[:, h:h + 1], ALU.mult, ALU.add, r=[B_f, b_lb], w=[B_f])
                act(t_lf[:], t_f[:], AF.Ln, r=[B_f], w=[B_lf])
                yield
                ts("dve", t_kk[:], t_f[:], -1.0, 1.0, ALU.mult, ALU.add, r=[B_f], w=[B_kk])
                kb.op("dve", lambda e: e.tensor_tensor_scan(out=t_b[:], data0=resetm[:], data1=t_lf[:], initial=0.0,
                                                            op0=ALU.mult, op1=ALU.add), r=[b_resetm, B_lf], w=[B_b])
                yield
                act(t_eb[:], t_b[:], AF.Exp, r=[B_b], w=[B_eb])
                act(t_enb[:], t_b[:], AF.Exp, r=[B_b], w=[B_enb], scale=-1.0)
                yield
                tt("dve", t_kk[:], t_kk[:], t_enb[:], ALU.mult, r=[B_kk, B_enb], w=[B_kk])
                cp("act", kt_bf[:], t_kk[:], r=[B_kk], w=[b_kt])
                eb3 = t_eb[:].rearrange("p (c t) -> p c t", t=64)
                tt("dve", kdT_bf[:].rearrange("p (c t) -> p c t", t=64), t_kk[:].rearrange("p (c t) -> p c t", t=64),
                   eb3[:, :, 63:64].broadcast_to([128, 8, 64]), ALU.mult, r=[B_kk, B_eb], w=[b_kdT])
                tt("dve", qt_bf[st][:], t_q[:], t_eb[:], ALU.mult, r=[B_q, B_eb], w=[b_qt[st]])
                yield
                bk = bank()
                for i in range(4):
                    tr(pb(bk)[:, i * 128:(i + 1) * 128], kdT_bf[:, i * 128:(i + 1) * 128], identb[:], r=[b_kdT, b_identb], w=[PB[bk]])
                cp("act", kd_tm[st][:], pb(bk)[:, 0:512].rearrange("p (i k) -> p i k", k=128), r=[], w=[PB[bk], b_kdtm[st]])
                yield
                bk = bank()
                for i in range(4):
                    mm(pf(bk)[:, i * 128:(i + 1) * 128], kt_bf[:, i * 128:(i + 1) * 128], qt_bf[st][:, i * 128:(i + 1) * 128], True, True,
                       r=[b_kt, b_qt[st]], w=[PB[bk]])
                tt("dve", aT_sb[st][:], pf(bk).rearrange("p (i t) -> p i t", t=128), mask2[:].unsqueeze(1).broadcast_to([128, 4, 128]),
                   ALU.mult, r=[b_mask2], w=[PB[bk], b_aT[st]])
                yield

            def R(b, h, st):
                blk = slice(b * 512, (b + 1) * 512)
                t_eb = teb[st]; B_eb = b_teb[st]; t_g = tg[st]; B_g = b_tg[st]
                vt = v_tm[b % 2]; bv = b_vtm[b % 2]
                bo = lbank()
                for i in range(4):
                    mm(pf(bo)[:, i * 128:(i + 1) * 128], vt[:, i, h * 128:(h + 1) * 128], aT_sb[st][:, i, :], True, False,
                       r=[bv[i], b_aT[st]], w=[PB[bo]])
                    for cc in range(2):
                        c = 2 * i + cc
                        mm(pf(bo)[:, c * 64:(c + 1) * 64], Sbf[:, h, :], qt_bf[st][:, c * 64:(c + 1) * 64], False, cc == 1,
                           r=[b_Sbf[h], b_qt[st]], w=[PB[bo]])
                        bs = bank()
                        rows = slice(cc * 64, cc * 64 + 64)
                        mm(pf(bs)[:, 0:128], kd_tm[st][rows, i, :], vt[rows, i, h * 128:(h + 1) * 128], True, True,
                           r=[b_kdtm[st], bv[i]], w=[PB[bs]])
                        dec = t_eb[:, c * 64 + 63:c * 64 + 64]
                        stt("dve", Sbf[:, h, :], S32[:, h, :], dec, pf(bs)[:, 0:128], ALU.mult, ALU.add,
                            r=[B_eb, b_S32[h]], w=[PB[bs], b_Sbf[h]])
                        stt("dve", S32[:, h, :], S32[:, h, :], dec, pf(bs)[:, 0:128], ALU.mult, ALU.add,
                            r=[B_eb], w=[PB[bs], b_S32[h]])
                        yield
                act(osq[:], pf(bo), AF.Square, r=[], w=[PB[bo], b_osq])
                bn = bank()
                mm(pf(bn), ones_bf[:], osq[:], True, True, r=[b_ones, b_osq], w=[PB[bn]])
                act(t_x[:], pf(bn), AF.Ln, r=[b_eps], w=[PB[bn], B_x], scale=1.0 / 128, bias=epsb[:, 0:1])
                act(t_x[:], t_x[:], AF.Exp, r=[B_x], w=[B_x], scale=-0.5)
                tt("dve", t_x[:], pf(bo), t_x[:], ALU.mult, r=[B_x], w=[PB[bo], B_x])
                stt("dve", mixH[:, h, blk], t_x[:], g_hn[:, h:h + 1], t_g[:], ALU.mult, ALU.mult, r=[B_x, B_g, b_ghn], w=[b_mixH[h][b]])

            seq = [(b, h) for b in range(NB) for h in range(4)]

            def run_merged(gens):
                gens = list(gens)
                while gens:
                    for g_ in list(gens):
                        try:
                            next(g_)
                        except StopIteration:
                            gens.remove(g_)

            for _ in A123(0):
                pass
            carry = None
            for k in range(len(seq) + 1):
                gens = []
                if k < len(seq):
                    b, h = seq[k]
                    gens.append(G(b, h, k % 2))
                    if h == 1 and b + 1 < NB:
                        carry = A123(b + 1)
                if k > 0:
                    gens.append(R(seq[k - 1][0], seq[k - 1][1], (k - 1) % 2))
                if carry is not None:
                    hh = seq[k][1] if k < len(seq) else 0
                    if hh == 3:
                        gens.append(carry)
                        carry = None
                    else:
                        def part(gc, n):
                            for _ in range(n):
                                try:
                                    next(gc)
                                except StopIteration:
                                    return
                                yield
                        gens.append(part(carry, 5))
                run_merged(gens)
            print("arena phase A: lo", ar["lo"], "hi", ar["hi"])
            tt("dve", cosq[:], cosT[:], rstdq[:].unsqueeze(2).broadcast_to([128, NT, 16]), ALU.mult, r=[b_cs, b_rstdq], w=[b_csq])
            tt("dve", sinq[:], sinT[:], rstdq[:].unsqueeze(2).broadcast_to([128, NT, 16]), ALU.mult, r=[b_cs, b_rstdq], w=[b_csq])
            if debug:
                dump(d_mixH, mixH[:], b_mixH[0][0]); dump(d_cq, c_qT[:], b_cq[0]); dump(d_rq, rstdq[:], b_rstdq); dump(d_kr, None, None) if False else None
            kb.barrier()

        ar["lo"] = markA["lo"]
        if True:
            stB = None
            Wout = T(stB, "Wout", [128, 8, D], BF16); b_Wout = Buf("Wout")
            st_wout = kb.stream()
            for ecx in range(8):
                dma("pool", st_wout, Wout[:, ecx, :], w_out_d[ecx * 128:(ecx + 1) * 128, :], r=[], w=[b_Wout])
            gpost_b = T(stB, "gpost_b", [128, D], F32); b_gpost = Buf("gpost")
            dma("sp", kb.stream(), gpost_b[:], g_post_d.partition_broadcast(128), r=[], w=[b_gpost])
            markC = ar["lo"]
            QT = [T(stB, "QT%d" % i, [128, S], BF16) for i in range(2)]; b_QT = [[Buf("QT%d_%d" % (i, b)) for b in range(NB)] for i in range(2)]
            KT = [T(stB, "KT%d" % i, [128, S], BF16) for i in range(2)]; b_KT = [[Buf("KT%d_%d" % (i, b)) for b in range(NB)] for i in range(2)]
            Vh = [T(stB, "Vh%d" % i, [128, NT, 128], BF16) for i in range(2)]; b_Vh = [[Buf("Vh%d_%d" % (i, b)) for b in range(NB)] for i in range(2)]
            q_tm2 = [T(stB, "q_tm%d" % i, [128, 4, 96], BF16) for i in range(2)]; k_tm2 = [T(stB, "k_tm%d" % i, [128, 4, 96], BF16) for i in range(2)]
            b_qtm2 = [Buf("qtm0"), Buf("qtm1")]; b_ktm2 = [Buf("ktm0"), Buf("ktm1")]
            rt2 = T(stB, "rt2", [128, 2, 4, 16], F32); b_rt2 = Buf("rt2")
            PT = [T(stB, "PT%d" % i, [128, 1024], BF16) for i in range(3)]; b_PT = [Buf("PT%d" % i) for i in range(3)]
            sqw = T(stB, "sqw", [128, 512], BF16); b_sqw = Buf("sqw")
            rs_t = T(stB, "rs_t", [128, 512], F32); b_rs = Buf("rs_t")
            kb.op("pool", lambda e: e.memset(Vh[0][:], 0.0), w=b_Vh[0])
            kb.op("pool", lambda e: e.memset(Vh[1][:], 0.0), w=b_Vh[1])
            kb.op("pool", lambda e: e.memset(Vh[0][:, :, 64:65], 1.0), w=b_Vh[0])
            kb.op("pool", lambda e: e.memset(Vh[1][:, :, 0:1], 1.0), w=b_Vh[1])
            if True:
                pt_state = {"ctr": 0}
                pair_state = {"ctr": 0}
                NPT = 3

                def prep_head(h):
                    hp = h % 2
                    voff = 0 if hp == 0 else 64

                    def seg1(b):
                        q_tm = q_tm2[b % 2]; k_tm = k_tm2[b % 2]; b_qtm = b_qtm2[b % 2]; b_ktm = b_ktm2[b % 2]
                        t4 = slice(4 * b, 4 * b + 4)
                        bq = bank(); bkv = bank()
                        for i in range(4):
                            tok = slice(b * 512 + i * 128, b * 512 + (i + 1) * 128)
                            for rc in range(3):
                                mm(pf(bq)[:, i * 96:(i + 1) * 96], c_qT[:, rc, tok], Wuq[:, rc, h * 96:(h + 1) * 96], rc == 0, rc == 2,
                                   r=[b_cq[b], b_Wu], w=[PB[bq]])
                            mm(pf(bkv)[:, i * 128:(i + 1) * 128], c_kvT[:, tok], Wukv[:, h * 128:(h + 1) * 128], True, True,
                               r=[b_ckv[b], b_Wu], w=[PB[bkv]])
                        q3 = pf(bq)[:, 0:384].rearrange("p (i c) -> p i c", c=96)
                        kv3 = pf(bkv).rearrange("p (i c) -> p i c", c=128)
                        rq_b = rstdq[:, t4].unsqueeze(2).broadcast_to([128, 4, 64])
                        rkv_b = rstdkv[:, t4].unsqueeze(2).broadcast_to([128, 4, 64])
                        tt("dve", q_tm[:, :, 0:64], q3[:, :, 0:64], rq_b, ALU.mult, r=[b_rstdq], w=[PB[bq], b_qtm])
                        x1 = q3[:, :, 64:80]; x2 = q3[:, :, 80:96]
                        cs = cosq[:, t4, :]; sn = sinq[:, t4, :]
                        tt("dve", rt2[:, 0], x1, cs, ALU.mult, r=[b_csq], w=[PB[bq], b_rt2])
                        tt("dve", rt2[:, 1], x2, sn, ALU.mult, r=[b_csq], w=[PB[bq], b_rt2])
                        tt("dve", q_tm[:, :, 64:80], rt2[:, 0], rt2[:, 1], ALU.subtract, r=[b_rt2], w=[b_qtm])
                        tt("dve", rt2[:, 0], x1, sn, ALU.mult, r=[b_csq], w=[PB[bq], b_rt2])
                        tt("dve", rt2[:, 1], x2, cs, ALU.mult, r=[b_csq], w=[PB[bq], b_rt2])
                        tt("dve", q_tm[:, :, 80:96], rt2[:, 0], rt2[:, 1], ALU.add, r=[b_rt2], w=[b_qtm])
                        tt("dve", k_tm[:, :, 0:64], kv3[:, :, 0:64], rkv_b, ALU.mult, r=[b_rstdkv], w=[PB[bkv], b_ktm])
                        cp("pool", k_tm[:, :, 64:96], krope[:, t4, :], r=[b_krope[b]], w=[b_ktm])
                        tt("dve", Vh[hp][:, t4, voff:voff + 64], kv3[:, :, 64:128], rkv_b, ALU.mult, r=[b_rstdkv], w=[PB[bkv], b_Vh[hp][b]])

                    def seg2(b):
                        q_tm = q_tm2[b % 2]; k_tm = k_tm2[b % 2]; b_qtm = b_qtm2[b % 2]; b_ktm = b_ktm2[b % 2]
                        blk = slice(b * 512, (b + 1) * 512)
                        bk = bank()
                        for i in range(4):
                            tr(pb(bk)[0:96, i * 128:(i + 1) * 128], q_tm[:, i, :], identb[:], r=[b_qtm, b_identb], w=[PB[bk]])
                        cp("act", QT[hp][0:96, blk], pb(bk)[0:96, 0:512], r=[], w=[PB[bk], b_QT[hp][b]])
                        bk = bank()
                        for i in range(4):
                            tr(pb(bk)[0:96, i * 128:(i + 1) * 128], k_tm[:, i, :], identb[:], r=[b_ktm, b_identb], w=[PB[bk]])
                        cp("act", KT[hp][0:96, blk], pb(bk)[0:96, 0:512], r=[], w=[PB[bk], b_KT[hp][b]])

                    seg1(0)
                    yield ("s1", 0)
                    for b in range(NB):
                        if b + 1 < NB:
                            seg1(b + 1)
                            yield ("s1", b + 1)
                        seg2(b)
                        yield ("s2", b)

                def attn_head(h):
                    hp = h % 2
                    voff = 0 if hp == 0 else 64
                    ec = h // 2
                    M = 65 if hp == 0 else 128
                    Mo = 64 if hp == 0 else 128
                    groups = []
                    for qb in range(NB):
                        for j in range(2 * qb):
                            groups.append((qb, [2 * j, 2 * j + 1]))
                        for i in range(4):
                            groups.append((qb, [4 * qb + i]))
                    obank = {}
                    info = {}

                    def emit_qk(gi):
                        qb, kbs = groups[gi]
                        pr = pair_state["ctr"] % 3
                        pair_state["ctr"] += 1
                        b0 = 2 * pr
                        p = pr
                        if len(kbs) == 2:
                            for j, kbi in enumerate(kbs):
                                mm(pf(b0 + j), KT[hp][0:96, kbi * 128:(kbi + 1) * 128], QT[hp][0:96, qb * 512:(qb + 1) * 512], True, True,
                                   r=[b_KT[hp][kbi // 4], b_QT[hp][qb]], w=[PB[b0 + j]])
                            act(PT[p][:, 0:1024], psum[:, b0 * 512:(b0 + 2) * 512], AF.Exp, r=[], w=[PB[b0], PB[b0 + 1], b_PT[p]], scale=SCALE)
                            info[gi] = (p, [(kbs[0], 0, 0), (kbs[1], 512, 0)])
                        else:
                            kbi = kbs[0]
                            i = kbi - 4 * qb
                            qlo = 128 * i
                            qs = slice(qb * 512 + qlo, (qb + 1) * 512)
                            mm(pf(b0)[:, qlo:512], KT[hp][0:96, kbi * 128:(kbi + 1) * 128], QT[hp][0:96, qs], True, False,
                               r=[b_KT[hp][kbi // 4], b_QT[hp][qb]], w=[PB[b0]])
                            mm(pf(b0)[:, qlo:qlo + 128], identb[:], negtrib[:], False, True, r=[b_identb, b_negtri], w=[PB[b0]])
                            act(PT[p][:, qlo:512], pf(b0)[:, qlo:512], AF.Exp, r=[], w=[PB[b0], b_PT[p]], scale=SCALE)
                            info[gi] = (p, [(kbi, 0, qlo)])

                    def emit_pv(gi):
                        qb, kbs = groups[gi]
                        p, lst = info.pop(gi)
                        nkb = 4 * qb + 4
                        for (kbi, off, qlo) in lst:
                            if kbi == 0:
                                obank[qb] = lbank()
                            bo = obank[qb]
                            mm(pf(bo)[0:M, qlo:512], Vh[hp][:, kbi, 0:M], PT[p][:, off + qlo:off + 512], kbi == 0, kbi == nkb - 1,
                               r=[b_Vh[hp][kbi // 4], b_PT[p]], w=[PB[bo]])
                            if kbi == nkb - 1:
                                blk = slice(qb * 512, (qb + 1) * 512)
                                act(sqw[0:M, :], pf(bo)[0:M, :], AF.Square, r=[b_wcol], w=[PB[bo], b_sqw], scale=wcol[0:M, hp:hp + 1])
                                bn = bank()
                                mm(pf(bn)[0:Mo, :], ones_bf[0:M, 0:Mo], sqw[0:M, :], True, True, r=[b_ones, b_sqw], w=[PB[bn]])
                                act(rs_t[0:Mo, :], pf(bn)[0:Mo, :], AF.Ln, r=[], w=[PB[bn], b_rs], scale=1.0 / 64)
                                act(rs_t[0:Mo, :], rs_t[0:Mo, :], AF.Exp, r=[b_rs], w=[b_rs], scale=-0.5)
                                rows = slice(voff, voff + 64)
                                stt("dve", mixM[rows, ec, blk], pf(bo)[rows, :], g_mla[rows, ec:ec + 1], rs_t[rows, :], ALU.mult, ALU.mult,
                                    r=[b_rs, b_gmla], w=[PB[bo], b_mixM])

                    LOOK = 2
                    for gi in range(len(groups)):
                        yield groups[gi][0]
                        emit_qk(gi)
                        if gi >= LOOK:
                            emit_pv(gi - LOOK)
                    yield None
                    for gi in range(max(0, len(groups) - LOOK), len(groups)):
                        emit_pv(gi)

                gp0 = prep_head(0)
                p0_done = [-1]

                def advance_p0(upto):
                    while p0_done[0] < upto:
                        try:
                            tag = next(gp0)
                        except StopIteration:
                            p0_done[0] = NB
                            return
                        if tag[0] == "s2":
                            p0_done[0] = tag[1]

                for h in range(8):
                    ga = attn_head(h)
                    gp = prep_head(h + 1) if h + 1 < 8 else None
                    n_groups = sum(2 * qb + 4 for qb in range(NB))
                    n_seg = 2 * NB
                    every = max(1, n_groups // (n_seg + 1))
                    cnt = 0
                    for need in ga:
                        if h == 0 and need is not None:
                            advance_p0(need)
                        cnt += 1
                        ok = (h != 0) or (p0_done[0] >= NB - 1)
                        ev = every if h != 0 else 1
                        if gp is not None and ok and cnt % ev == 0:
                            try:
                                next(gp)
                            except StopIteration:
                                gp = None
                    if h == 0:
                        advance_p0(NB)
                    if gp is not None:
                        for _ in gp:
                            pass
            if debug:
                dump(d_mixM, mixM[:], b_mixM)
            kb.barrier()

        ar["lo"] = markC; ar["hi"] = ARENA_BYTES
        if True:
            stC = None
            Wg = T("top", "Wg", [128, 8, DFF], BF16); Wu = T("top", "Wu", [128, 8, DFF], BF16)
            NWG = 4
            WGC = DFF // NWG
            b_Wg = [Buf("Wg%d" % i) for i in range(NWG)]; b_Wuu = [Buf("Wu%d" % i) for i in range(NWG)]
            sg_ = kb.stream(); su_ = kb.stream()
            bwg1 = Buf("Wg_all"); bwu1 = Buf("Wu_all")
            b_Wg = [bwg1] * NWG; b_Wuu = [bwu1] * NWG
            for dc in range(8):
                dma("pool", sg_, Wg[:, dc, :], w_gate_d[dc * 128:(dc + 1) * 128, :], r=[], w=[bwg1])
                dma("pool", su_, Wu[:, dc, :], w_up_d[dc * 128:(dc + 1) * 128, :], r=[], w=[bwu1])
            NXC = 3
            xc = [T(stC, "xc%d" % i, [128, D], F32) for i in range(NXC)]; b_xc = [Buf("xc%d" % i) for i in range(NXC)]
            st_xc = [kb.stream() for _ in range(NXC)]
            NYC = 3
            yc = [T(stC, "yc%d" % i, [128, D], F32) for i in range(NYC)]; b_yc = [Buf("yc%d" % i) for i in range(NYC)]
            st_oc = [kb.stream() for _ in range(NYC)]
            junkc = T(stC, "junkc", [128, 512], BF16); b_junkc = Buf("junkc")
            ssc = T(stC, "ssc", [128, 2, 3], F32); b_ssc = [Buf("ssc0"), Buf("ssc1")]
            print("arena phase C: lo", ar["lo"], "hi", ar["hi"])

            def ldx(g):
                xs = g % NXC
                dma("sp", st_xc[xs], xc[xs][:], x_d[g * 128:(g + 1) * 128, :], r=[], w=[b_xc[xs]])

            for g in range(min(NXC, NT)):
                ldx(g)
            for g in range(NT):
                sl = g % 2
                xs = g % NXC
                ys = g % NYC
                tok = slice(g * 128, (g + 1) * 128)
                by = [bank(), bank()]
                for hh in range(2):
                    for ecx in range(8):
                        lhs = mixM[:, ecx, tok] if ecx < 4 else mixH[:, ecx - 4, tok]
                        mm(pf(by[hh]), lhs, Wout[:, ecx, hh * 512:(hh + 1) * 512], ecx == 0, ecx == 7,
                           r=[b_mixM, b_Wout] + [b_mixH[hx][g // 4] for hx in range(4)], w=[PB[by[hh]]])
                    act(junkc[:], pf(by[hh]), AF.Square, r=[], w=[PB[by[hh]], b_junkc, b_ssc[sl]], accum=ssc[:, sl, hh:hh + 1])
                tt("dve", ssc[:, sl, 2:3], ssc[:, sl, 0:1], ssc[:, sl, 1:2], ALU.add, r=[b_ssc[sl]], w=[b_ssc[sl]])
                rsqrt_act(ssc[:, sl, 2:3], ssc[:, sl, 2:3], 1.0 / D, r=[b_ssc[sl], b_eps], w=[b_ssc[sl]])
                for hh in range(2):
                    cs_ = slice(hh * 512, (hh + 1) * 512)
                    tt("dve", yc[ys][:, cs_], pf(by[hh]), gpost_b[:, cs_], ALU.mult, r=[b_gpost], w=[PB[by[hh]], b_yc[ys]])
                stt("dve", yc[ys][:], yc[ys][:], ssc[:, sl, 2:3], xc[xs][:], ALU.mult, ALU.add,
                    r=[b_ssc[sl], b_xc[xs], b_yc[ys]], w=[b_yc[ys]])
                dma("sp", st_oc[ys], out_d[tok, :], yc[ys][:], r=[b_yc[ys]], w=[ob[g]])
                if g + NXC < NT:
                    ldx(g + NXC)
            kb.barrier()

        ar["lo"] = markMix
        if True:
            stD = None
            Wd = T(stD, "Wd", [128, NFC, D], BF16)
            b_Wd = [Buf("Wd0"), Buf("Wd1")]
            st_wd = [kb.stream(), kb.stream()]
            for fc in range(NFC):
                j = 0 if fc < NFC // 2 else 1
                dma("pool", st_wd[j], Wd[:, fc, :], w_down_d[fc * 128:(fc + 1) * 128, :], r=[], w=[b_Wd[j]])
            gfpre_b = T(stD, "gfpre_b", [128, D], F32); gfpost_b = T(stD, "gfpost_b", [128, D], F32); b_gfpre = Buf("gfpre"); b_gfpost = Buf("gfpost")
            dma("sp", kb.stream(), gfpre_b[:], g_fpre_d.partition_broadcast(128), r=[], w=[b_gfpre])
            dma("sp", kb.stream(), gfpost_b[:], g_fpost_d.partition_broadcast(128), r=[], w=[b_gfpost])
            h1 = [T(stD, "h1_%d" % i, [128, D], F32) for i in range(2)]; b_h1 = [Buf("h1_%d" % i) for i in range(2)]
            zb = T(stD, "zb", [128, D], BF16); b_zb = Buf("zb")
            zT = T(stD, "zT", [128, 8, 512], BF16); b_zT = Buf("zT")
            ffT = T(stD, "ffT", [128, NFC, 512], BF16); b_ffTs = [Buf("ffT%d" % i) for i in range(NFC)]
            sg = [T(stD, "sg%d" % i, [128, 512], F32) for i in range(2)]; b_sg = [Buf("sg%d" % i) for i in range(2)]
            junkd = T(stD, "junkd", [128, D], BF16); b_junkd = Buf("junkd")
            yd = [T(stD, "yd%d" % i, [128, D], F32) for i in range(2)]; b_yd = [Buf("yd%d" % i) for i in range(2)]
            ssd = T(stD, "ssd", [128, 2, 3], F32); b_ssd = [Buf("ssd0"), Buf("ssd1")]
            ssp = T(stD, "ssp", [128, 2], F32); b_ssp = Buf("ssp")
            print("arena phase D: lo", ar["lo"], "hi", ar["hi"])

            def s1(bb, i):
                g = 4 * bb + i
                tok = slice(g * 128, (g + 1) * 128)
                dma("sp", st_x[0], h1[0][:], out_d[tok, :], r=[ob[g]], w=[b_h1[0]])
                act(junkd[:], h1[0][:], AF.Square, r=[b_h1[0]], w=[b_junkd, b_ssp], accum=ssp[:, 0:1])
                rsqrt_act(ssp[:, 0:1], ssp[:, 0:1], 1.0 / D, r=[b_ssp, b_eps], w=[b_ssp])
                stt("dve", zb[:], h1[0][:], ssp[:, 0:1], gfpre_b[:], ALU.mult, ALU.mult, r=[b_h1[0], b_ssp, b_gfpre], w=[b_zb])

            def s2(bb, i):
                bk = bank()
                for dc in range(8):
                    tr(pb(bk)[:, dc * 128:(dc + 1) * 128], zb[:, dc * 128:(dc + 1) * 128], identb[:], r=[b_zb, b_identb], w=[PB[bk]])
                cp("act", zT[:, :, i * 128:(i + 1) * 128], pb(bk).rearrange("p (c t) -> p c t", t=128), r=[], w=[PB[bk], b_zT])

            def gateup(bb):
                for fc in range(NFC):
                    bg_ = bank(); bu_ = bank()
                    for dc in range(8):
                        mm(pf(bg_), Wg[:, dc, fc * 128:(fc + 1) * 128], zT[:, dc, :], dc == 0, dc == 7, r=[b_Wg[fc * 128 // WGC], b_zT], w=[PB[bg_]])
                    for dc in range(8):
                        mm(pf(bu_), Wu[:, dc, fc * 128:(fc + 1) * 128], zT[:, dc, :], dc == 0, dc == 7, r=[b_Wuu[fc * 128 // WGC], b_zT], w=[PB[bu_]])
                    s = fc % 2
                    act(sg[s][:], pf(bg_), AF.Silu, r=[], w=[PB[bg_], b_sg[s]])
                    tt("dve", ffT[:, fc, :], pf(bu_), sg[s][:], ALU.mult, r=[b_sg[s]], w=[PB[bu_], b_ffTs[fc]])

            def down(bb, i):
                g = 4 * bb + i
                sl = g % 2
                tok = slice(g * 128, (g + 1) * 128)
                dma("sp", st_x[1], h1[1][:], out_d[tok, :], r=[ob[g]], w=[b_h1[1]])
                bd = [bank(), bank()]
                for hh in range(2):
                    for fc in range(NFC):
                        mm(pf(bd[hh]), ffT[:, fc, i * 128:(i + 1) * 128], Wd[:, fc, hh * 512:(hh + 1) * 512], fc == 0, fc == NFC - 1,
                           r=[b_ffTs[fc], b_Wd[0 if fc < NFC // 2 else 1]], w=[PB[bd[hh]]])
                    act(junkd[:, 0:512], pf(bd[hh]), AF.Square, r=[], w=[PB[bd[hh]], b_junkd, b_ssd[sl]], accum=ssd[:, sl, hh:hh + 1])
                tt("dve", ssd[:, sl, 2:3], ssd[:, sl, 0:1], ssd[:, sl, 1:2], ALU.add, r=[b_ssd[sl]], w=[b_ssd[sl]])
                rsqrt_act(ssd[:, sl, 2:3], ssd[:, sl, 2:3], 1.0 / D, r=[b_ssd[sl], b_eps], w=[b_ssd[sl]])
                for hh in range(2):
                    cs_ = slice(hh * 512, (hh + 1) * 512)
                    tt("dve", yd[sl][:, cs_], pf(bd[hh]), gfpost_b[:, cs_], ALU.mult, r=[b_gfpost], w=[PB[bd[hh]], b_yd[sl]])
                stt("dve", yd[sl][:], yd[sl][:], ssd[:, sl, 2:3], h1[1][:], ALU.mult, ALU.add,
                    r=[b_ssd[sl], b_h1[1], b_yd[sl]], w=[b_yd[sl]])
                dma("pool", st_o[sl], out_d[tok, :], yd[sl][:], r=[b_yd[sl]], w=[ob[g]])

            for i in range(4):
                s1(0, i)
                s2(0, i)
            for bb in range(NB):
                gateup(bb)
                for i in range(4):
                    if bb + 1 < NB:
                        s1(bb + 1, i)
                    down(bb, i)
                    if bb + 1 < NB:
                        s2(bb + 1, i)
            kb.wait_all("sp", ob)
        kb.emit()
    return nc


def _consts():
    ident = np.eye(128, dtype=np.float32)
    tri = (np.arange(128)[:, None] <= np.arange(128)[None, :]).astype(np.float32)
    s = np.arange(128)[:, None]
    t = np.arange(128)[None, :]
    mask2 = ((s // 64 == t // 64) & (s <= t)).astype(np.float32)
    resetm = (np.arange(512) % 64 != 0).astype(np.float32)
    invf64 = 1.0 / (10000.0 ** (np.arange(0, 32, 2, dtype=np.float64) / 32.0))
    invf = invf64.astype(np.float32)
    invf_lo = (invf64 - invf.astype(np.float64)).astype(np.float32)
    wcol = np.zeros((128, 2), np.float32)
    c = np.float32(np.sqrt(64 * EPS))
    wcol[0:64, 0] = 1.0
    wcol[64, 0] = c
    wcol[0, 1] = c
    wcol[64:128, 1] = 1.0
    negtri = ((tri - 1.0) * 30000.0).astype(np.float32)
    return dict(ident=ident, tri=tri, negtri=negtri, mask2=mask2, resetm=resetm, invf=invf, invf_lo=invf_lo, wcol=wcol)


def _pcol(v, n):
    return np.ascontiguousarray(np.asarray(v, np.float32).reshape(n, 128).T)


def make_in_maps(inputs, S, batch_ids):
    f = lambda a: np.ascontiguousarray(np.asarray(a, np.float32))
    shared = dict(
        w_in=f(inputs["w_in"][0]), w_uq=f(inputs["mla_w_uq"][0]).reshape(384, 768),
        w_ukv=f(inputs["mla_w_ukv"][0]).reshape(128, 1024), w_out=f(inputs["w_out"][0]),
        w_gate=f(inputs["w_gate"][0]), w_up=f(inputs["w_up"][0]), w_down=f(inputs["w_down"][0]),
        g_pre=f(inputs["attn_pre_norm"][0]), g_post=f(inputs["attn_post_norm"][0]),
        g_fpre=f(inputs["ffn_pre_norm"][0]), g_fpost=f(inputs["ffn_post_norm"][0]),
        g_q=_pcol(inputs["mla_q_norm"][0], 3), g_kv=_pcol(inputs["mla_kv_norm"][0], 1),
        g_mla=_pcol(inputs["mla_out_norm"][0], 4), g_hn=_pcol(inputs["hgrn_out_norm"][0], 4),
        lbl=np.ascontiguousarray(np.concatenate([_pcol(inputs["hgrn_lb_logits"][0], 4), _pcol(inputs["hgrn_lb_logits"][1], 4)], axis=1)),
    )
    shared.update(_consts())
    maps = []
    NT = S // 128
    for b in batch_ids:
        m = dict(shared)
        m["x"] = f(inputs["x"][b])
        m["pos"] = np.ascontiguousarray(np.asarray(inputs["positions"][b], np.int32).reshape(NT, 128).T)
        maps.append(m)
    return maps


_NC_CACHE = {}


def kernel(**inputs):
    x = np.asarray(inputs["x"])
    B, S, _ = x.shape
    if S not in _NC_CACHE:
        _NC_CACHE[S] = build(S)
    nc = _NC_CACHE[S]
    maps = make_in_maps(inputs, S, list(range(B)))
    res = run_bass_kernel_spmd(nc, maps, core_ids=list(range(B)))
    out = np.stack([np.asarray(r["out"], np.float32) for r in res.results], axis=0)
    return out.astype(np.float32)
```
